# Optimizing a Trainium2 kernel written in Bass

```python
import jax, jax.numpy as jnp
from jax import lax
import numpy as np

D_MODEL = 4096
BATCH = 1
SEQ = 8192
DEPTH = 2

N_MEM = 256
NORM_EPS = 1e-6
MIX_A = D_MODEL // 2
MIX_B = D_MODEL // 2
MLSTM_HEADS = 4
MLSTM_DV = MIX_A // MLSTM_HEADS
MLSTM_DK = MLSTM_DV // 2
MLSTM_CHUNK = 64
GATE_CAP = 15.0
HGRN_EXPAND = 128
HGRN_HEADS = MIX_B // HGRN_EXPAND
HGRN_DK = HGRN_EXPAND
HGRN_DV = MIX_B // HGRN_HEADS
HGRN_CHUNK = 64
IN_SIZES = (MLSTM_HEADS * MLSTM_DK, MLSTM_HEADS * MLSTM_DK, MIX_A, MIX_A, MLSTM_HEADS, MLSTM_HEADS, MIX_B, MIX_B, MIX_B, MIX_B)
IN_COLS = sum(IN_SIZES)
RWKV_HEAD = 64
RWKV_HEADS = D_MODEL // RWKV_HEAD
RWKV_DECAY_LORA = max(32, int(round(1.8 * D_MODEL ** 0.5 / 32)) * 32)
RWKV_AAA_LORA = max(32, int(round(1.8 * D_MODEL ** 0.5 / 32)) * 32)
RWKV_GATE_LORA = max(32, int(round(0.6 * D_MODEL ** 0.8 / 32)) * 32)
RWKV_LNX_EPS = 64e-5
XATTN_HEADS = 4
XATTN_DIM = D_MODEL // XATTN_HEADS
MLP_HIDDEN = 4 * D_MODEL
N_EVEN = (DEPTH + 1) // 2
N_ODD = DEPTH // 2

kernel_name = 'hybrid_mlstm_hgrn2_rwkv7_memxattn'


def rms_norm(x, g, eps=NORM_EPS):
    xf = x.astype(jnp.float32)
    y = xf * lax.rsqrt(jnp.mean(xf * xf, axis=-1, keepdims=True) + eps)
    return (y * g.astype(jnp.float32)).astype(x.dtype)


def soft_cap(x, cap=GATE_CAP):
    return cap * jnp.tanh(x / cap)


def _to_chunks(a, L):
    B, T = a.shape[:2]
    a = a.reshape((B, T // L, L) + a.shape[2:])
    perm = (1, 0, 3, 2) + tuple(range(4, a.ndim))
    return a.transpose(perm)


def _from_chunks(a):
    NC, B, H, L = a.shape[:4]
    perm = (1, 0, 3, 2) + tuple(range(4, a.ndim))
    return a.transpose(perm).reshape((B, NC * L, H) + a.shape[4:])


def mlstm_chunkwise(q, k, v, i_pre, f_pre):
    B, T, H, DK = q.shape
    DV = v.shape[-1]
    L = MLSTM_CHUNK
    causal = jnp.tril(jnp.ones((L, L), dtype=bool))
    q = q * (DK ** -0.5)
    log_f = jax.nn.log_sigmoid(f_pre)
    xs = (_to_chunks(q, L), _to_chunks(k, L), _to_chunks(v, L), _to_chunks(i_pre, L), _to_chunks(log_f, L))

    def step(carry, inp):
        C, n, m = carry
        qc, kc, vc, lic, lfc = inp
        b = jnp.cumsum(lfc, axis=-1)
        log_d = b[..., :, None] - b[..., None, :] + lic[..., None, :]
        log_d = jnp.where(causal, log_d, -jnp.inf)
        inter = b + m[..., None]
        m_t = jnp.maximum(inter, jnp.max(log_d, axis=-1))
        s = jnp.einsum('bhtd,bhsd->bhts', qc, kc) * jnp.exp(log_d - m_t[..., None])
        g_inter = jnp.exp(inter - m_t)
        num = jnp.einsum('bhts,bhsv->bhtv', s, vc) + g_inter[..., None] * jnp.einsum('bhtd,bhdv->bhtv', qc, C)
        den = jnp.sum(s, axis=-1) + g_inter * jnp.einsum('bhtd,bhd->bht', qc, n)
        h = num / jnp.maximum(jnp.abs(den), jnp.exp(-m_t))[..., None]
        b_last = b[..., -1]
        log_w = b_last[..., None] - b + lic
        m_new = jnp.maximum(b_last + m, jnp.max(log_w, axis=-1))
        w = jnp.exp(log_w - m_new[..., None])
        decay = jnp.exp(b_last + m - m_new)
        kw = kc * w[..., None]
        C = decay[..., None, None] * C + jnp.einsum('bhsd,bhsv->bhdv', kw, vc)
        n = decay[..., None] * n + jnp.sum(kw, axis=2)
        return (C, n, m_new), h

    init = (jnp.zeros((B, H, DK, DV), jnp.float32), jnp.zeros((B, H, DK), jnp.float32), jnp.zeros((B, H), jnp.float32))
    _, h = lax.scan(step, init, xs)
    return _from_chunks(h)


def hgrn2_chunkwise(q, k, v, log_f):
    B, T, H, DK = q.shape
    DV = v.shape[-1]
    L = HGRN_CHUNK
    causal = jnp.tril(jnp.ones((L, L), dtype=bool))[:, :, None]
    xs = (_to_chunks(q, L), _to_chunks(k, L), _to_chunks(v, L), _to_chunks(log_f, L))

    def step(S, inp):
        qc, kc, vc, lfc = inp
        A = jnp.cumsum(lfc, axis=2)
        rel = A[:, :, :, None, :] - A[:, :, None, :, :]
        rel = jnp.exp(jnp.where(causal, rel, -jnp.inf))
        s = jnp.einsum('bhtd,bhtsd,bhsd->bhts', qc, rel, kc)
        o = jnp.einsum('bhts,bhsv->bhtv', s, vc) + jnp.einsum('bhtd,bhdv->bhtv', qc * jnp.exp(A), S)
        A_last = A[:, :, -1]
        S = jnp.exp(A_last)[..., None] * S + jnp.einsum('bhsd,bhsv->bhdv', kc * jnp.exp(A_last[:, :, None] - A), vc)
        return S, o

    _, o = lax.scan(step, jnp.zeros((B, H, DK, DV), jnp.float32), xs)
    return _from_chunks(o)


def mlstm_hgrn2_mixer(h, w_in, b_i, b_f, mlstm_g, lb, hgrn_g, w_out):
    B, T, _ = h.shape
    f32 = jnp.float32
    proj = (h @ w_in).astype(f32)
    bounds = [sum(IN_SIZES[:j + 1]) for j in range(len(IN_SIZES) - 1)]
    q_a, k_a, v_a, o_a, i_a, f_a, q_b, f_b, i_b, g_b = jnp.split(proj, bounds, axis=-1)
    q = q_a.reshape(B, T, MLSTM_HEADS, MLSTM_DK)
    k = k_a.reshape(B, T, MLSTM_HEADS, MLSTM_DK)
    v = v_a.reshape(B, T, MLSTM_HEADS, MLSTM_DV)
    i_pre = soft_cap(i_a + b_i.astype(f32))
    f_pre = soft_cap(f_a + b_f.astype(f32))
    h_a = mlstm_chunkwise(q, k, v, i_pre, f_pre)
    h_a = rms_norm(h_a, mlstm_g.reshape(MLSTM_HEADS, MLSTM_DV)) * jax.nn.sigmoid(o_a).reshape(B, T, MLSTM_HEADS, MLSTM_DV)
    lb = lb.astype(f32).reshape(HGRN_HEADS, HGRN_DK)
    f = lb + (1.0 - lb) * jax.nn.sigmoid(f_b.reshape(B, T, HGRN_HEADS, HGRN_DK))
    qh = jax.nn.silu(q_b).reshape(B, T, HGRN_HEADS, HGRN_DK)
    h_b = hgrn2_chunkwise(qh, 1.0 - f, i_b.reshape(B, T, HGRN_HEADS, HGRN_DV), jnp.log(f))
    h_b = rms_norm(h_b, hgrn_g.reshape(HGRN_HEADS, HGRN_DV)) * jax.nn.silu(g_b).reshape(B, T, HGRN_HEADS, HGRN_DV)
    y = jnp.concatenate([h_a.reshape(B, T, MIX_A), h_b.reshape(B, T, MIX_B)], axis=-1)
    return y.astype(h.dtype) @ w_out


def rwkv7_scan(r, decay, k, v, kk, kka):
    B, T, H, N = r.shape

    def step(S, inp):
        r_t, w_t, k_t, v_t, kk_t, b_t = inp
        sa = jnp.einsum('bhvk,bhk->bhv', S, -kk_t)
        S = S * w_t[:, :, None, :] + sa[..., None] * b_t[:, :, None, :] + v_t[..., None] * k_t[:, :, None, :]
        return S, jnp.einsum('bhvk,bhk->bhv', S, r_t)

    xs = tuple(a.transpose(1, 0, 2, 3) for a in (r, decay, k, v, kk, kka))
    _, y = lax.scan(step, jnp.zeros((B, H, N, N), jnp.float32), xs)
    return y.transpose(1, 0, 2, 3)


def rwkv7_time_mix(h, mu, w0, w1, w2, a0, a1, a2, g1, g2, k_k, k_a, r_k, w_r, w_k, w_v, w_o, lnx_w, lnx_b):
    B, T, D = h.shape
    H, N = RWKV_HEADS, RWKV_HEAD
    f32 = jnp.float32
    x_prev = jnp.concatenate([jnp.zeros_like(h[:, :1]), h[:, :-1]], axis=1)
    xx = x_prev - h
    shift = lambda j: h + xx * mu[j]
    r = (shift(0) @ w_r).astype(f32)
    w_log = -jax.nn.softplus(-(w0 + jnp.tanh(shift(1) @ w1) @ w2).astype(f32)) - 0.5
    k = (shift(2) @ w_k).astype(f32)
    v = (shift(3) @ w_v).astype(f32)
    a = jax.nn.sigmoid((a0 + (shift(4) @ a1) @ a2).astype(f32))
    g = (jax.nn.sigmoid(shift(5) @ g1) @ g2).astype(f32)
    heads = lambda t: t.reshape(B, T, H, N)
    kk = heads(k * k_k.astype(f32))
    kk = kk / jnp.maximum(jnp.sqrt(jnp.sum(kk * kk, axis=-1, keepdims=True)), 1e-12)
    k = heads(k * (1.0 + (a - 1.0) * k_a.astype(f32)))
    r, v, a = heads(r), heads(v), heads(a)
    decay = heads(jnp.exp(-jnp.exp(w_log)))
    y = rwkv7_scan(r, decay, k, v, kk, kk * a)
    mean = jnp.mean(y, axis=-1, keepdims=True)
    var = jnp.mean(jnp.square(y - mean), axis=-1, keepdims=True)
    y = (y - mean) * lax.rsqrt(var + RWKV_LNX_EPS)
    y = y * lnx_w.astype(f32).reshape(H, N) + lnx_b.astype(f32).reshape(H, N)
    y = y + jnp.sum(r * k * r_k.astype(f32), axis=-1, keepdims=True) * v
    y = y.reshape(B, T, D) * g
    return y.astype(h.dtype) @ w_o


def mem_cross_attention(h, mem_k, mem_v, w_q, w_o):
    B, T, D = h.shape
    q = (h @ w_q).reshape(B, T, XATTN_HEADS, XATTN_DIM)
    s = jnp.einsum('bthd,bmhd->bhtm', q, mem_k).astype(jnp.float32) * (XATTN_DIM ** -0.5)
    p = jax.nn.softmax(s, axis=-1).astype(h.dtype)
    o = jnp.einsum('bhtm,bmhd->bthd', p, mem_v).reshape(B, T, D)
    return o @ w_o


def sq_relu_mlp(h, w_up, w_down):
    return jnp.square(jax.nn.relu(h @ w_up)) @ w_down


def setup_inputs(seed: int = 0) -> dict:
    key = jax.random.key(seed)
    ks = iter(jax.random.split(key, 64))
    f32 = jnp.float32
    D = D_MODEL

    def nrm(shape, fan_in, scale=1.0):
        return jax.random.normal(next(ks), shape, f32) * (scale * fan_in ** -0.5)

    def gain(shape):
        return 1.0 + 0.02 * jax.random.normal(next(ks), shape, f32)

    def small(shape, s=0.01, center=0.0):
        return center + s * jax.random.normal(next(ks), shape, f32)

    return {
        'x': jax.random.normal(next(ks), (BATCH, SEQ, D), f32),
        'mem': jax.random.normal(next(ks), (BATCH, N_MEM, D), f32),
        'norm_mix_g': gain((DEPTH, D)),
        'norm_xattn_g': gain((DEPTH, D)),
        'norm_mlp_g': gain((DEPTH, D)),
        'final_norm_g': gain((D,)),
        'mem_norm_g': gain((D,)),
        'ab_w_in': nrm((N_EVEN, D, IN_COLS), D),
        'mlstm_b_i': small((N_EVEN, MLSTM_HEADS), 0.1),
        'mlstm_b_f': small((N_EVEN, MLSTM_HEADS), 0.5, 3.0),
        'mlstm_norm_g': gain((N_EVEN, MIX_A)),
        'hgrn_lb_logits': small((N_EVEN + 1, MIX_B), 0.5),
        'hgrn_norm_g': gain((N_EVEN, MIX_B)),
        'ab_w_out': nrm((N_EVEN, D, D), D),
        'rwkv_mu': jax.random.uniform(next(ks), (N_ODD, 6, D), f32),
        'rwkv_w0': small((N_ODD, D), 0.5),
        'rwkv_w1': nrm((N_ODD, D, RWKV_DECAY_LORA), D),
        'rwkv_w2': nrm((N_ODD, RWKV_DECAY_LORA, D), RWKV_DECAY_LORA, 0.1),
        'rwkv_a0': small((N_ODD, D), 0.1),
        'rwkv_a1': nrm((N_ODD, D, RWKV_AAA_LORA), D),
        'rwkv_a2': nrm((N_ODD, RWKV_AAA_LORA, D), RWKV_AAA_LORA, 0.1),
        'rwkv_g1': nrm((N_ODD, D, RWKV_GATE_LORA), D),
        'rwkv_g2': nrm((N_ODD, RWKV_GATE_LORA, D), RWKV_GATE_LORA),
        'rwkv_k_k': small((N_ODD, D), 0.02, 0.85),
        'rwkv_k_a': gain((N_ODD, D)),
        'rwkv_r_k': small((N_ODD, RWKV_HEADS, RWKV_HEAD), 0.1),
        'rwkv_w_r': nrm((N_ODD, D, D), D),
        'rwkv_w_k': nrm((N_ODD, D, D), D),
        'rwkv_w_v': nrm((N_ODD, D, D), D),
        'rwkv_w_o': nrm((N_ODD, D, D), D),
        'rwkv_lnx_w': gain((N_ODD, D)),
        'rwkv_lnx_b': small((N_ODD, D), 0.01),
        'xattn_w_q': nrm((DEPTH, D, D), D),
        'xattn_w_o': nrm((DEPTH, D, D), D),
        'mem_w_kv': nrm((D, 2 * D), D),
        'mlp_w_up': nrm((DEPTH, D, MLP_HIDDEN), D),
        'mlp_w_down': nrm((DEPTH, MLP_HIDDEN, D), MLP_HIDDEN),
    }


def reference(x, mem, norm_mix_g, norm_xattn_g, norm_mlp_g, final_norm_g, mem_norm_g,
              ab_w_in, mlstm_b_i, mlstm_b_f, mlstm_norm_g, hgrn_lb_logits, hgrn_norm_g, ab_w_out,
              rwkv_mu, rwkv_w0, rwkv_w1, rwkv_w2, rwkv_a0, rwkv_a1, rwkv_a2, rwkv_g1, rwkv_g2,
              rwkv_k_k, rwkv_k_a, rwkv_r_k, rwkv_w_r, rwkv_w_k, rwkv_w_v, rwkv_w_o, rwkv_lnx_w, rwkv_lnx_b,
              xattn_w_q, xattn_w_o, mem_w_kv, mlp_w_up, mlp_w_down):
    B, M, D = mem.shape
    mem_kv = rms_norm(mem, mem_norm_g) @ mem_w_kv
    mem_k = mem_kv[..., :D].reshape(B, M, XATTN_HEADS, XATTN_DIM)
    mem_v = mem_kv[..., D:].reshape(B, M, XATTN_HEADS, XATTN_DIM)
    lb_all = jnp.cumsum(jax.nn.softmax(hgrn_lb_logits.astype(jnp.float32), axis=0), axis=0)
    h = x
    for layer in range(DEPTH):
        j = layer // 2
        hn = rms_norm(h, norm_mix_g[layer])
        if layer % 2 == 0:
            mix = mlstm_hgrn2_mixer(hn, ab_w_in[j], mlstm_b_i[j], mlstm_b_f[j], mlstm_norm_g[j],
                                    lb_all[j], hgrn_norm_g[j], ab_w_out[j])
        else:
            mix = rwkv7_time_mix(hn, rwkv_mu[j], rwkv_w0[j], rwkv_w1[j], rwkv_w2[j], rwkv_a0[j], rwkv_a1[j],
                                 rwkv_a2[j], rwkv_g1[j], rwkv_g2[j], rwkv_k_k[j], rwkv_k_a[j], rwkv_r_k[j],
                                 rwkv_w_r[j], rwkv_w_k[j], rwkv_w_v[j], rwkv_w_o[j], rwkv_lnx_w[j], rwkv_lnx_b[j])
        h = h + mix
        h = h + mem_cross_attention(rms_norm(h, norm_xattn_g[layer]), mem_k, mem_v, xattn_w_q[layer], xattn_w_o[layer])
        h = h + sq_relu_mlp(rms_norm(h, norm_mlp_g[layer]), mlp_w_up[layer], mlp_w_down[layer])
    return rms_norm(h, final_norm_g)
```

```python
import numpy as np
import concourse.bass as bass
import concourse.mybir as mybir
from concourse.bass_utils import run_bass_kernel_spmd

F32 = mybir.dt.float32
F32R = mybir.dt.float32r
ALU = mybir.AluOpType
AF = mybir.ActivationFunctionType
AX = mybir.AxisListType

D = 4096
KC = D // 128
SEQ = 8192
NMEM = 256
EPS = 1e-6
SEM_CHUNK = 32000


class T:
    __slots__ = ("ap", "name", "w", "r")

    def __init__(self, ap, name):
        self.ap = ap
        self.name = name
        self.w = []
        self.r = {}

    def __getitem__(self, idx):
        return self.ap[idx]

    def r32(self):
        return self.ap.bitcast(F32R)


NPOOL = 20


class Prog:
    def __init__(self, nc, same_engine_sync=True):
        self.nc = nc
        self.eng = {"pe": nc.tensor, "act": nc.scalar, "dve": nc.vector,
                    "pool": nc.gpsimd, "sp": nc.sync}
        self.seq = {k: 0 for k in self.eng}
        self.sems = {k: [] for k in self.eng}
        self.waited = {}
        self.same_engine_sync = same_engine_sync
        self.stack = []
        self.nsem = 0
        self.n_wait = 0
        self.n_ins = 0
        self.dpool = {}
        self.dpi = {}

    def _enter(self, cm):
        obj = cm.__enter__()
        self.stack.append(cm)
        return obj

    def close(self):
        while self.stack:
            self.stack.pop().__exit__(None, None, None)

    def sem(self, name):
        self.nsem += 1
        return self._enter(self.nc.semaphore(name))

    def sbuf(self, name, shape, dtype=F32):
        h = self._enter(self.nc.sbuf_tensor(name, list(shape), dtype))
        return T(h[tuple(slice(None) for _ in shape)], name)

    def psum(self, name, shape, dtype=F32):
        h = self._enter(self.nc.psum_tensor(name, list(shape), dtype))
        return T(h[tuple(slice(None) for _ in shape)], name)

    def dram(self, name, shape, dtype=F32, kind="Internal"):
        return T(self.nc.dram_tensor(name, list(shape), dtype, kind=kind).ap(), name)

    def subs(self, t, n, axis=1):
        out = []
        for i in range(n):
            idx = [slice(None)] * len(t.ap.shape)
            idx[axis] = i
            out.append(T(t.ap[tuple(idx)], f"{t.name}_{i}"))
        return out

    def _csem(self, e, k):
        ci = (k - 1) // SEM_CHUNK
        while len(self.sems[e]) <= ci:
            self.sems[e].append(self.sem(f"s_{e}_{len(self.sems[e])}"))
        return self.sems[e][ci], (k - 1) % SEM_CHUNK + 1

    def _wait(self, e, tok):
        sem, val, src = tok
        if src == e and (e == "pe" or not self.same_engine_sync):
            return
        key = (e, id(sem))
        if self.waited.get(key, 0) >= val:
            return
        self.waited[key] = val
        self.eng[e].wait_ge(sem, val)
        self.n_wait += 1

    def _deps(self, e, reads, writes, join=False):
        toks = []
        for t in reads:
            toks.extend(t.w)
        for t in writes:
            if not join:
                toks.extend(t.w)
            else:
                toks.extend(tk for tk in t.w if tk[2] != "dma")
            toks.extend(t.r.values())
        for tok in toks:
            self._wait(e, tok)

    def _commit(self, tok, reads, writes, join=False):
        key = id(tok[0])
        for t in reads:
            old = t.r.get(key)
            if old is None or old[1] < tok[1]:
                t.r[key] = tok
        for t in writes:
            if join:
                t.w = [tk for tk in t.w if tk[2] == "dma"] + [tok]
            else:
                t.w = [tok]
                t.r = {}

    limit = None

    def op(self, e, fn, reads=(), writes=()):
        if self.limit is not None and self.n_ins >= self.limit:
            return None
        self._deps(e, reads, writes)
        ins = fn(self.eng[e])
        self.seq[e] += 1
        sem, val = self._csem(e, self.seq[e])
        ins.then_inc(sem, 1)
        self._commit((sem, val, e), reads, writes)
        self.n_ins += 1
        return ins

    def dma(self, q, out_ap, in_ap, reads=(), writes=(), join=False, **kw):
        if self.limit is not None and self.n_ins >= self.limit:
            return None
        if q not in self.dpool:
            self.dpool[q] = [[self.sem(f"dq_{q}_{i}"), 0] for i in range(NPOOL)]
            self.dpi[q] = 0
        slot = self.dpool[q][self.dpi[q] % NPOOL]
        self.dpi[q] += 1
        if slot[1] > 0:
            self._wait(q, (slot[0], slot[1], "dma"))
        self._deps(q, reads, writes, join=join)
        ins = self.eng[q].dma_start(out=out_ap, in_=in_ap, **kw)
        slot[1] += 16
        ins.then_inc(slot[0], 16)
        self._commit((slot[0], slot[1], "dma"), reads, writes, join=join)
        self.n_ins += 1
        return ins

    def finish(self, tiles, e="sp"):
        for t in tiles:
            for tok in t.w:
                self._wait(e, tok)
            for tok in t.r.values():
                self._wait(e, tok)


class Ctx:
    pass


def make_common(P, NT, n_wbuf=3, wcols=128, n_acc=3, own_ssps=True):
    c = Ctx()
    c.P = P
    c.NT = NT
    c.wcols = wcols
    c.wbufs = [P.sbuf(f"wbuf{i}", [128, KC, wcols], F32R) for i in range(n_wbuf)]
    c.accs = [P.psum(f"acc{i}", [128, 512], F32) for i in range(n_acc)]
    c.wi = 0
    c.ai = 0
    c.ones = P.sbuf("ones", [128, 128], F32)
    c.ones_f = P.sbuf("ones_f", [128, 128], F32)
    P.op("dve", lambda e: e.memset(c.ones_f.ap, 1.0), writes=[c.ones_f])
    P.op("dve", lambda e: e.tensor_copy(out=c.ones.r32(), in_=c.ones_f.ap), reads=[c.ones_f], writes=[c.ones])
    c.sq = [P.sbuf(f"sq{i}", [128, NT], F32) for i in range(2)]
    c.sqi = 0
    c.ssps = P.psum("ssps", [128, 512], F32) if own_ssps else c.accs[0]
    c.rstd = P.sbuf("rstd", [128, NT], F32)
    c.nrm_tmp = [P.sbuf(f"nrm_tmp{i}", [128, NT], F32) for i in range(2)]
    c.nti = 0
    return c


def next_w(c):
    w = c.wbufs[c.wi % len(c.wbufs)]
    c.wi += 1
    return w


def next_acc(c):
    a = c.accs[c.ai % len(c.accs)]
    c.ai += 1
    return a


def gemm_fm(c, W, row0, kc, col0, n_oc, act_chunks, evac, NT=None, q="sp"):
    P = c.P
    NT = NT or c.NT
    for oc in range(n_oc):
        wb = next_w(c)
        c0 = col0 + oc * 128
        src = W.ap[row0:row0 + kc * 128, c0:c0 + 128].bitcast(F32R).rearrange("(k p) c -> p k c", p=128)
        half = (kc + 1) // 2
        P.dma(q, wb.ap[:, 0:half, :], src[:, 0:half, :], writes=[wb])
        if kc > half:
            P.dma(q, wb.ap[:, half:kc, :], src[:, half:kc, :], writes=[wb], join=True)
        acc = next_acc(c)
        for k in range(kc):
            a = act_chunks[k]
            P.op("pe", lambda e, k=k, a=a: e.matmul(acc.ap[:, 0:NT], lhsT=wb.ap[:, k, :], rhs=a.r32(),
                                                    start=(k == 0), stop=(k == kc - 1)),
                 reads=[wb, a], writes=[acc])
        evac(oc, acc)


def rms_stats(c, chunks, dim, eps, NT=None):
    P = c.P
    NT = NT or c.NT
    n = len(chunks)
    for i, x in enumerate(chunks):
        sq = c.sq[c.sqi % 2]
        c.sqi += 1
        P.op("act", lambda e, x=x, sq=sq: e.activation(out=sq.r32(), in_=x.ap, func=AF.Square),
             reads=[x], writes=[sq])
        P.op("pe", lambda e, sq=sq, i=i: e.matmul(c.ssps.ap[:, 0:NT], lhsT=c.ones.r32(), rhs=sq.r32(),
                                                  start=(i == 0), stop=(i == n - 1)),
             reads=[c.ones, sq], writes=[c.ssps])
    P.op("dve", lambda e: e.tensor_scalar(out=c.rstd.ap, in0=c.ssps.ap[:, 0:NT], scalar1=1.0 / dim, scalar2=eps,
                                          op0=ALU.mult, op1=ALU.add), reads=[c.ssps], writes=[c.rstd])
    P.op("act", lambda e: e.activation(out=c.rstd.ap, in_=c.rstd.ap, func=AF.Sqrt), reads=[c.rstd], writes=[c.rstd])
    P.op("dve", lambda e: e.reciprocal(out=c.rstd.ap, in_=c.rstd.ap), reads=[c.rstd], writes=[c.rstd])
    return c.rstd


def rms_apply(c, src_chunks, dst_chunks, gain, rstd):
    P = c.P
    for k, (s, d) in enumerate(zip(src_chunks, dst_chunks)):
        if k % 2 == 0:
            P.op("dve", lambda e, s=s, d=d, k=k: e.scalar_tensor_tensor(out=d.r32(), in0=s.ap, scalar=gain.ap[:, k:k + 1],
                                                                        in1=rstd.ap, op0=ALU.mult, op1=ALU.mult),
                 reads=[s, gain, rstd], writes=[d])
        else:
            tmp = c.nrm_tmp[c.nti % 2]; c.nti += 1
            P.op("act", lambda e, s=s, k=k, tmp=tmp: e.activation(out=tmp.ap, in_=s.ap, func=AF.Copy, scale=gain.ap[:, k:k + 1]),
                 reads=[s, gain], writes=[tmp])
            P.op("pool", lambda e, d=d, tmp=tmp: e.tensor_tensor(out=d.r32(), in0=tmp.ap, in1=rstd.ap, op=ALU.mult),
                 reads=[tmp, rstd], writes=[d])


def build_tok(NTOK, NT, layer0, final, HB=16):
    nc = bass.Bass("TRN2", target_bir_lowering=False)
    nc.dge_precook = False
    P = Prog(nc)
    NTILE = NTOK // NT
    hT = P.dram("hT", [D, NTOK], F32, "ExternalInput")
    zT = P.dram("zT", [D, NTOK], F32, "ExternalInput")
    w_mo = P.dram("w_mo", [D, D], F32, "ExternalInput")
    w_q = P.dram("w_q", [D, D], F32, "ExternalInput")
    w_o = P.dram("w_o", [D, D], F32, "ExternalInput")
    w_up = P.dram("w_up", [D, 4 * D], F32, "ExternalInput")
    w_dn = P.dram("w_dn", [4 * D, D], F32, "ExternalInput")
    kT = P.dram("kT", [D, NMEM], F32, "ExternalInput")
    vM = P.dram("vM", [NMEM, D], F32, "ExternalInput")
    gains_d = P.dram("gains", [128, 4 * KC], F32, "ExternalInput")
    ident_d = P.dram("ident", [128, 128], F32, "ExternalInput")
    if layer0:
        ss_d = P.dram("ss", [24, NTOK], F32, "ExternalInput")
    oT = P.dram("oT", [D, NTOK], F32, "ExternalOutput")

    c = make_common(P, NT)
    gains = P.sbuf("gains_sb", [128, 4 * KC], F32)
    P.dma("sp", gains.ap, gains_d.ap, writes=[gains])
    g_x = T(gains.ap[:, 0:KC], "g_x"); g_m = T(gains.ap[:, KC:2 * KC], "g_m")
    g_f = T(gains.ap[:, 2 * KC:3 * KC], "g_f"); g_h = T(gains.ap[:, 3 * KC:4 * KC], "g_h")
    for g in (g_x, g_m, g_f, g_h):
        g.w = gains.w
    ident = P.sbuf("ident_sb", [128, 128], F32)
    P.dma("sp", ident.ap, ident_d.ap, writes=[ident])

    h_all = P.sbuf("h", [128, KC, NT], F32); h = P.subs(h_all, KC)
    x_all = P.sbuf("xin", [128, KC, NT], F32); xin = P.subs(x_all, KC)
    o_all = P.sbuf("ot", [128, KC, NT], F32); ot = P.subs(o_all, KC)
    q_all = P.sbuf("qh", [128, 8, NT], F32); qh = P.subs(q_all, 8)
    hid_all = P.sbuf("hid", [128, HB, NT], F32); hid = P.subs(hid_all, HB)
    relu_t = [P.sbuf(f"relu{i}", [128, NT], F32) for i in range(2)]
    kth = P.sbuf("kth", [128, 8, NMEM], F32R)
    vh = P.sbuf("vh", [128, 2, 1024], F32R)
    pexp = [P.sbuf(f"pexp{i}", [128, NMEM], F32) for i in range(2)]
    pT = P.sbuf("pT", [128, 2, NT], F32)
    stat = [P.sbuf(f"stat{i}", [128, 4], F32) for i in range(2)]
    sps = [P.psum(f"sps{i}", [128, 512], F32) for i in range(2)]
    trps = [P.psum(f"trps{i}", [128, 512], F32) for i in range(2)]
    ssb = [P.sbuf(f"ssb{i}", [128, NT], F32) for i in range(2)] if layer0 else None
    NSUB = NT // 128

    def resid_evac(oc, acc):
        P.op("dve", lambda e: e.tensor_tensor(out=h[oc].ap, in0=acc.ap[:, 0:NT], in1=h[oc].ap, op=ALU.add),
             reads=[acc, h[oc]], writes=[h[oc]])

    for ti in range(NTILE):
        t0 = ti * NT
        for k in range(KC):
            P.dma("sp" if k % 2 == 0 else "act", h[k].ap, hT.ap[k * 128:(k + 1) * 128, t0:t0 + NT], writes=[h[k]])
        for k in range(KC):
            P.dma("act" if k % 2 == 0 else "sp", xin[k].r32(), zT.ap[k * 128:(k + 1) * 128, t0:t0 + NT].bitcast(F32R),
                  writes=[xin[k]])
        if layer0:
            for hd in range(20):
                sa = ssb[0]
                if hd < 4:
                    P.dma("sp", sa.ap, ss_d.ap[2 * hd:2 * hd + 1, t0:t0 + NT].partition_broadcast(128), writes=[sa])
                    sb_ = ssb[1]
                    P.dma("sp", sb_.ap, ss_d.ap[2 * hd + 1:2 * hd + 2, t0:t0 + NT].partition_broadcast(128), writes=[sb_])
                    P.op("dve", lambda e: e.tensor_tensor(out=sa.ap, in0=sa.ap, in1=sb_.ap, op=ALU.add), reads=[sa, sb_], writes=[sa])
                    dv = 512.0
                    chunks = list(range(hd * 4, hd * 4 + 4))
                else:
                    P.dma("sp", sa.ap, ss_d.ap[8 + hd - 4:8 + hd - 3, t0:t0 + NT].partition_broadcast(128), writes=[sa])
                    dv = 128.0
                    chunks = [16 + hd - 4]
                P.op("dve", lambda e: e.tensor_scalar(out=sa.ap, in0=sa.ap, scalar1=1.0 / dv, scalar2=EPS, op0=ALU.mult, op1=ALU.add),
                     reads=[sa], writes=[sa])
                P.op("act", lambda e: e.activation(out=sa.ap, in_=sa.ap, func=AF.Sqrt), reads=[sa], writes=[sa])
                P.op("dve", lambda e: e.reciprocal(out=sa.ap, in_=sa.ap), reads=[sa], writes=[sa])
                for k in chunks:
                    P.op("dve", lambda e, k=k: e.scalar_tensor_tensor(out=xin[k].r32(), in0=xin[k].ap, scalar=g_h.ap[:, k:k + 1],
                                                                      in1=sa.ap, op0=ALU.mult, op1=ALU.mult),
                         reads=[xin[k], g_h, sa], writes=[xin[k]])
        gemm_fm(c, w_mo, 0, KC, 0, KC, xin, resid_evac)
        rstd = rms_stats(c, h, float(D), EPS)
        rms_apply(c, h, xin, g_x, rstd)
        for hd in range(4):
            P.dma("act", kth.ap, kT.ap[hd * 1024:(hd + 1) * 1024, :].bitcast(F32R).rearrange("(k p) m -> p k m", p=128),
                  writes=[kth])
            P.dma("act", vh.ap, vM.ap[:, hd * 1024:(hd + 1) * 1024].bitcast(F32R).rearrange("(t p) d -> p t d", p=128),
                  writes=[vh])

            def q_evac(oc, acc):
                P.op("act", lambda e: e.activation(out=qh[oc].r32(), in_=acc.ap[:, 0:NT], func=AF.Copy, scale=1.0 / 32.0),
                     reads=[acc], writes=[qh[oc]])
            gemm_fm(c, w_q, 0, KC, hd * 1024, 8, xin, q_evac)
            for ts in range(NSUB):
                sp_ = sps[ts % 2]; pe_ = pexp[ts % 2]; st = stat[ts % 2]
                for j in range(8):
                    P.op("pe", lambda e, j=j: e.matmul(sp_.ap[:, 0:NMEM], lhsT=qh[j].r32()[:, ts * 128:(ts + 1) * 128],
                                                       rhs=kth.ap[:, j, :], start=(j == 0), stop=(j == 7)),
                         reads=[qh[j], kth], writes=[sp_])
                P.op("dve", lambda e: e.reduce_max(out=st.ap[:, 0:1], in_=sp_.ap[:, 0:NMEM], axis=AX.X), reads=[sp_], writes=[st])
                P.op("dve", lambda e: e.tensor_single_scalar(out=st.ap[:, 1:2], in_=st.ap[:, 0:1], scalar=-1.0, op=ALU.mult),
                     reads=[st], writes=[st])
                P.op("act", lambda e: e.activation(out=pe_.ap, in_=sp_.ap[:, 0:NMEM], func=AF.Exp, bias=st.ap[:, 1:2], scale=1.0,
                                                   accum_out=st.ap[:, 2:3]), reads=[sp_, st], writes=[pe_, st])
                P.op("dve", lambda e: e.reciprocal(out=st.ap[:, 3:4], in_=st.ap[:, 2:3]), reads=[st], writes=[st])
                P.op("dve", lambda e: e.tensor_scalar(out=pe_.ap, in0=pe_.ap, scalar1=st.ap[:, 3:4], scalar2=None, op0=ALU.mult),
                     reads=[pe_, st], writes=[pe_])
                for mt in range(2):
                    tp = trps[mt]
                    P.op("pe", lambda e, mt=mt, tp=tp: e.transpose(tp.ap[:, 0:128], pe_.ap[:, mt * 128:(mt + 1) * 128], ident.ap),
                         reads=[pe_, ident], writes=[tp])
                    P.op("act" if mt == 0 else "dve",
                         (lambda e, mt=mt, tp=tp: e.activation(out=pT.r32()[:, mt, ts * 128:(ts + 1) * 128], in_=tp.ap[:, 0:128], func=AF.Copy))
                         if mt == 0 else
                         (lambda e, mt=mt, tp=tp: e.tensor_copy(out=pT.r32()[:, mt, ts * 128:(ts + 1) * 128], in_=tp.ap[:, 0:128])),
                         reads=[tp], writes=[pT])
            for j in range(8):
                acc = next_acc(c)
                for mt in range(2):
                    P.op("pe", lambda e, j=j, mt=mt: e.matmul(acc.ap[:, 0:NT], lhsT=vh.ap[:, mt, j * 128:(j + 1) * 128],
                                                              rhs=pT.r32()[:, mt, :], start=(mt == 0), stop=(mt == 1)),
                         reads=[vh, pT], writes=[acc])
                oc = hd * 8 + j
                P.op("act", lambda e, oc=oc, acc=acc: e.activation(out=ot[oc].r32(), in_=acc.ap[:, 0:NT], func=AF.Copy),
                     reads=[acc], writes=[ot[oc]])
        gemm_fm(c, w_o, 0, KC, 0, KC, ot, resid_evac)
        rstd = rms_stats(c, h, float(D), EPS)
        rms_apply(c, h, xin, g_m, rstd)
        for hb in range(4 * KC // HB):
            def up_evac(oc, acc):
                rt = relu_t[oc % 2]
                P.op("act", lambda e: e.activation(out=rt.ap, in_=acc.ap[:, 0:NT], func=AF.Relu), reads=[acc], writes=[rt])
                P.op("pool", lambda e: e.tensor_tensor(out=hid[oc].r32(), in0=rt.ap, in1=rt.ap, op=ALU.mult), reads=[rt], writes=[hid[oc]])
            gemm_fm(c, w_up, 0, KC, hb * HB * 128, HB, xin, up_evac)
            gemm_fm(c, w_dn, hb * HB * 128, HB, 0, KC, hid, resid_evac)
        if final:
            rstd = rms_stats(c, h, float(D), EPS)
            rms_apply(c, h, xin, g_f, rstd)
            src = xin
        else:
            src = h
        for k in range(KC):
            P.dma("sp" if k % 2 == 0 else "act", oT.ap[k * 128:(k + 1) * 128, t0:t0 + NT], src[k].ap, reads=[src[k]], writes=[oT], join=True)
    P.finish([oT], "sp")
    P.finish([oT], "act")
    P.close()
    return nc, P


NFM = 18
L = 64


def build_mixA(TT, NT=512, limit=None):
    nc = bass.Bass("TRN2", target_bir_lowering=False)
    nc.dge_precook = False
    P = Prog(nc)
    P.limit = limit
    NG = TT // NT
    NCH = NT // L
    xT = P.dram("xT", [D, TT], F32, "ExternalInput")
    wfm = P.dram("wfm", [D, NFM * 128], F32, "ExternalInput")
    g_d = P.dram("g_mix", [128, KC], F32, "ExternalInput")
    par_d = P.dram("par", [128, 8], F32, "ExternalInput")
    msk_d = P.dram("masks", [64, 3 * 64], F32, "ExternalInput")
    scm_d = P.dram("scanmask", [128, 2 * NT], F32, "ExternalInput")
    ident_d = P.dram("ident", [128, 128], F32, "ExternalInput")
    zT = P.dram("zT", [512, TT], F32, "ExternalOutput")
    ssO = P.dram("ssO", [3, TT], F32, "ExternalOutput")

    c = make_common(P, NT, n_wbuf=2, n_acc=2, own_ssps=False)
    g_mix = P.sbuf("g_mix_sb", [128, KC]); P.dma("sp", g_mix.ap, g_d.ap, writes=[g_mix])
    par = P.sbuf("par_sb", [128, 8]); P.dma("sp", par.ap, par_d.ap, writes=[par])
    msk = P.sbuf("msk_sb", [64, 192]); P.dma("sp", msk.ap, msk_d.ap, writes=[msk])
    scm = P.sbuf("scm_sb", [128, 2 * NT]); P.dma("sp", scm.ap, scm_d.ap, writes=[scm])
    ident = P.sbuf("ident_sb", [128, 128]); P.dma("sp", ident.ap, ident_d.ap, writes=[ident])
    mask01 = msk.ap[:, 0:64]; maskneg = msk.ap[:, 64:128]
    rmask = scm.ap[:, 0:NT]; rneg = scm.ap[:, NT:2 * NT]
    pp = P.sbuf("pp", [128, 8])
    P.op("dve", lambda e: e.tensor_single_scalar(out=pp.ap[:, 0:2], in_=par.ap[:, 0:2], scalar=1.0 / 15.0, op=ALU.mult), reads=[par], writes=[pp])
    P.op("dve", lambda e: e.tensor_tensor(out=pp.ap[:, 2:4], in0=par.ap[:, 2:4], in1=par.ap[:, 4:6], op=ALU.subtract), reads=[par, pp], writes=[pp])
    P.op("act", lambda e: e.activation(out=pp.ap[:, 2:4], in_=pp.ap[:, 2:4], func=AF.Sigmoid), reads=[pp], writes=[pp])
    P.op("dve", lambda e: e.tensor_scalar(out=pp.ap[:, 4:6], in0=pp.ap[:, 2:4], scalar1=-1.0, scalar2=1.0, op0=ALU.mult, op1=ALU.add), reads=[pp], writes=[pp])

    x_all = P.sbuf("xg", [128, KC, NT]); xg = P.subs(x_all, KC)
    pf_all = P.sbuf("pf", [128, NFM, NT]); pf = P.subs(pf_all, NFM)
    gt = {n: P.sbuf("g_" + n, [128, NT]) for n in ["b", "a", "cm", "gi", "fl"]}
    gt["M"] = gt["cm"]; gt["negM"] = gt["cm"]; gt["w"] = c.nrm_tmp[0]
    mv = P.sbuf("mvec", [128, 2 * NCH + 2])
    pk = P.sbuf("pk", [64, NT])
    qg_all = P.sbuf("qg", [128, 2, NT]); qg = P.subs(qg_all, 2)
    hA = [P.sbuf(f"hA{i}", [128, NT]) for i in range(2)]
    heA = [P.sbuf(f"heA{i}", [128, NT]) for i in range(2)]
    hqt = [P.sbuf(f"hqt{i}", [128, NT]) for i in range(2)]
    hkt = [P.sbuf(f"hkt{i}", [128, NT]) for i in range(2)]
    hkh = hA
    Cst = [P.sbuf(f"Cst{i}", [128, 256]) for i in range(2)]
    nbc = [P.sbuf(f"nbc{i}", [128, 128]) for i in range(2)]
    Sst = [P.sbuf(f"Sst{i}", [128, 128]) for i in range(2)]
    for t_ in Cst + nbc + Sst:
        P.op("pool", lambda e, t_=t_: e.memset(t_.ap, 0.0), writes=[t_])
    P.op("dve", lambda e: e.memset(mv.ap, 0.0), writes=[mv])
    tmA = P.sbuf("tmA", [64, 512])
    tmB = P.sbuf("tmB", [64, 320])
    tmC = P.sbuf("tmC", [64, 256])
    kw = P.sbuf("kw", [64, 256])
    dmt = P.sbuf("dmt", [64, 64]); smt = P.sbuf("smt", [64, 64])
    hsm = [P.sbuf(f"hsm{i}", [64, 64]) for i in range(2)]
    rec = P.sbuf("rec", [128, 64])
    hm_all = P.sbuf("hm", [128, 2, NT]); hm = P.subs(hm_all, 2)
    ho = [P.sbuf(f"ho{i}", [128, NT]) for i in range(2)]
    ssr = P.sbuf("ssr", [128, NT])
    tr1 = P.psum("tr1", [128, 512]); tr2 = P.psum("tr2", [128, 512]); tr3 = None
    nd = P.psum("nd", [128, 512]); cps = P.psum("cps", [128, 512]); nps = P.psum("nps", [128, 512])
    sTp = P.psum("sTp", [128, 512])
    sT = T(sTp.ap[0:64, 0:192], "sT")
    tr2a = tr2
    f32 = lambda t: t.ap

    for g in range(NG):
        t0 = g * NT
        for k in range(KC):
            P.dma("sp" if k % 2 == 0 else "act", xg[k].r32(), xT.ap[k * 128:(k + 1) * 128, t0:t0 + NT].bitcast(F32R), writes=[xg[k]])
        rstd = rms_stats(c, xg, float(D), EPS)
        rms_apply(c, xg, xg, g_mix, rstd)

        def evac(oc, acc):
            a_ = acc.ap[:, 0:NT]; o_ = pf[oc]
            if oc in (0, 1):
                P.op("act", lambda e: e.activation(out=o_.ap, in_=a_, func=AF.Copy, scale=1.0 / 16.0), reads=[acc], writes=[o_])
            elif oc in (2, 3, 14, 15, 16, 17):
                P.op("dve", lambda e: e.tensor_copy(out=o_.ap, in_=a_), reads=[acc], writes=[o_])
            elif oc in (4, 5, 10, 11):
                P.op("act", lambda e: e.activation(out=o_.ap, in_=a_, func=AF.Sigmoid), reads=[acc], writes=[o_])
            elif oc in (6, 7):
                P.op("act", lambda e: e.activation(out=o_.ap, in_=a_, func=AF.Tanh, scale=1.0 / 15.0, bias=pp.ap[:, oc - 6:oc - 5]),
                     reads=[acc, pp], writes=[o_])
            else:
                P.op("act", lambda e: e.activation(out=o_.ap, in_=a_, func=AF.Silu), reads=[acc], writes=[o_])
        gemm_fm(c, wfm, 0, KC, 0, NFM, xg, evac)

        ti_, tf_ = pf[6], pf[7]
        b, a, cm, M, negM, gi, fl, wv = (gt[n] for n in ["b", "a", "cm", "M", "negM", "gi", "fl", "w"])
        P.op("act", lambda e: e.activation(out=tf_.ap, in_=tf_.ap, func=AF.Sigmoid, scale=15.0), reads=[tf_], writes=[tf_])
        P.op("act", lambda e: e.activation(out=tf_.ap, in_=tf_.ap, func=AF.Ln), reads=[tf_], writes=[tf_])
        P.op("dve", lambda e: e.tensor_tensor_scan(out=b.ap, data0=rmask, data1=tf_.ap, initial=0.0, op0=ALU.mult, op1=ALU.add),
             reads=[scm, tf_], writes=[b])
        P.op("dve", lambda e: e.scalar_tensor_tensor(out=a.ap, in0=ti_.ap, scalar=15.0, in1=b.ap, op0=ALU.mult, op1=ALU.subtract),
             reads=[ti_, b], writes=[a])
        P.op("dve", lambda e: e.tensor_tensor_scan(out=cm.ap, data0=rneg, data1=a.ap, initial=-1e30, op0=ALU.add, op1=ALU.max),
             reads=[scm, a], writes=[cm])
        cm_end = cm.ap.rearrange("p (c l) -> p c l", l=L)[:, :, L - 1]
        b_end = b.ap.rearrange("p (c l) -> p c l", l=L)[:, :, L - 1]
        P.op("dve", lambda e: e.tensor_tensor_scan(out=mv.ap[:, NCH:2 * NCH], data0=cm_end, data1=b_end, initial=mv.ap[:, 2 * NCH:2 * NCH + 1],
                                                   op0=ALU.max, op1=ALU.add), reads=[cm, b, mv], writes=[mv])
        P.op("dve", lambda e: e.tensor_copy(out=mv.ap[:, 0:1], in_=mv.ap[:, 2 * NCH:2 * NCH + 1]), reads=[mv], writes=[mv])
        P.op("dve", lambda e: e.tensor_copy(out=mv.ap[:, 1:NCH], in_=mv.ap[:, NCH:2 * NCH - 1]), reads=[mv], writes=[mv])
        P.op("dve", lambda e: e.tensor_copy(out=mv.ap[:, 2 * NCH:2 * NCH + 1], in_=mv.ap[:, 2 * NCH - 1:2 * NCH]), reads=[mv], writes=[mv])
        mprev_bc = mv.ap[:, 0:NCH].unsqueeze(2).to_broadcast([128, NCH, L])
        v3 = lambda t: t.ap.rearrange("p (c l) -> p c l", l=L)
        P.op("dve", lambda e: e.tensor_tensor(out=v3(M), in0=v3(cm), in1=mprev_bc, op=ALU.max), reads=[cm, mv], writes=[M])
        P.op("dve", lambda e: e.tensor_single_scalar(out=negM.ap, in_=M.ap, scalar=-1.0, op=ALU.mult), reads=[M], writes=[negM])
        P.op("dve", lambda e: e.tensor_tensor(out=v3(gi), in0=v3(negM), in1=mprev_bc, op=ALU.add), reads=[negM, mv], writes=[gi])
        P.op("act", lambda e: e.activation(out=gi.ap, in_=gi.ap, func=AF.Exp), reads=[gi], writes=[gi])
        P.op("dve", lambda e: e.tensor_tensor(out=fl.ap, in0=negM.ap, in1=b.ap, op=ALU.subtract), reads=[negM, b], writes=[fl])
        P.op("act", lambda e: e.activation(out=fl.ap, in_=fl.ap, func=AF.Exp), reads=[fl], writes=[fl])
        negML_bc = v3(negM)[:, :, L - 1:L].to_broadcast([128, NCH, L])
        P.op("dve", lambda e: e.tensor_tensor(out=v3(wv), in0=v3(a), in1=negML_bc, op=ALU.add), reads=[a, negM], writes=[wv])
        P.op("act", lambda e: e.activation(out=wv.ap, in_=wv.ap, func=AF.Exp), reads=[wv], writes=[wv])
        P.op("dve", lambda e: e.tensor_copy(out=pk.ap[0:32, :], in_=a.ap[0:32, :]), reads=[a], writes=[pk])
        P.op("dve", lambda e: e.tensor_copy(out=pk.ap[32:64, :], in_=wv.ap[32:64, :]), reads=[wv, pk], writes=[pk])
        for dch in range(2):
            P.op("pool", lambda e, dch=dch: e.tensor_tensor(out=qg[dch].ap, in0=pf[dch].ap, in1=gi.ap, op=ALU.mult),
                 reads=[pf[dch], gi], writes=[qg[dch]])

        for hh in range(2):
            sg = pf[10 + hh]
            P.op("dve", lambda e, hh=hh, sg=sg: e.tensor_scalar(out=sg.ap, in0=sg.ap, scalar1=pp.ap[:, 4 + hh:5 + hh], scalar2=pp.ap[:, 2 + hh:3 + hh],
                                                         op0=ALU.mult, op1=ALU.add), reads=[sg, pp], writes=[sg])
            P.op("act", lambda e, hh=hh, sg=sg: e.activation(out=hA[hh].ap, in_=sg.ap, func=AF.Ln), reads=[sg], writes=[hA[hh]])
            P.op("dve", lambda e, hh=hh: e.tensor_tensor_scan(out=hA[hh].ap, data0=rmask, data1=hA[hh].ap, initial=0.0, op0=ALU.mult, op1=ALU.add),
                 reads=[scm, hA[hh]], writes=[hA[hh]])
            P.op("act", lambda e, hh=hh: e.activation(out=heA[hh].ap, in_=hA[hh].ap, func=AF.Exp), reads=[hA[hh]], writes=[heA[hh]])
            P.op("pool", lambda e, hh=hh: e.tensor_tensor(out=hqt[hh].ap, in0=pf[8 + hh].ap, in1=heA[hh].ap, op=ALU.mult),
                 reads=[pf[8 + hh], heA[hh]], writes=[hqt[hh]])
            P.op("dve", lambda e, hh=hh, sg=sg: e.tensor_scalar(out=sg.ap, in0=sg.ap, scalar1=-1.0, scalar2=1.0, op0=ALU.mult, op1=ALU.add),
                 reads=[sg], writes=[sg])
            P.op("act", lambda e, hh=hh: e.activation(out=hkt[hh].ap, in_=hA[hh].ap, func=AF.Exp, scale=-1.0), reads=[hA[hh]], writes=[hkt[hh]])
            P.op("pool", lambda e, hh=hh, sg=sg: e.tensor_tensor(out=hkt[hh].ap, in0=hkt[hh].ap, in1=sg.ap, op=ALU.mult),
                 reads=[hkt[hh], sg], writes=[hkt[hh]])
            eAl_bc = heA[hh].ap.rearrange("p (c l) -> p c l", l=L)[:, :, L - 1:L].to_broadcast([128, NCH, L])
            P.op("dve", lambda e, hh=hh: e.tensor_tensor(out=v3(hkh[hh]), in0=v3(hkt[hh]), in1=eAl_bc, op=ALU.mult),
                 reads=[hkt[hh], heA[hh]], writes=[hkh[hh]])

        for ch in range(NCH):
            tc = slice(ch * L, (ch + 1) * L)
            for j in range(2):
                P.op("pe", lambda e, j=j: e.transpose(tr1.ap[0:64, j * 128:(j + 1) * 128], pf[2 + j].ap[:, tc], ident.ap), reads=[pf[2 + j], ident], writes=[tr1])
            for j in range(2):
                P.op("pe", lambda e, j=j: e.transpose(tr1.ap[0:64, 256 + j * 128:256 + (j + 1) * 128], pf[14 + j].ap[:, tc], ident.ap), reads=[pf[14 + j], ident], writes=[tr1])
            P.op("act", lambda e: e.activation(out=tmA.ap, in_=tr1.ap[0:64, :], func=AF.Copy), reads=[tr1], writes=[tmA])
            P.op("pe", lambda e: e.transpose(tr2.ap[0:64, 0:64], pk.ap[:, tc], ident.ap[0:64, 0:64]), reads=[pk, ident], writes=[tr2a])
            for j in range(2):
                P.op("pe", lambda e, j=j: e.transpose(tr2.ap[0:64, 64 + j * 128:64 + (j + 1) * 128], pf[16 + j].ap[:, tc], ident.ap), reads=[pf[16 + j], ident], writes=[tr2a])
            P.op("dve", lambda e: e.tensor_copy(out=tmB.ap, in_=tr2.ap[0:64, 0:320]), reads=[tr2a], writes=[tmB])
            for j in range(2):
                P.op("pe", lambda e, j=j: e.transpose(sTp.ap[0:64, 192 + j * 128:192 + (j + 1) * 128], hkh[j].ap[:, tc], ident.ap), reads=[hkh[j], ident], writes=[sTp])
            P.op("act", lambda e: e.activation(out=tmC.ap, in_=sTp.ap[0:64, 192:448], func=AF.Copy), reads=[sTp], writes=[tmC])
            ktm = tmA.ap[:, 0:256]; vtm = tmA.ap[:, 256:512]
            a_col = tmB.ap[:, 0:1]; w_col = tmB.ap[:, 32:33]
            for dch in range(2):
                P.op("pe", lambda e, dch=dch: e.matmul(sT.ap[:, 0:64], lhsT=pf[2 + dch].ap[:, tc], rhs=pf[dch].ap[:, tc], start=(dch == 0), stop=(dch == 1)),
                     reads=[pf[2 + dch], pf[dch]], writes=[sTp])
            for hh in range(2):
                P.op("pe", lambda e, hh=hh: e.matmul(sT.ap[:, 64 + hh * 64:128 + hh * 64], lhsT=hkt[hh].ap[:, tc], rhs=hqt[hh].ap[:, tc], start=True, stop=True),
                     reads=[hkt[hh], hqt[hh]], writes=[sTp])
            P.op("dve", lambda e: e.scalar_tensor_tensor(out=dmt.ap, in0=negM.ap[0:64, tc], scalar=a_col, in1=maskneg, op0=ALU.add, op1=ALU.add),
                 reads=[negM, tmB, msk], writes=[dmt])
            P.op("act", lambda e: e.activation(out=dmt.ap, in_=dmt.ap, func=AF.Exp), reads=[dmt], writes=[dmt])
            P.op("dve", lambda e: e.tensor_tensor(out=smt.ap, in0=sT.ap[:, 0:64], in1=dmt.ap, op=ALU.mult), reads=[sTp, dmt], writes=[smt])
            for hh in range(2):
                P.op("dve", lambda e, hh=hh: e.tensor_tensor(out=hsm[hh].ap, in0=sT.ap[:, 64 + hh * 64:128 + hh * 64], in1=mask01, op=ALU.mult),
                     reads=[sTp, msk], writes=[hsm[hh]])
            for vch in range(2):
                o_ = nd.ap[:, vch * 64:(vch + 1) * 64]
                P.op("pe", lambda e, vch=vch, o_=o_: e.matmul(o_, lhsT=vtm[:, vch * 128:(vch + 1) * 128], rhs=smt.ap, start=True, stop=False),
                     reads=[tmA, smt], writes=[nd])
                for dch in range(2):
                    P.op("pe", lambda e, vch=vch, dch=dch, o_=o_: e.matmul(o_, lhsT=Cst[dch].ap[:, vch * 128:(vch + 1) * 128], rhs=qg[dch].ap[:, tc],
                                                                          start=False, stop=(dch == 1)), reads=[Cst[dch], qg[dch]], writes=[nd])
            o_ = nd.ap[:, 128:192]
            P.op("pe", lambda e: e.matmul(o_, lhsT=c.ones.ap[0:64, :], rhs=smt.ap, start=True, stop=False), reads=[c.ones, smt], writes=[nd])
            for dch in range(2):
                P.op("pe", lambda e, dch=dch: e.matmul(o_, lhsT=nbc[dch].ap, rhs=qg[dch].ap[:, tc], start=False, stop=(dch == 1)),
                     reads=[nbc[dch], qg[dch]], writes=[nd])
            for hh in range(2):
                oo = nd.ap[:, 192 + hh * 64:256 + hh * 64]
                P.op("pe", lambda e, hh=hh, oo=oo: e.matmul(oo, lhsT=tmB.ap[:, 64 + hh * 128:192 + hh * 128], rhs=hsm[hh].ap, start=True, stop=False),
                     reads=[tmB, hsm[hh]], writes=[nd])
                P.op("pe", lambda e, hh=hh, oo=oo: e.matmul(oo, lhsT=Sst[hh].ap, rhs=hqt[hh].ap[:, tc], start=False, stop=True),
                     reads=[Sst[hh], hqt[hh]], writes=[nd])
            P.op("act", lambda e: e.activation(out=rec.ap, in_=nd.ap[:, 128:192], func=AF.Abs), reads=[nd], writes=[rec])
            P.op("dve", lambda e: e.tensor_tensor(out=rec.ap, in0=rec.ap, in1=fl.ap[:, tc], op=ALU.max), reads=[rec, fl], writes=[rec])
            P.op("dve", lambda e: e.reciprocal(out=rec.ap, in_=rec.ap), reads=[rec], writes=[rec])
            for vch in range(2):
                P.op("dve", lambda e, vch=vch: e.tensor_tensor(out=hm[vch].ap[:, tc], in0=nd.ap[:, vch * 64:(vch + 1) * 64], in1=rec.ap, op=ALU.mult),
                     reads=[nd, rec], writes=[hm[vch]])
            for hh in range(2):
                P.op("act", lambda e, hh=hh: e.activation(out=ho[hh].ap[:, tc], in_=nd.ap[:, 192 + hh * 64:256 + hh * 64], func=AF.Copy),
                     reads=[nd], writes=[ho[hh]])
            P.op("dve", lambda e: e.tensor_scalar(out=kw.ap, in0=ktm, scalar1=w_col, scalar2=None, op0=ALU.mult), reads=[tmA, tmB], writes=[kw])
            for dch in range(2):
                P.op("pe", lambda e, dch=dch: e.matmul(cps.ap[:, dch * 256:(dch + 1) * 256], lhsT=kw.ap[:, dch * 128:(dch + 1) * 128], rhs=vtm, start=True, stop=True),
                     reads=[kw, tmA], writes=[cps])
            for dch in range(2):
                P.op("pe", lambda e, dch=dch: e.matmul(nps.ap[:, dch * 128:(dch + 1) * 128], lhsT=kw.ap[:, dch * 128:(dch + 1) * 128], rhs=c.ones.ap[0:64, :], start=True, stop=True),
                     reads=[kw, c.ones], writes=[nps])
            for hh in range(2):
                P.op("pe", lambda e, hh=hh: e.matmul(nps.ap[:, 256 + hh * 128:384 + hh * 128], lhsT=tmC.ap[:, hh * 128:(hh + 1) * 128],
                                                     rhs=tmB.ap[:, 64 + hh * 128:192 + hh * 128], start=True, stop=True), reads=[tmC, tmB], writes=[nps])
            dec = gi.ap[:, ch * L + L - 1:ch * L + L]
            for dch in range(2):
                P.op("dve", lambda e, dch=dch: e.scalar_tensor_tensor(out=Cst[dch].ap, in0=Cst[dch].ap, scalar=dec, in1=cps.ap[:, dch * 256:(dch + 1) * 256],
                                                                      op0=ALU.mult, op1=ALU.add), reads=[Cst[dch], gi, cps], writes=[Cst[dch]])
                P.op("dve", lambda e, dch=dch: e.scalar_tensor_tensor(out=nbc[dch].ap, in0=nbc[dch].ap, scalar=dec, in1=nps.ap[:, dch * 128:(dch + 1) * 128],
                                                                      op0=ALU.mult, op1=ALU.add), reads=[nbc[dch], gi, nps], writes=[nbc[dch]])
            for hh in range(2):
                eal = heA[hh].ap[:, ch * L + L - 1:ch * L + L]
                P.op("dve", lambda e, hh=hh, eal=eal: e.scalar_tensor_tensor(out=Sst[hh].ap, in0=Sst[hh].ap, scalar=eal, in1=nps.ap[:, 256 + hh * 128:384 + hh * 128],
                                                                            op0=ALU.mult, op1=ALU.add), reads=[Sst[hh], heA[hh], nps], writes=[Sst[hh]])

        outs = [(hm[0], pf[4]), (hm[1], pf[5]), (ho[0], pf[12]), (ho[1], pf[13])]
        groups = [[0, 1], [2], [3]]
        for gi_, idxs in enumerate(groups):
            for ii, ix in enumerate(idxs):
                sq = c.sq[c.sqi % 2]; c.sqi += 1
                src = outs[ix][0]
                P.op("act", lambda e, sq=sq, src=src: e.activation(out=sq.r32(), in_=src.ap, func=AF.Square), reads=[src], writes=[sq])
                P.op("pe", lambda e, sq=sq, ii=ii, idxs=idxs: e.matmul(c.accs[1].ap[:, 0:NT], lhsT=c.ones.r32(), rhs=sq.r32(), start=(ii == 0), stop=(ii == len(idxs) - 1)),
                     reads=[c.ones, sq], writes=[c.accs[1]])
            P.op("dve", lambda e: e.tensor_copy(out=ssr.ap, in_=c.accs[1].ap[:, 0:NT]), reads=[c.accs[1]], writes=[ssr])
            P.dma("sp", ssO.ap[gi_:gi_ + 1, t0:t0 + NT], ssr.ap[0:1, :], reads=[ssr], writes=[ssO], join=True)
        for ix, (src, gate) in enumerate(outs):
            P.op("pool", lambda e, ix=ix, src=src, gate=gate: e.tensor_tensor(out=gate.ap, in0=src.ap, in1=gate.ap, op=ALU.mult),
                 reads=[src, gate], writes=[gate])
            P.dma("sp", zT.ap[ix * 128:(ix + 1) * 128, t0:t0 + NT], gate.ap, reads=[gate], writes=[zT], join=True)
    P.finish([zT, ssO], "sp")
    P.close()
    return nc, P


IN_SIZES = (1024, 1024, 2048, 2048, 4, 4, 2048, 2048, 2048, 2048)


def fm(a):
    return np.ascontiguousarray(np.asarray(a).T)


def pp(g):
    g = np.asarray(g, dtype=np.float32).reshape(-1)
    return np.ascontiguousarray(g.reshape(-1, 128).T)


def mixA_consts(NT):
    s = np.arange(64)[:, None]; t = np.arange(64)[None, :]
    m01 = (s <= t).astype(np.float32)
    mneg = np.where(s <= t, 0.0, -1e30).astype(np.float32)
    masks = np.concatenate([m01, mneg, np.zeros((64, 64), np.float32)], axis=1)
    st = (np.arange(NT) % 64 == 0)
    rmask = np.where(st, 0.0, 1.0).astype(np.float32)
    rneg = np.where(st, -1e30, 0.0).astype(np.float32)
    scan = np.broadcast_to(np.concatenate([rmask, rneg])[None, :], (128, 2 * NT)).copy()
    return masks, scan


def mixA_core_inputs(inp, c):
    W = inp["ab_w_in"][0]
    offs = np.cumsum([0] + list(IN_SIZES))
    hm, half = c // 2, c % 2
    r = np.arange
    cols = np.concatenate([
        offs[0] + hm * 256 + r(256), offs[1] + hm * 256 + r(256), offs[3] + hm * 512 + half * 256 + r(256),
        np.full(128, offs[4] + hm), np.full(128, offs[5] + hm),
        offs[6] + 2 * c * 128 + r(256), offs[7] + 2 * c * 128 + r(256), offs[9] + 2 * c * 128 + r(256),
        offs[2] + hm * 512 + half * 256 + r(256), offs[8] + 2 * c * 128 + r(256)])
    wfm = np.ascontiguousarray(W[:, cols])
    par = np.zeros((128, 8), np.float32)
    par[:, 0] = inp["mlstm_b_i"][0, hm]
    par[:, 1] = inp["mlstm_b_f"][0, hm]
    lg = inp["hgrn_lb_logits"]
    for hh in range(2):
        par[:, 2 + hh] = lg[0, (2 * c + hh) * 128:(2 * c + hh + 1) * 128]
        par[:, 4 + hh] = lg[1, (2 * c + hh) * 128:(2 * c + hh + 1) * 128]
    return wfm, par


CN = 64
EXPM05 = 0.6065306597126334


def build_mixC(TT, NT=256, limit=None):
    nc = bass.Bass("TRN2", target_bir_lowering=False)
    nc.dge_precook = False
    P = Prog(nc)
    P.limit = limit
    NG = TT // NT
    NCH = NT // CN
    hT = P.dram("hT", [D, TT], F32, "ExternalInput")
    g_d = P.dram("g_mix", [128, KC], F32, "ExternalInput")
    mu_d = P.dram("mu", [128, 6 * KC], F32, "ExternalInput")
    wr_d = P.dram("wr", [D, 512], F32, "ExternalInput")
    wk_d = P.dram("wk", [D, 512], F32, "ExternalInput")
    wv_d = P.dram("wv", [D, 512], F32, "ExternalInput")
    w1_d = P.dram("w1", [D, 128], F32, "ExternalInput")
    a1_d = P.dram("a1", [D, 128], F32, "ExternalInput")
    g1_d = P.dram("g1", [D, 480], F32, "ExternalInput")
    w2_d = P.dram("w2", [128, 512], F32, "ExternalInput")
    a2_d = P.dram("a2", [128, 512], F32, "ExternalInput")
    g2_d = P.dram("g2", [480, 512], F32, "ExternalInput")
    par_d = P.dram("par", [128, 28], F32, "ExternalInput")
    msk_d = P.dram("masks", [64, 192], F32, "ExternalInput")
    scm_d = P.dram("scanmask", [128, NT], F32, "ExternalInput")
    ident_d = P.dram("ident", [128, 128], F32, "ExternalInput")
    bones_d = P.dram("bones", [128, 128], F32, "ExternalInput")
    zT = P.dram("zT", [512, TT], F32, "ExternalOutput")

    def ld(name, src, shape, dtype=F32, ap=None):
        t = P.sbuf(name, shape, dtype)
        P.dma("sp", t.ap, ap if ap is not None else src.ap, writes=[t])
        return t
    g_mix = ld("g_mix_sb", g_d, [128, KC])
    mu = ld("mu_sb", mu_d, [128, 6 * KC])
    par = ld("par_sb", par_d, [128, 28])
    msk = ld("msk_sb", msk_d, [64, 192])
    scm = ld("scm_sb", scm_d, [128, NT])
    ident = ld("ident_sb", ident_d, [128, 128])
    bones = ld("bones_sb", bones_d, [128, 128])
    w2 = ld("w2_sb", w2_d, [128, 512], F32R, w2_d.ap.bitcast(F32R))
    a2 = ld("a2_sb", a2_d, [128, 512], F32R, a2_d.ap.bitcast(F32R))
    g2 = ld("g2_sb", g2_d, [120, 4, 512], F32R, g2_d.ap.bitcast(F32R).rearrange("(k p) c -> p k c", p=120))
    omu = P.sbuf("omu", [128, 6 * KC])
    P.op("dve", lambda e: e.tensor_scalar(out=omu.ap, in0=mu.ap, scalar1=-1.0, scalar2=1.0, op0=ALU.mult, op1=ALU.add), reads=[mu], writes=[omu])
    m_le = msk.ap[:, 0:64]; m_lt = msk.ap[:, 64:128]; m_gt = msk.ap[:, 128:192]
    PW0, PA0, PKK, PKA, PRK, PLW, PLB = ((lambda i: (lambda pr: par.ap[:, 4 * i + pr:4 * i + pr + 1]))(i) for i in range(7))

    ones = P.sbuf("ones", [128, 128]); ones_f = P.sbuf("ones_f", [128, 128])
    P.op("dve", lambda e: e.memset(ones_f.ap, 1.0), writes=[ones_f])
    P.op("dve", lambda e: e.tensor_copy(out=ones.r32(), in_=ones_f.ap), reads=[ones_f], writes=[ones])

    pb = [P.psum(f"pb{i}", [128, 512]) for i in range(8)]
    pbi = [0]

    def nb():
        b = pb[pbi[0] % 8]; pbi[0] += 1
        return b

    hx_all = P.sbuf("hx", [128, KC, NT + 1]); hx = P.subs(hx_all, KC)
    P.op("pool", lambda e: e.memset(hx_all.ap[:, :, 0:1], 0.0), writes=hx)
    xs = [P.sbuf(f"xs{i}", [128, NT]) for i in range(3)]
    xt = [P.sbuf(f"xt{i}", [128, NT]) for i in range(3)]
    KB = 4
    wk_ = [P.sbuf(f"wkb{i}", [128, KB, 512], F32R) for i in range(2)]
    sq = [P.sbuf(f"sq{i}", [128, NT]) for i in range(2)]
    rstd = P.sbuf("rstd", [128, NT])
    lora = [P.sbuf(f"lora{i}", [128, NT]) for i in range(2)]
    lg = P.sbuf("lorag", [120, 4, NT])
    fmn = ["r", "k", "v", "a", "g", "kk", "bon", "lw", "P", "y"]
    fmt = {n: P.sbuf("t_" + n, [128, 4, NT]) for n in fmn}
    t_r, t_k, t_v, t_a, t_g, t_kk, t_bon, t_lw, t_P, t_y = (fmt[n] for n in fmn)
    tmp4 = P.sbuf("tmp4", [128, 4, NT])
    tmB = P.sbuf("tmB", [64, 512]); tmK = P.sbuf("tmK", [64, 512]); tmA = P.sbuf("tmA", [64, 512]); tmV = P.sbuf("tmV", [64, 512])
    Nn = [P.sbuf(f"Nn{i}", [64, 512]) for i in range(2)]
    Nt = [P.sbuf(f"Nt{i}", [64, 512]) for i in range(2)]
    Tt = P.sbuf("Tt", [64, 512]); RbT = P.sbuf("RbT", [64, 512]); RkT = P.sbuf("RkT", [64, 512]); Aak = P.sbuf("Aak", [64, 512])
    Wt = P.sbuf("Wt", [128, 4, 64]); Gt = P.sbuf("Gt", [64, 512]); Usb = P.sbuf("Usb", [64, 512])
    Hst = P.sbuf("Hst", [128, 4, 64]); Htmp = P.sbuf("Htmp", [128, 4, 64])
    P.op("pool", lambda e: e.memset(Hst.ap, 0.0), writes=[Hst])
    identI = P.sbuf("identI", [64, 8, 64])
    for h_ in range(8):
        P.op("pool", lambda e, h_=h_: e.tensor_copy(out=identI.ap[:, h_, :], in_=ident.ap[0:64, 0:64]), reads=[ident], writes=[identI])
    v8 = lambda t: t.ap.rearrange("p (h c) -> p h c", c=64)
    bc8 = lambda m: m.unsqueeze(1).to_broadcast([64, 8, 64])

    projs = [(0, wr_d, 512), (1, w1_d, 128), (2, wk_d, 512), (3, wv_d, 512), (4, a1_d, 128), (5, g1_d, 480)]
    xsi = [0]
    wki = [0]

    for g in range(NG):
        t0 = g * NT
        if g > 0:
            P.op("pool", lambda e: e.tensor_copy(out=hx_all.ap[:, :, 0:1], in_=hx_all.ap[:, :, NT:NT + 1]), reads=hx, writes=hx)
        for k in range(KC):
            P.dma("sp" if k % 2 == 0 else "act", hx[k].ap[:, 1:NT + 1], hT.ap[k * 128:(k + 1) * 128, t0:t0 + NT], writes=[hx[k]])
        ssps = nb()
        for k in range(KC):
            s_ = sq[k % 2]
            P.op("act", lambda e, k=k, s_=s_: e.activation(out=s_.r32(), in_=hx[k].ap[:, 1:NT + 1], func=AF.Square), reads=[hx[k]], writes=[s_])
            P.op("pe", lambda e, k=k, s_=s_: e.matmul(ssps.ap[:, 0:NT], lhsT=ones.r32(), rhs=s_.r32(), start=(k == 0), stop=(k == KC - 1)),
                 reads=[ones, s_], writes=[ssps])
        P.op("dve", lambda e: e.tensor_scalar(out=rstd.ap, in0=ssps.ap[:, 0:NT], scalar1=1.0 / D, scalar2=EPS, op0=ALU.mult, op1=ALU.add), reads=[ssps], writes=[rstd])
        P.op("act", lambda e: e.activation(out=rstd.ap, in_=rstd.ap, func=AF.Sqrt), reads=[rstd], writes=[rstd])
        P.op("dve", lambda e: e.reciprocal(out=rstd.ap, in_=rstd.ap), reads=[rstd], writes=[rstd])
        for k in range(KC):
            P.op("dve", lambda e, k=k: e.scalar_tensor_tensor(out=hx[k].ap[:, 1:NT + 1], in0=hx[k].ap[:, 1:NT + 1], scalar=g_mix.ap[:, k:k + 1], in1=rstd.ap,
                                                              op0=ALU.mult, op1=ALU.mult), reads=[hx[k], g_mix, rstd], writes=[hx[k]])
        for (j, Wd, ncols) in projs:
            ocw = 120 if ncols == 480 else 128
            noc = ncols // ocw
            accs = [nb() for _ in range(noc)]
            for k in range(KC):
                if k % KB == 0:
                    wt = wk_[wki[0] % 2]; wki[0] += 1
                    P.dma("sp", wt.ap[:, :, 0:ncols], Wd.ap[k * 128:(k + KB) * 128, :].bitcast(F32R).rearrange("(k p) c -> p k c", p=128), writes=[wt])
                x_ = xs[xsi[0] % 3]; xt_ = xt[xsi[0] % 3]; xsi[0] += 1
                mcol = j * KC + k
                P.op("act", lambda e, k=k, xt_=xt_, mcol=mcol: e.activation(out=xt_.ap, in_=hx[k].ap[:, 0:NT], func=AF.Copy, scale=mu.ap[:, mcol:mcol + 1]),
                     reads=[hx[k], mu], writes=[xt_])
                P.op("dve", lambda e, k=k, x_=x_, xt_=xt_, mcol=mcol: e.scalar_tensor_tensor(out=x_.r32(), in0=hx[k].ap[:, 1:NT + 1], scalar=omu.ap[:, mcol:mcol + 1],
                                                                                            in1=xt_.ap, op0=ALU.mult, op1=ALU.add),
                     reads=[hx[k], omu, xt_], writes=[x_])
                for oc in range(noc):
                    P.op("pe", lambda e, oc=oc, k=k, wt=wt, x_=x_: e.matmul(accs[oc].ap[0:ocw, 0:NT], lhsT=wt.ap[:, k % KB, oc * ocw:(oc + 1) * ocw], rhs=x_.r32(),
                                                                          start=(k == 0), stop=(k == KC - 1)), reads=[wt, x_], writes=[accs[oc]])
            if j in (0, 2, 3):
                dst = {0: t_r, 2: t_k, 3: t_v}[j]
                for oc in range(4):
                    P.op("act" if oc % 2 == 0 else "dve",
                         (lambda e, oc=oc, dst=dst: e.activation(out=dst.ap[:, oc, :], in_=accs[oc].ap[:, 0:NT], func=AF.Copy)) if oc % 2 == 0 else
                         (lambda e, oc=oc, dst=dst: e.tensor_copy(out=dst.ap[:, oc, :], in_=accs[oc].ap[:, 0:NT])),
                         reads=[accs[oc]], writes=[dst])
            elif j == 1:
                P.op("act", lambda e: e.activation(out=lora[0].r32(), in_=accs[0].ap[:, 0:NT], func=AF.Tanh), reads=[accs[0]], writes=[lora[0]])
            elif j == 4:
                P.op("act", lambda e: e.activation(out=lora[1].r32(), in_=accs[0].ap[:, 0:NT], func=AF.Copy), reads=[accs[0]], writes=[lora[1]])
            else:
                for oc in range(4):
                    P.op("act", lambda e, oc=oc: e.activation(out=lg.r32()[:, oc, :], in_=accs[oc].ap[0:120, 0:NT], func=AF.Sigmoid), reads=[accs[oc]], writes=[lg])
        for oc in range(4):
            acc = nb()
            P.op("pe", lambda e, oc=oc, acc=acc: e.matmul(acc.ap[:, 0:NT], lhsT=w2.ap[:, oc * 128:(oc + 1) * 128], rhs=lora[0].r32(), start=True, stop=True),
                 reads=[w2, lora[0]], writes=[acc])
            P.op("act", lambda e, oc=oc, acc=acc: e.activation(out=t_lw.ap[:, oc, :], in_=acc.ap[:, 0:NT], func=AF.Sigmoid, bias=PW0(oc)), reads=[acc, par], writes=[t_lw])
            acc = nb()
            P.op("pe", lambda e, oc=oc, acc=acc: e.matmul(acc.ap[:, 0:NT], lhsT=a2.ap[:, oc * 128:(oc + 1) * 128], rhs=lora[1].r32(), start=True, stop=True),
                 reads=[a2, lora[1]], writes=[acc])
            P.op("act", lambda e, oc=oc, acc=acc: e.activation(out=t_a.ap[:, oc, :], in_=acc.ap[:, 0:NT], func=AF.Sigmoid, bias=PA0(oc)), reads=[acc, par], writes=[t_a])
            acc = nb()
            for kk_ in range(4):
                P.op("pe", lambda e, oc=oc, acc=acc, kk_=kk_: e.matmul(acc.ap[:, 0:NT], lhsT=g2.ap[:, kk_, oc * 128:(oc + 1) * 128], rhs=lg.r32()[:, kk_, :],
                                                                      start=(kk_ == 0), stop=(kk_ == 3)), reads=[g2, lg], writes=[acc])
            P.op("dve", lambda e, oc=oc, acc=acc: e.tensor_copy(out=t_g.ap[:, oc, :], in_=acc.ap[:, 0:NT]), reads=[acc], writes=[t_g])
        for pr in range(4):
            P.op("dve", lambda e, pr=pr: e.tensor_scalar(out=t_kk.ap[:, pr, :], in0=t_k.ap[:, pr, :], scalar1=PKK(pr), scalar2=None, op0=ALU.mult),
                 reads=[t_k, par], writes=[t_kk])
            P.op("pool", lambda e, pr=pr: e.tensor_tensor(out=tmp4.ap[:, pr, :], in0=t_kk.ap[:, pr, :], in1=t_kk.ap[:, pr, :], op=ALU.mult), reads=[t_kk], writes=[tmp4])
            acc = nb()
            P.op("pe", lambda e, pr=pr, acc=acc: e.matmul(acc.ap[:, 0:NT], lhsT=bones.ap, rhs=tmp4.ap[:, pr, :], start=True, stop=True), reads=[bones, tmp4], writes=[acc])
            P.op("act", lambda e, pr=pr, acc=acc: e.activation(out=tmp4.ap[:, pr, :], in_=acc.ap[:, 0:NT], func=AF.Sqrt), reads=[acc], writes=[tmp4])
            P.op("dve", lambda e, pr=pr: e.tensor_scalar(out=tmp4.ap[:, pr, :], in0=tmp4.ap[:, pr, :], scalar1=1e-12, scalar2=None, op0=ALU.max), reads=[tmp4], writes=[tmp4])
            P.op("dve", lambda e, pr=pr: e.reciprocal(out=tmp4.ap[:, pr, :], in_=tmp4.ap[:, pr, :]), reads=[tmp4], writes=[tmp4])
            P.op("pool", lambda e, pr=pr: e.tensor_tensor(out=t_kk.ap[:, pr, :], in0=t_kk.ap[:, pr, :], in1=tmp4.ap[:, pr, :], op=ALU.mult), reads=[t_kk, tmp4], writes=[t_kk])
            P.op("dve", lambda e, pr=pr: e.tensor_scalar(out=tmp4.ap[:, pr, :], in0=t_a.ap[:, pr, :], scalar1=-1.0, scalar2=PKA(pr), op0=ALU.add, op1=ALU.mult),
                 reads=[t_a, par, tmp4], writes=[tmp4])
            P.op("dve", lambda e, pr=pr: e.scalar_tensor_tensor(out=t_k.ap[:, pr, :], in0=tmp4.ap[:, pr, :], scalar=1.0, in1=t_k.ap[:, pr, :], op0=ALU.add, op1=ALU.mult),
                 reads=[tmp4, t_k], writes=[t_k])
            P.op("dve", lambda e, pr=pr: e.scalar_tensor_tensor(out=tmp4.ap[:, pr, :], in0=t_r.ap[:, pr, :], scalar=PRK(pr), in1=t_k.ap[:, pr, :], op0=ALU.mult, op1=ALU.mult),
                 reads=[t_r, par, t_k, tmp4], writes=[tmp4])
            acc = nb()
            P.op("pe", lambda e, pr=pr, acc=acc: e.matmul(acc.ap[:, 0:NT], lhsT=bones.ap, rhs=tmp4.ap[:, pr, :], start=True, stop=True), reads=[bones, tmp4], writes=[acc])
            P.op("dve", lambda e, pr=pr, acc=acc: e.tensor_tensor(out=t_bon.ap[:, pr, :], in0=acc.ap[:, 0:NT], in1=t_v.ap[:, pr, :], op=ALU.mult), reads=[acc, t_v], writes=[t_bon])
        fl = lambda t: t.ap.rearrange("p a n -> p (a n)")
        P.op("dve", lambda e: e.tensor_single_scalar(out=fl(t_lw), in_=fl(t_lw), scalar=-EXPM05, op=ALU.mult), reads=[t_lw], writes=[t_lw])
        for pr in range(4):
            P.op("dve", lambda e, pr=pr: e.tensor_tensor_scan(out=t_P.ap[:, pr, :], data0=scm.ap, data1=t_lw.ap[:, pr, :], initial=0.0, op0=ALU.mult, op1=ALU.add),
                 reads=[scm, t_lw], writes=[t_P])
        P.op("dve", lambda e: e.tensor_tensor(out=fl(t_lw), in0=fl(t_P), in1=fl(t_lw), op=ALU.subtract), reads=[t_P, t_lw], writes=[t_lw])
        P.op("act", lambda e: e.activation(out=fl(t_lw), in_=fl(t_lw), func=AF.Exp), reads=[t_lw], writes=[t_lw])
        P.op("dve", lambda e: e.scalar_tensor_tensor(out=fl(t_lw), in0=fl(t_lw), scalar=-1.0, in1=fl(t_kk), op0=ALU.mult, op1=ALU.mult), reads=[t_lw, t_kk], writes=[t_lw])
        P.op("pool", lambda e: e.tensor_tensor(out=fl(t_a), in0=fl(t_a), in1=fl(t_kk), op=ALU.mult), reads=[t_a, t_kk], writes=[t_a])
        P.op("act", lambda e: e.activation(out=fl(t_kk), in_=fl(t_P), func=AF.Exp, scale=-1.0), reads=[t_P, t_a, t_lw], writes=[t_kk])
        P.op("pool", lambda e: e.tensor_tensor(out=fl(t_a), in0=fl(t_a), in1=fl(t_kk), op=ALU.mult), reads=[t_a, t_kk], writes=[t_a])
        P.op("dve", lambda e: e.tensor_tensor(out=fl(t_k), in0=fl(t_k), in1=fl(t_kk), op=ALU.mult), reads=[t_k, t_kk], writes=[t_k])
        P.op("act", lambda e: e.activation(out=fl(t_P), in_=fl(t_P), func=AF.Exp), reads=[t_P, t_kk], writes=[t_P])
        P.op("pool", lambda e: e.tensor_tensor(out=fl(t_r), in0=fl(t_r), in1=fl(t_P), op=ALU.mult), reads=[t_r, t_P, t_bon], writes=[t_r])
        abar, bbar, kbar, rbar, eP = t_lw, t_a, t_k, t_r, t_P

        for ch in range(NCH):
            tc = slice(ch * CN, (ch + 1) * CN)
            hs = lambda t, h_: t.ap[(h_ % 2) * 64:(h_ % 2) * 64 + 64, h_ // 2, tc]
            for (src, dst) in ((bbar, tmB), (kbar, tmK), (abar, tmA), (t_v, tmV)):
                bk = nb()
                for pr in range(4):
                    P.op("pe", lambda e, pr=pr, bk=bk, src=src: e.transpose(bk.ap[0:64, pr * 128:(pr + 1) * 128], src.ap[:, pr, tc], ident.ap), reads=[src, ident], writes=[bk])
                P.op("act" if dst in (tmB, tmA) else "dve",
                     (lambda e, bk=bk, dst=dst: e.activation(out=dst.ap, in_=bk.ap[0:64, :], func=AF.Copy)) if dst in (tmB, tmA) else
                     (lambda e, bk=bk, dst=dst: e.tensor_copy(out=dst.ap, in_=bk.ap[0:64, :])), reads=[bk], writes=[dst])
            specs = [(bbar, abar, Nt[0], m_lt), (abar, bbar, Nn[0], m_gt), (abar, kbar, Aak, m_gt), (bbar, rbar, RbT, m_le), (kbar, rbar, RkT, m_le)]
            for si, (la, ra, dst, mk) in enumerate(specs):
                bke = [nb(), nb()]
                for h_ in range(8):
                    bk = bke[h_ % 2]
                    P.op("pe", lambda e, h_=h_, bk=bk, la=la, ra=ra: e.matmul(bk.ap[0:64, (h_ // 2) * 64:(h_ // 2 + 1) * 64], lhsT=hs(la, h_), rhs=hs(ra, h_), start=True, stop=True),
                         reads=[la, ra], writes=[bk])
                for e_ in range(2):
                    bk = bke[e_]
                    dv = dst.ap.rearrange("p (a e c) -> p a e c", e=2, c=64)[:, :, e_, :]
                    P.op("dve", lambda e, bk=bk, dv=dv, mk=mk: e.tensor_tensor(out=dv, in0=bk.ap[0:64, 0:256].rearrange("p (a c) -> p a c", c=64),
                                                                               in1=mk.unsqueeze(1).to_broadcast([64, 4, 64]), op=ALU.mult),
                         reads=[bk, msk], writes=[dst])
            P.op("pool", lambda e: e.tensor_tensor(out=v8(Tt), in0=v8(Nt[0]), in1=identI.ap, op=ALU.add), reads=[Nt[0], identI], writes=[Tt])
            cur = 0
            for st in range(5):
                nxt = 1 - cur
                bN = nb()
                for h_ in range(8):
                    hc = slice(h_ * 64, (h_ + 1) * 64)
                    P.op("pe", lambda e, hc=hc, bN=bN, cur=cur: e.matmul(bN.ap[0:64, hc], lhsT=Nt[cur].ap[:, hc], rhs=Nn[cur].ap[:, hc], start=True, stop=True),
                         reads=[Nt[cur], Nn[cur]], writes=[bN])
                P.op("act", lambda e, bN=bN, nxt=nxt: e.activation(out=Nn[nxt].ap, in_=bN.ap[0:64, :], func=AF.Copy), reads=[bN], writes=[Nn[nxt]])
                if st < 4:
                    bNt = nb()
                    for h_ in range(8):
                        hc = slice(h_ * 64, (h_ + 1) * 64)
                        P.op("pe", lambda e, hc=hc, bNt=bNt, cur=cur: e.matmul(bNt.ap[0:64, hc], lhsT=Nn[cur].ap[:, hc], rhs=Nt[cur].ap[:, hc], start=True, stop=True),
                             reads=[Nt[cur], Nn[cur]], writes=[bNt])
                    P.op("dve", lambda e, bNt=bNt, nxt=nxt: e.tensor_copy(out=Nt[nxt].ap, in_=bNt.ap[0:64, :]), reads=[bNt], writes=[Nt[nxt]])
                bT = nb()
                for h_ in range(8):
                    hc = slice(h_ * 64, (h_ + 1) * 64)
                    P.op("pe", lambda e, hc=hc, bT=bT, nxt=nxt: e.matmul(bT.ap[0:64, hc], lhsT=Nn[nxt].ap[:, hc], rhs=Tt.ap[:, hc], start=True, stop=True),
                         reads=[Nn[nxt], Tt], writes=[bT])
                P.op("dve", lambda e, bT=bT: e.tensor_tensor(out=Tt.ap, in0=bT.ap[0:64, :], in1=Tt.ap, op=ALU.add), reads=[bT, Tt], writes=[Tt])
                cur = nxt
            bW = nb()
            for h_ in range(8):
                e_, pr = h_ % 2, h_ // 2
                P.op("pe", lambda e, h_=h_, e_=e_, pr=pr, bW=bW: e.matmul(bW.ap[e_ * 64:e_ * 64 + 64, pr * 64:(pr + 1) * 64], lhsT=tmA.ap[:, h_ * 64:(h_ + 1) * 64],
                                                                          rhs=Tt.ap[:, h_ * 64:(h_ + 1) * 64], start=True, stop=True), reads=[tmA, Tt], writes=[bW])
            P.op("act", lambda e, bW=bW: e.activation(out=Wt.ap, in_=bW.ap[:, 0:256].rearrange("p (a c) -> p a c", c=64), func=AF.Copy), reads=[bW], writes=[Wt])
            bG = nb()
            for h_ in range(8):
                hc = slice(h_ * 64, (h_ + 1) * 64)
                P.op("pe", lambda e, hc=hc, bG=bG: e.matmul(bG.ap[0:64, hc], lhsT=Aak.ap[:, hc], rhs=Tt.ap[:, hc], start=True, stop=True), reads=[Aak, Tt], writes=[bG])
            P.op("dve", lambda e, bG=bG: e.tensor_copy(out=Gt.ap, in_=bG.ap[0:64, :]), reads=[bG], writes=[Gt])
            bU0 = nb()
            for h_ in range(8):
                hc = slice(h_ * 64, (h_ + 1) * 64)
                P.op("pe", lambda e, hc=hc, bU0=bU0: e.matmul(bU0.ap[0:64, hc], lhsT=Gt.ap[:, hc], rhs=tmV.ap[:, hc], start=True, stop=True), reads=[Gt, tmV], writes=[bU0])
            P.op("act", lambda e, bU0=bU0: e.activation(out=Usb.ap, in_=bU0.ap[0:64, :], func=AF.Copy), reads=[bU0], writes=[Usb])
            bUe = [nb(), nb()]
            for h_ in range(8):
                e_, pr = h_ % 2, h_ // 2
                P.op("pe", lambda e, e_=e_, pr=pr: e.matmul(bUe[e_].ap[0:64, pr * 64:(pr + 1) * 64], lhsT=Wt.ap[e_ * 64:e_ * 64 + 64, pr, :], rhs=Hst.ap[e_ * 64:e_ * 64 + 64, pr, :],
                                                           start=True, stop=True), reads=[Wt, Hst], writes=[bUe[e_]])
            for e_ in range(2):
                uv = Usb.ap.rearrange("p (a e c) -> p a e c", e=2, c=64)[:, :, e_, :]
                P.op("dve", lambda e, e_=e_, uv=uv: e.tensor_tensor(out=uv, in0=bUe[e_].ap[0:64, 0:256].rearrange("p (a c) -> p a c", c=64), in1=uv, op=ALU.add),
                     reads=[bUe[e_], Usb], writes=[Usb])
            bY = nb()
            for h_ in range(8):
                e_, pr = h_ % 2, h_ // 2
                hc = slice(h_ * 64, (h_ + 1) * 64)
                o_ = bY.ap[e_ * 64:e_ * 64 + 64, pr * 64:(pr + 1) * 64]
                P.op("pe", lambda e, hc=hc, o_=o_: e.matmul(o_, lhsT=Usb.ap[:, hc], rhs=RbT.ap[:, hc], start=True, stop=False), reads=[Usb, RbT], writes=[bY])
                P.op("pe", lambda e, hc=hc, o_=o_: e.matmul(o_, lhsT=tmV.ap[:, hc], rhs=RkT.ap[:, hc], start=False, stop=True), reads=[tmV, RkT], writes=[bY])
            P.op("act", lambda e, bY=bY: e.activation(out=t_y.ap[:, :, tc], in_=bY.ap[:, 0:256].rearrange("p (a c) -> p a c", c=64), func=AF.Copy), reads=[bY], writes=[t_y])
            bYe = [nb(), nb()]
            for h_ in range(8):
                e_, pr = h_ % 2, h_ // 2
                P.op("pe", lambda e, h_=h_, e_=e_, pr=pr: e.matmul(bYe[e_].ap[e_ * 64:e_ * 64 + 64, pr * 64:(pr + 1) * 64], lhsT=Hst.ap[e_ * 64:e_ * 64 + 64, pr, :], rhs=hs(rbar, h_),
                                                                  start=True, stop=True), reads=[Hst, rbar], writes=[bYe[e_]])
            for e_ in range(2):
                yv = t_y.ap[e_ * 64:e_ * 64 + 64, :, tc]
                P.op("dve", lambda e, e_=e_, yv=yv: e.tensor_tensor(out=yv, in0=bYe[e_].ap[e_ * 64:e_ * 64 + 64, 0:256].rearrange("p (a c) -> p a c", c=64), in1=yv, op=ALU.add),
                     reads=[bYe[e_], t_y], writes=[t_y])
            bH = nb()
            for h_ in range(8):
                e_, pr = h_ % 2, h_ // 2
                hc = slice(h_ * 64, (h_ + 1) * 64)
                o_ = bH.ap[e_ * 64:e_ * 64 + 64, pr * 64:(pr + 1) * 64]
                P.op("pe", lambda e, hc=hc, o_=o_: e.matmul(o_, lhsT=tmB.ap[:, hc], rhs=Usb.ap[:, hc], start=True, stop=False), reads=[tmB, Usb], writes=[bH])
                P.op("pe", lambda e, hc=hc, o_=o_: e.matmul(o_, lhsT=tmK.ap[:, hc], rhs=tmV.ap[:, hc], start=False, stop=True), reads=[tmK, tmV], writes=[bH])
            P.op("dve", lambda e, bH=bH: e.tensor_tensor(out=Htmp.ap, in0=bH.ap[:, 0:256].rearrange("p (a c) -> p a c", c=64), in1=Hst.ap, op=ALU.add), reads=[bH, Hst], writes=[Htmp])
            pC = eP.ap[:, :, ch * CN + CN - 1:ch * CN + CN].to_broadcast([128, 4, 64])
            P.op("dve", lambda e, pC=pC: e.tensor_tensor(out=Hst.ap, in0=Htmp.ap, in1=pC, op=ALU.mult), reads=[Htmp, eP], writes=[Hst])

        for pr in range(4):
            b1 = nb()
            P.op("pe", lambda e, pr=pr, b1=b1: e.matmul(b1.ap[:, 0:NT], lhsT=bones.ap, rhs=t_y.ap[:, pr, :], start=True, stop=True), reads=[bones, t_y], writes=[b1])
            P.op("pool", lambda e, pr=pr: e.tensor_tensor(out=tmp4.ap[:, pr, :], in0=t_y.ap[:, pr, :], in1=t_y.ap[:, pr, :], op=ALU.mult), reads=[t_y, tmp4], writes=[tmp4])
            b2 = nb()
            P.op("pe", lambda e, pr=pr, b2=b2: e.matmul(b2.ap[:, 0:NT], lhsT=bones.ap, rhs=tmp4.ap[:, pr, :], start=True, stop=True), reads=[bones, tmp4], writes=[b2])
            mean = xt[0]; var = xt[1]
            P.op("act", lambda e, b1=b1: e.activation(out=mean.ap, in_=b1.ap[:, 0:NT], func=AF.Copy, scale=1.0 / 64.0), reads=[b1], writes=[mean])
            P.op("pool", lambda e: e.tensor_tensor(out=var.ap, in0=mean.ap, in1=mean.ap, op=ALU.mult), reads=[mean], writes=[var])
            P.op("dve", lambda e, b2=b2: e.scalar_tensor_tensor(out=var.ap, in0=b2.ap[:, 0:NT], scalar=1.0 / 64.0, in1=var.ap, op0=ALU.mult, op1=ALU.subtract), reads=[b2, var], writes=[var])
            P.op("dve", lambda e: e.tensor_scalar(out=var.ap, in0=var.ap, scalar1=64e-5, scalar2=None, op0=ALU.add), reads=[var], writes=[var])
            P.op("act", lambda e: e.activation(out=var.ap, in_=var.ap, func=AF.Sqrt), reads=[var], writes=[var])
            P.op("dve", lambda e: e.reciprocal(out=var.ap, in_=var.ap), reads=[var], writes=[var])
            yv = t_y.ap[:, pr, :]
            P.op("dve", lambda e, yv=yv: e.tensor_tensor(out=yv, in0=yv, in1=mean.ap, op=ALU.subtract), reads=[t_y, mean], writes=[t_y])
            P.op("pool", lambda e, yv=yv: e.tensor_tensor(out=yv, in0=yv, in1=var.ap, op=ALU.mult), reads=[t_y, var], writes=[t_y])
            P.op("dve", lambda e, yv=yv, pr=pr: e.tensor_scalar(out=yv, in0=yv, scalar1=PLW(pr), scalar2=PLB(pr), op0=ALU.mult, op1=ALU.add), reads=[t_y, par], writes=[t_y])
            P.op("pool", lambda e, yv=yv, pr=pr: e.tensor_tensor(out=yv, in0=yv, in1=t_bon.ap[:, pr, :], op=ALU.add), reads=[t_y, t_bon], writes=[t_y])
            P.op("dve", lambda e, yv=yv, pr=pr: e.tensor_tensor(out=tmp4.ap[:, pr, :], in0=yv, in1=t_g.ap[:, pr, :], op=ALU.mult), reads=[t_y, t_g, tmp4], writes=[tmp4])
            P.dma("sp", zT.ap[pr * 128:(pr + 1) * 128, t0:t0 + NT], tmp4.ap[:, pr, :], reads=[tmp4], writes=[zT], join=True)
    P.finish([zT], "sp")
    P.close()
    return nc, P


def mixC_consts(NT):
    s = np.arange(64)[:, None]; t = np.arange(64)[None, :]
    masks = np.concatenate([(s <= t), (s < t), (s > t)], axis=1).astype(np.float32)
    rmask = np.where(np.arange(NT) % 64 == 0, 0.0, 1.0).astype(np.float32)
    scan = np.broadcast_to(rmask[None, :], (128, NT)).copy()
    bones = np.zeros((128, 128), np.float32); bones[:64, :64] = 1.0; bones[64:, 64:] = 1.0
    return masks, scan, bones


def mixC_core_inputs(inp, c):
    cs = slice(c * 512, (c + 1) * 512)
    d = dict(wr=np.ascontiguousarray(inp["rwkv_w_r"][0][:, cs]), wk=np.ascontiguousarray(inp["rwkv_w_k"][0][:, cs]),
             wv=np.ascontiguousarray(inp["rwkv_w_v"][0][:, cs]), w1=inp["rwkv_w1"][0], a1=inp["rwkv_a1"][0], g1=inp["rwkv_g1"][0],
             w2=np.ascontiguousarray(inp["rwkv_w2"][0][:, cs]), a2=np.ascontiguousarray(inp["rwkv_a2"][0][:, cs]),
             g2=np.ascontiguousarray(inp["rwkv_g2"][0][:, cs]))
    names = ["rwkv_w0", "rwkv_a0", "rwkv_k_k", "rwkv_k_a", "rwkv_r_k", "rwkv_lnx_w", "rwkv_lnx_b"]
    d["par"] = np.concatenate([pp(np.asarray(inp[n][0]).reshape(-1)[cs]) for n in names], axis=1)
    return d


def build_memkv():
    nc = bass.Bass("TRN2", target_bir_lowering=False)
    nc.dge_precook = False
    P = Prog(nc)
    NT = NMEM
    memT = P.dram("memT", [D, NMEM], F32, "ExternalInput")
    wkv = P.dram("wkv", [D, 1024], F32, "ExternalInput")
    g_d = P.dram("g_mem", [128, KC], F32, "ExternalInput")
    ident_d = P.dram("ident", [128, 128], F32, "ExternalInput")
    kvT = P.dram("kvT", [1024, NMEM], F32, "ExternalOutput")
    kvM = P.dram("kvM", [NMEM, 1024], F32, "ExternalOutput")
    c = make_common(P, NT)
    g = P.sbuf("g_sb", [128, KC]); P.dma("sp", g.ap, g_d.ap, writes=[g])
    ident = P.sbuf("ident_sb", [128, 128]); P.dma("sp", ident.ap, ident_d.ap, writes=[ident])
    m_all = P.sbuf("m", [128, KC, NT]); m = P.subs(m_all, KC)
    n_all = P.sbuf("mn", [128, KC, NT]); mn = P.subs(n_all, KC)
    kv_all = P.sbuf("kv", [128, 8, NT]); kv = P.subs(kv_all, 8)
    kvm = P.sbuf("kvm", [128, 2, 1024])
    trp = [P.psum(f"trp{i}", [128, 512]) for i in range(2)]
    for k in range(KC):
        P.dma("sp" if k % 2 == 0 else "act", m[k].ap, memT.ap[k * 128:(k + 1) * 128, :], writes=[m[k]])
    rstd = rms_stats(c, m, float(D), EPS)
    rms_apply(c, m, mn, g, rstd)

    def evac(oc, acc):
        P.op("act", lambda e: e.activation(out=kv[oc].ap, in_=acc.ap[:, 0:NT], func=AF.Copy), reads=[acc], writes=[kv[oc]])
        P.dma("sp", kvT.ap[oc * 128:(oc + 1) * 128, :], kv[oc].ap, reads=[kv[oc]], writes=[kvT], join=True)
        for mt in range(2):
            tp = trp[mt]
            P.op("pe", lambda e, mt=mt, tp=tp: e.transpose(tp.ap[:, 0:128], kv[oc].ap[:, mt * 128:(mt + 1) * 128], ident.ap), reads=[kv[oc], ident], writes=[tp])
            P.op("dve", lambda e, mt=mt, tp=tp: e.tensor_copy(out=kvm.ap[:, mt, oc * 128:(oc + 1) * 128], in_=tp.ap[:, 0:128]), reads=[tp], writes=[kvm])
    gemm_fm(c, wkv, 0, KC, 0, 8, mn, evac)
    P.dma("sp", kvM.ap.rearrange("(t p) d -> p t d", p=128), kvm.ap, reads=[kvm], writes=[kvM])
    P.finish([kvT, kvM], "sp")
    P.close()
    return nc, P


NCORE = 8
_CACHE = {}


def _prog(name, fn):
    if name not in _CACHE:
        _CACHE[name] = fn()[0]
    return _CACHE[name]


def _run(nc, in_maps):
    res = run_bass_kernel_spmd(nc, in_maps, core_ids=list(range(len(in_maps))))
    return res.results


def kernel(**inp):
    inp = {k: np.asarray(v) for k, v in inp.items()}
    Tn = inp["x"].shape[1]
    TOK = Tn // NCORE
    ident = np.eye(128, dtype=np.float32)
    xT = fm(inp["x"][0])
    memT = fm(inp["mem"][0])
    g_mem = pp(inp["mem_norm_g"])
    res = _run(_prog("memkv", build_memkv),
               [dict(memT=memT, wkv=np.ascontiguousarray(inp["mem_w_kv"][:, c * 1024:(c + 1) * 1024]), g_mem=g_mem, ident=ident) for c in range(NCORE)])
    kT = np.concatenate([res[c]["kvT"] for c in range(4)], axis=0)
    vM = np.concatenate([res[c]["kvM"] for c in range(4, 8)], axis=1)
    masks, scan = mixA_consts(512)
    g_mix0 = pp(inp["norm_mix_g"][0])
    in_maps = []
    for c in range(NCORE):
        wfm, par = mixA_core_inputs(inp, c)
        in_maps.append(dict(xT=xT, wfm=wfm, g_mix=g_mix0, par=par, masks=masks, scanmask=scan, ident=ident))
    res = _run(_prog(("mixA", Tn), lambda: build_mixA(Tn)), in_maps)
    z0T = np.empty((D, Tn), np.float32)
    ss = np.empty((24, Tn), np.float32)
    for c in range(NCORE):
        hm, half = c // 2, c % 2
        z0T[hm * 512 + half * 256:hm * 512 + half * 256 + 256] = res[c]["zT"][0:256]
        z0T[2048 + 2 * c * 128:2048 + (2 * c + 2) * 128] = res[c]["zT"][256:512]
        ss[c] = res[c]["ssO"][0]
        ss[8 + 2 * c] = res[c]["ssO"][1]
        ss[8 + 2 * c + 1] = res[c]["ssO"][2]
    del res, in_maps
    g_head = pp(np.concatenate([inp["mlstm_norm_g"][0], inp["hgrn_norm_g"][0]]))

    def tok_layer(layer, hT_full, zT_full, w_mo, layer0, final, ss_full=None):
        gains = np.concatenate([pp(inp["norm_xattn_g"][layer]), pp(inp["norm_mlp_g"][layer]), pp(inp["final_norm_g"]),
                                g_head if layer0 else pp(np.ones(D, np.float32))], axis=1)
        common = dict(w_mo=w_mo, w_q=inp["xattn_w_q"][layer], w_o=inp["xattn_w_o"][layer], w_up=inp["mlp_w_up"][layer],
                      w_dn=inp["mlp_w_down"][layer], kT=kT, vM=vM, gains=gains, ident=ident)
        maps = []
        for c in range(NCORE):
            ts = slice(c * TOK, (c + 1) * TOK)
            d = dict(common, hT=np.ascontiguousarray(hT_full[:, ts]), zT=np.ascontiguousarray(zT_full[:, ts]))
            if layer0:
                d["ss"] = np.ascontiguousarray(ss_full[:, ts])
            maps.append(d)
        r = _run(_prog(("tok", TOK, layer0, final), lambda: build_tok(TOK, 256, layer0, final)), maps)
        return np.concatenate([r[c]["oT"] for c in range(NCORE)], axis=1)

    h1T = tok_layer(0, xT, z0T, inp["ab_w_out"][0], True, False, ss)
    del z0T
    masksC, scanC, bones = mixC_consts(256)
    mu = np.concatenate([pp(inp["rwkv_mu"][0, j]) for j in range(6)], axis=1)
    g_mix1 = pp(inp["norm_mix_g"][1])
    in_maps = []
    for c in range(NCORE):
        d = mixC_core_inputs(inp, c)
        d.update(hT=h1T, g_mix=g_mix1, mu=mu, masks=masksC, scanmask=scanC, ident=ident, bones=bones)
        in_maps.append(d)
    res = _run(_prog(("mixC", Tn), lambda: build_mixC(Tn, 256)), in_maps)
    z1T = np.concatenate([res[c]["zT"] for c in range(NCORE)], axis=0)
    del res, in_maps
    outT = tok_layer(1, h1T, z1T, inp["rwkv_w_o"][0], False, True)
    return np.ascontiguousarray(outT.T)[None].astype(np.float32)
```

```python
import numpy as np
import concourse.bass as bass
import concourse.mybir as mybir
from concourse.bass_utils import run_bass_kernel_spmd

F32 = mybir.dt.float32
F32R = mybir.dt.float32r
ALU = mybir.AluOpType
AF = mybir.ActivationFunctionType
AX = mybir.AxisListType

D = 4096
KC = D // 128
SEQ = 8192
NMEM = 256
EPS = 1e-6
SEM_CHUNK = 32000


class T:
    __slots__ = ("ap", "name", "w", "r")

    def __init__(self, ap, name):
        self.ap = ap
        self.name = name
        self.w = []
        self.r = {}

    def __getitem__(self, idx):
        return self.ap[idx]

    def r32(self):
        return self.ap.bitcast(F32R)


NPOOL = 20


class Prog:
    def __init__(self, nc, same_engine_sync=True):
        self.nc = nc
        self.eng = {"pe": nc.tensor, "act": nc.scalar, "dve": nc.vector,
                    "pool": nc.gpsimd, "sp": nc.sync}
        self.seq = {k: 0 for k in self.eng}
        self.sems = {k: [] for k in self.eng}
        self.waited = {}
        self.same_engine_sync = same_engine_sync
        self.stack = []
        self.nsem = 0
        self.n_wait = 0
        self.n_ins = 0
        self.dpool = {}
        self.dpi = {}
        self.last = {}
        self.dma_toks = {}

    def _enter(self, cm):
        obj = cm.__enter__()
        self.stack.append(cm)
        return obj

    def close(self):
        while self.stack:
            self.stack.pop().__exit__(None, None, None)

    def sem(self, name):
        self.nsem += 1
        return self._enter(self.nc.semaphore(name))

    def sbuf(self, name, shape, dtype=F32):
        h = self._enter(self.nc.sbuf_tensor(name, list(shape), dtype))
        return T(h[tuple(slice(None) for _ in shape)], name)

    def psum(self, name, shape, dtype=F32):
        h = self._enter(self.nc.psum_tensor(name, list(shape), dtype))
        return T(h[tuple(slice(None) for _ in shape)], name)

    def dram(self, name, shape, dtype=F32, kind="Internal"):
        return T(self.nc.dram_tensor(name, list(shape), dtype, kind=kind).ap(), name)

    def subs(self, t, n, axis=1):
        out = []
        for i in range(n):
            idx = [slice(None)] * len(t.ap.shape)
            idx[axis] = i
            out.append(T(t.ap[tuple(idx)], f"{t.name}_{i}"))
        return out

    def _csem(self, e, k):
        ci = (k - 1) // SEM_CHUNK
        while len(self.sems[e]) <= ci:
            self.sems[e].append(self.sem(f"s_{e}_{len(self.sems[e])}"))
        return self.sems[e][ci], (k - 1) % SEM_CHUNK + 1

    def _wait(self, e, tok):
        sem, val, src = tok
        if src == e and (e == "pe" or not self.same_engine_sync):
            return
        key = (e, id(sem))
        if self.waited.get(key, 0) >= val:
            return
        self.waited[key] = val
        self.eng[e].wait_ge(sem, val)
        self.n_wait += 1

    def _deps(self, e, reads, writes, join=False):
        toks = []
        for t in reads:
            toks.extend(t.w)
        for t in writes:
            if not join:
                toks.extend(t.w)
            else:
                toks.extend(tk for tk in t.w if tk[2] != "dma")
            toks.extend(t.r.values())
        for tok in toks:
            self._wait(e, tok)

    def _commit(self, tok, reads, writes, join=False):
        key = id(tok[0])
        for t in reads:
            old = t.r.get(key)
            if old is None or old[1] < tok[1]:
                t.r[key] = tok
        for t in writes:
            if join:
                t.w = [tk for tk in t.w if tk[2] == "dma"] + [tok]
            else:
                t.w = [tok]
                t.r = {}

    limit = None

    def op(self, e, fn, reads=(), writes=()):
        if self.limit is not None and self.n_ins >= self.limit:
            return None
        self._deps(e, reads, writes)
        ins = fn(self.eng[e])
        self.seq[e] += 1
        sem, val = self._csem(e, self.seq[e])
        ins.then_inc(sem, 1)
        self._commit((sem, val, e), reads, writes)
        self.last[e] = (sem, val, e)
        self.n_ins += 1
        return ins

    def barrier(self):
        toks = list(self.last.values()) + list(self.dma_toks.values())
        for e in self.eng:
            for tok in toks:
                self._wait(e, tok)

    def dma(self, q, out_ap, in_ap, reads=(), writes=(), join=False, **kw):
        if self.limit is not None and self.n_ins >= self.limit:
            return None
        if q not in self.dpool:
            self.dpool[q] = [[self.sem(f"dq_{q}_{i}"), 0] for i in range(NPOOL)]
            self.dpi[q] = 0
        slot = self.dpool[q][self.dpi[q] % NPOOL]
        self.dpi[q] += 1
        if slot[1] > 0:
            self._wait(q, (slot[0], slot[1], "dma"))
        self._deps(q, reads, writes, join=join)
        ins = self.eng[q].dma_start(out=out_ap, in_=in_ap, **kw)
        slot[1] += 16
        ins.then_inc(slot[0], 16)
        self._commit((slot[0], slot[1], "dma"), reads, writes, join=join)
        self.dma_toks[id(slot[0])] = (slot[0], slot[1], "dma")
        self.n_ins += 1
        return ins

    def finish(self, tiles, e="sp"):
        for t in tiles:
            for tok in t.w:
                self._wait(e, tok)
            for tok in t.r.values():
                self._wait(e, tok)


class Ctx:
    pass


def make_common(P, NT, n_wbuf=3, wcols=128, n_acc=3, own_ssps=True):
    c = Ctx()
    c.P = P
    c.NT = NT
    c.wcols = wcols
    c.wbufs = [P.sbuf(f"wbuf{i}", [128, KC, wcols], F32R) for i in range(n_wbuf)]
    c.accs = [P.psum(f"acc{i}", [128, 512], F32) for i in range(n_acc)]
    c.wi = 0
    c.ai = 0
    c.ones = P.sbuf("ones", [128, 128], F32)
    c.ones_f = P.sbuf("ones_f", [128, 128], F32)
    P.op("dve", lambda e: e.memset(c.ones_f.ap, 1.0), writes=[c.ones_f])
    P.op("dve", lambda e: e.tensor_copy(out=c.ones.r32(), in_=c.ones_f.ap), reads=[c.ones_f], writes=[c.ones])
    c.sq = [P.sbuf(f"sq{i}", [128, NT], F32) for i in range(2)]
    c.sqi = 0
    c.ssps = P.psum("ssps", [128, 512], F32) if own_ssps else c.accs[0]
    c.rstd = P.sbuf("rstd", [128, NT], F32)
    c.nrm_tmp = [P.sbuf(f"nrm_tmp{i}", [128, NT], F32) for i in range(2)]
    c.nti = 0
    return c


def next_w(c):
    w = c.wbufs[c.wi % len(c.wbufs)]
    c.wi += 1
    return w


def next_acc(c):
    a = c.accs[c.ai % len(c.accs)]
    c.ai += 1
    return a


def gemm_fm(c, W, row0, kc, col0, n_oc, act_chunks, evac, NT=None, q="sp"):
    P = c.P
    NT = NT or c.NT
    for oc in range(n_oc):
        wb = next_w(c)
        c0 = col0 + oc * 128
        src = W.ap[row0:row0 + kc * 128, c0:c0 + 128].bitcast(F32R).rearrange("(k p) c -> p k c", p=128)
        half = (kc + 1) // 2
        P.dma(q, wb.ap[:, 0:half, :], src[:, 0:half, :], writes=[wb])
        if kc > half:
            P.dma(q, wb.ap[:, half:kc, :], src[:, half:kc, :], writes=[wb], join=True)
        acc = next_acc(c)
        for k in range(kc):
            a = act_chunks[k]
            P.op("pe", lambda e, k=k, a=a: e.matmul(acc.ap[:, 0:NT], lhsT=wb.ap[:, k, :], rhs=a.r32(),
                                                    start=(k == 0), stop=(k == kc - 1)),
                 reads=[wb, a], writes=[acc])
        evac(oc, acc)


def rms_stats(c, chunks, dim, eps, NT=None):
    P = c.P
    NT = NT or c.NT
    n = len(chunks)
    for i, x in enumerate(chunks):
        sq = c.sq[c.sqi % 2]
        c.sqi += 1
        P.op("act", lambda e, x=x, sq=sq: e.activation(out=sq.r32(), in_=x.ap, func=AF.Square),
             reads=[x], writes=[sq])
        P.op("pe", lambda e, sq=sq, i=i: e.matmul(c.ssps.ap[:, 0:NT], lhsT=c.ones.r32(), rhs=sq.r32(),
                                                  start=(i == 0), stop=(i == n - 1)),
             reads=[c.ones, sq], writes=[c.ssps])
    P.op("dve", lambda e: e.tensor_scalar(out=c.rstd.ap, in0=c.ssps.ap[:, 0:NT], scalar1=1.0 / dim, scalar2=eps,
                                          op0=ALU.mult, op1=ALU.add), reads=[c.ssps], writes=[c.rstd])
    P.op("act", lambda e: e.activation(out=c.rstd.ap, in_=c.rstd.ap, func=AF.Sqrt), reads=[c.rstd], writes=[c.rstd])
    P.op("dve", lambda e: e.reciprocal(out=c.rstd.ap, in_=c.rstd.ap), reads=[c.rstd], writes=[c.rstd])
    return c.rstd


def rms_apply(c, src_chunks, dst_chunks, gain, rstd):
    P = c.P
    for k, (s, d) in enumerate(zip(src_chunks, dst_chunks)):
        if k % 2 == 0:
            P.op("dve", lambda e, s=s, d=d, k=k: e.scalar_tensor_tensor(out=d.r32(), in0=s.ap, scalar=gain.ap[:, k:k + 1],
                                                                        in1=rstd.ap, op0=ALU.mult, op1=ALU.mult),
                 reads=[s, gain, rstd], writes=[d])
        else:
            tmp = c.nrm_tmp[c.nti % 2]; c.nti += 1
            P.op("act", lambda e, s=s, k=k, tmp=tmp: e.activation(out=tmp.ap, in_=s.ap, func=AF.Copy, scale=gain.ap[:, k:k + 1]),
                 reads=[s, gain], writes=[tmp])
            P.op("pool", lambda e, d=d, tmp=tmp: e.tensor_tensor(out=d.r32(), in0=tmp.ap, in1=rstd.ap, op=ALU.mult),
                 reads=[tmp, rstd], writes=[d])


def build_tok(NTOK, NT, layer0, final, HB=8):
    nc = bass.Bass("TRN2", target_bir_lowering=False)
    nc.dge_precook = False
    P = Prog(nc)
    NTILE = NTOK // NT
    hT = P.dram("hT", [D, NTOK], F32, "ExternalInput")
    zT = P.dram("zT", [D, NTOK], F32, "ExternalInput")
    w_mo = P.dram("w_mo", [D, D], F32, "ExternalInput")
    w_q = P.dram("w_q", [D, D], F32, "ExternalInput")
    w_o = P.dram("w_o", [D, D], F32, "ExternalInput")
    w_up = P.dram("w_up", [D, 4 * D], F32, "ExternalInput")
    w_dn = P.dram("w_dn", [4 * D, D], F32, "ExternalInput")
    kT = P.dram("kT", [D, NMEM], F32, "ExternalInput")
    vM = P.dram("vM", [NMEM, D], F32, "ExternalInput")
    gains_d = P.dram("gains", [128, 4 * KC], F32, "ExternalInput")
    ident_d = P.dram("ident", [128, 128], F32, "ExternalInput")
    if layer0:
        ss_d = P.dram("ss", [24, NTOK], F32, "ExternalInput")
    oT = P.dram("oT", [D, NTOK], F32, "ExternalOutput")

    MT = 2 * NT
    c = make_common(P, NT, n_wbuf=2)
    c5 = Ctx()
    c5.__dict__.update(c.__dict__)
    c5.NT = MT
    c5.sq = [P.sbuf(f"sq5_{i}", [128, MT], F32) for i in range(2)]
    c5.rstd = P.sbuf("rstd5", [128, MT], F32)
    c5.nrm_tmp = [P.sbuf(f"nrm5_{i}", [128, MT], F32) for i in range(2)]
    gains = P.sbuf("gains_sb", [128, 4 * KC], F32)
    P.dma("sp", gains.ap, gains_d.ap, writes=[gains])
    g_x = T(gains.ap[:, 0:KC], "g_x"); g_m = T(gains.ap[:, KC:2 * KC], "g_m")
    g_f = T(gains.ap[:, 2 * KC:3 * KC], "g_f"); g_h = T(gains.ap[:, 3 * KC:4 * KC], "g_h")
    for g in (g_x, g_m, g_f, g_h):
        g.w = gains.w
    ident = P.sbuf("ident_sb", [128, 128], F32)
    P.dma("sp", ident.ap, ident_d.ap, writes=[ident])

    H = P.sbuf("h", [128, KC, MT], F32)
    XO = P.sbuf("xo", [128, KC, MT], F32)
    hs = [[T(H.ap[:, k, sb * NT:(sb + 1) * NT], f"h{sb}_{k}") for k in range(KC)] for sb in range(2)]
    hm = [T(H.ap[:, k, :], f"hm_{k}") for k in range(KC)]
    xin = [T(XO.ap[:, k, 0:NT], f"xin_{k}") for k in range(KC)]
    ot = [T(XO.ap[:, k, NT:MT], f"ot_{k}") for k in range(KC)]
    xm = [T(XO.ap[:, k, :], f"xm_{k}") for k in range(KC)]
    assert HB * MT == 8 * NT + 8 * NMEM
    arena = P.sbuf("arena", [128, HB * MT], F32)
    qh = [T(arena.ap[:, j * NT:(j + 1) * NT], f"qh_{j}") for j in range(8)]
    kth = T(arena.ap[:, 8 * NT:8 * NT + 8 * NMEM].bitcast(F32R).rearrange("p (k m) -> p k m", m=NMEM), "kth")
    hid = [T(arena.ap[:, j * MT:(j + 1) * MT], f"hid_{j}") for j in range(HB)]
    relu_t = c5.nrm_tmp
    vh = P.sbuf("vh", [128, 2, 1024], F32R)
    pexp = [P.sbuf(f"pexp{i}", [128, NMEM], F32) for i in range(2)]
    pT = P.sbuf("pT", [128, 2, NT], F32)
    stat = [P.sbuf(f"stat{i}", [128, 4], F32) for i in range(2)]
    sps = [P.psum(f"sps{i}", [128, 512], F32) for i in range(2)]
    trps = [P.psum(f"trps{i}", [128, 512], F32) for i in range(2)]
    ssb = c.nrm_tmp if layer0 else None
    NSUB = NT // 128

    hcur = [None]

    def resid_evac(oc, acc):
        h = hcur[0]
        n_ = h[oc].ap.shape[-1]
        P.op("dve", lambda e: e.tensor_tensor(out=h[oc].ap, in0=acc.ap[:, 0:n_], in1=h[oc].ap, op=ALU.add),
             reads=[acc, h[oc]], writes=[h[oc]])

    for ti in range(NTILE):
        t0 = ti * NT
        sb = ti % 2
        h = hs[sb]
        hcur[0] = h
        for k in range(KC):
            P.dma("sp" if k % 2 == 0 else "act", h[k].ap, hT.ap[k * 128:(k + 1) * 128, t0:t0 + NT], writes=[h[k]])
        for k in range(KC):
            P.dma("act" if k % 2 == 0 else "sp", xin[k].r32(), zT.ap[k * 128:(k + 1) * 128, t0:t0 + NT].bitcast(F32R),
                  writes=[xin[k]])
        if layer0:
            for hd in range(20):
                sa = ssb[0]
                if hd < 4:
                    P.dma("sp", sa.ap, ss_d.ap[2 * hd:2 * hd + 1, t0:t0 + NT].partition_broadcast(128), writes=[sa])
                    sb_ = ssb[1]
                    P.dma("sp", sb_.ap, ss_d.ap[2 * hd + 1:2 * hd + 2, t0:t0 + NT].partition_broadcast(128), writes=[sb_])
                    P.op("dve", lambda e: e.tensor_tensor(out=sa.ap, in0=sa.ap, in1=sb_.ap, op=ALU.add), reads=[sa, sb_], writes=[sa])
                    dv = 512.0
                    chunks = list(range(hd * 4, hd * 4 + 4))
                else:
                    P.dma("sp", sa.ap, ss_d.ap[8 + hd - 4:8 + hd - 3, t0:t0 + NT].partition_broadcast(128), writes=[sa])
                    dv = 128.0
                    chunks = [16 + hd - 4]
                P.op("dve", lambda e: e.tensor_scalar(out=sa.ap, in0=sa.ap, scalar1=1.0 / dv, scalar2=EPS, op0=ALU.mult, op1=ALU.add),
                     reads=[sa], writes=[sa])
                P.op("act", lambda e: e.activation(out=sa.ap, in_=sa.ap, func=AF.Sqrt), reads=[sa], writes=[sa])
                P.op("dve", lambda e: e.reciprocal(out=sa.ap, in_=sa.ap), reads=[sa], writes=[sa])
                for k in chunks:
                    P.op("dve", lambda e, k=k: e.scalar_tensor_tensor(out=xin[k].r32(), in0=xin[k].ap, scalar=g_h.ap[:, k:k + 1],
                                                                      in1=sa.ap, op0=ALU.mult, op1=ALU.mult),
                         reads=[xin[k], g_h, sa], writes=[xin[k]])
        gemm_fm(c, w_mo, 0, KC, 0, KC, xin, resid_evac)
        rstd = rms_stats(c, h, float(D), EPS)
        rms_apply(c, h, xin, g_x, rstd)
        for hd in range(4):
            P.dma("act", kth.ap, kT.ap[hd * 1024:(hd + 1) * 1024, :].bitcast(F32R).rearrange("(k p) m -> p k m", p=128),
                  writes=[kth])
            P.dma("act", vh.ap, vM.ap[:, hd * 1024:(hd + 1) * 1024].bitcast(F32R).rearrange("(t p) d -> p t d", p=128),
                  writes=[vh])

            def q_evac(oc, acc):
                P.op("act", lambda e: e.activation(out=qh[oc].r32(), in_=acc.ap[:, 0:NT], func=AF.Copy, scale=1.0 / 32.0),
                     reads=[acc], writes=[qh[oc]])
            gemm_fm(c, w_q, 0, KC, hd * 1024, 8, xin, q_evac)
            for ts in range(NSUB):
                sp_ = sps[ts % 2]; pe_ = pexp[ts % 2]; st = stat[ts % 2]
                for j in range(8):
                    P.op("pe", lambda e, j=j: e.matmul(sp_.ap[:, 0:NMEM], lhsT=qh[j].r32()[:, ts * 128:(ts + 1) * 128],
                                                       rhs=kth.ap[:, j, :], start=(j == 0), stop=(j == 7)),
                         reads=[qh[j], kth], writes=[sp_])
                P.op("dve", lambda e: e.reduce_max(out=st.ap[:, 0:1], in_=sp_.ap[:, 0:NMEM], axis=AX.X), reads=[sp_], writes=[st])
                P.op("dve", lambda e: e.tensor_single_scalar(out=st.ap[:, 1:2], in_=st.ap[:, 0:1], scalar=-1.0, op=ALU.mult),
                     reads=[st], writes=[st])
                P.op("act", lambda e: e.activation(out=pe_.ap, in_=sp_.ap[:, 0:NMEM], func=AF.Exp, bias=st.ap[:, 1:2], scale=1.0,
                                                   accum_out=st.ap[:, 2:3]), reads=[sp_, st], writes=[pe_, st])
                P.op("dve", lambda e: e.reciprocal(out=st.ap[:, 3:4], in_=st.ap[:, 2:3]), reads=[st], writes=[st])
                P.op("dve", lambda e: e.tensor_scalar(out=pe_.ap, in0=pe_.ap, scalar1=st.ap[:, 3:4], scalar2=None, op0=ALU.mult),
                     reads=[pe_, st], writes=[pe_])
                for mt in range(2):
                    tp = trps[mt]
                    P.op("pe", lambda e, mt=mt, tp=tp: e.transpose(tp.ap[:, 0:128], pe_.ap[:, mt * 128:(mt + 1) * 128], ident.ap),
                         reads=[pe_, ident], writes=[tp])
                    P.op("act" if mt == 0 else "dve",
                         (lambda e, mt=mt, tp=tp: e.activation(out=pT.r32()[:, mt, ts * 128:(ts + 1) * 128], in_=tp.ap[:, 0:128], func=AF.Copy))
                         if mt == 0 else
                         (lambda e, mt=mt, tp=tp: e.tensor_copy(out=pT.r32()[:, mt, ts * 128:(ts + 1) * 128], in_=tp.ap[:, 0:128])),
                         reads=[tp], writes=[pT])
            for j in range(8):
                acc = next_acc(c)
                for mt in range(2):
                    P.op("pe", lambda e, j=j, mt=mt: e.matmul(acc.ap[:, 0:NT], lhsT=vh.ap[:, mt, j * 128:(j + 1) * 128],
                                                              rhs=pT.r32()[:, mt, :], start=(mt == 0), stop=(mt == 1)),
                         reads=[vh, pT], writes=[acc])
                oc = hd * 8 + j
                P.op("act", lambda e, oc=oc, acc=acc: e.activation(out=ot[oc].r32(), in_=acc.ap[:, 0:NT], func=AF.Copy),
                     reads=[acc], writes=[ot[oc]])
        gemm_fm(c, w_o, 0, KC, 0, KC, ot, resid_evac)
        if sb == 0:
            continue
        P.barrier()
        tm = (ti - 1) * NT
        hcur[0] = hm
        rstd = rms_stats(c5, hm, float(D), EPS, NT=MT)
        rms_apply(c5, hm, xm, g_m, rstd)
        for hb in range(4 * KC // HB):
            def up_evac(oc, acc):
                rt = relu_t[oc % 2]
                P.op("act", lambda e: e.activation(out=rt.ap, in_=acc.ap[:, 0:MT], func=AF.Relu), reads=[acc], writes=[rt])
                P.op("pool", lambda e: e.tensor_tensor(out=hid[oc].r32(), in0=rt.ap, in1=rt.ap, op=ALU.mult), reads=[rt], writes=[hid[oc]])
            gemm_fm(c5, w_up, 0, KC, hb * HB * 128, HB, xm, up_evac, NT=MT)
            gemm_fm(c5, w_dn, hb * HB * 128, HB, 0, KC, hid, resid_evac, NT=MT)
        if final:
            rstd = rms_stats(c5, hm, float(D), EPS, NT=MT)
            rms_apply(c5, hm, xm, g_f, rstd)
            src = xm
        else:
            src = hm
        for k in range(KC):
            P.dma("sp" if k % 2 == 0 else "act", oT.ap[k * 128:(k + 1) * 128, tm:tm + MT], src[k].ap, reads=[src[k]], writes=[oT], join=True)
        P.barrier()
    P.finish([oT], "sp")
    P.finish([oT], "act")
    P.close()
    return nc, P


NFM = 18
L = 64


def build_mixA(TT, NT=512, limit=None):
    nc = bass.Bass("TRN2", target_bir_lowering=False)
    nc.dge_precook = False
    P = Prog(nc)
    P.limit = limit
    NG = TT // NT
    NCH = NT // L
    xT = P.dram("xT", [D, TT], F32, "ExternalInput")
    wfm = P.dram("wfm", [D, NFM * 128], F32, "ExternalInput")
    g_d = P.dram("g_mix", [128, KC], F32, "ExternalInput")
    par_d = P.dram("par", [128, 8], F32, "ExternalInput")
    msk_d = P.dram("masks", [64, 3 * 64], F32, "ExternalInput")
    scm_d = P.dram("scanmask", [128, 2 * NT], F32, "ExternalInput")
    ident_d = P.dram("ident", [128, 128], F32, "ExternalInput")
    zT = P.dram("zT", [512, TT], F32, "ExternalOutput")
    ssO = P.dram("ssO", [3, TT], F32, "ExternalOutput")

    c = make_common(P, NT, n_wbuf=2, n_acc=2, own_ssps=False)
    g_mix = P.sbuf("g_mix_sb", [128, KC]); P.dma("sp", g_mix.ap, g_d.ap, writes=[g_mix])
    par = P.sbuf("par_sb", [128, 8]); P.dma("sp", par.ap, par_d.ap, writes=[par])
    msk = P.sbuf("msk_sb", [64, 192]); P.dma("sp", msk.ap, msk_d.ap, writes=[msk])
    scm = P.sbuf("scm_sb", [128, 2 * NT]); P.dma("sp", scm.ap, scm_d.ap, writes=[scm])
    ident = P.sbuf("ident_sb", [128, 128]); P.dma("sp", ident.ap, ident_d.ap, writes=[ident])
    mask01 = msk.ap[:, 0:64]; maskneg = msk.ap[:, 64:128]
    rmask = scm.ap[:, 0:NT]; rneg = scm.ap[:, NT:2 * NT]
    pp = P.sbuf("pp", [128, 8])
    P.op("dve", lambda e: e.tensor_single_scalar(out=pp.ap[:, 0:2], in_=par.ap[:, 0:2], scalar=1.0 / 15.0, op=ALU.mult), reads=[par], writes=[pp])
    P.op("dve", lambda e: e.tensor_tensor(out=pp.ap[:, 2:4], in0=par.ap[:, 2:4], in1=par.ap[:, 4:6], op=ALU.subtract), reads=[par, pp], writes=[pp])
    P.op("act", lambda e: e.activation(out=pp.ap[:, 2:4], in_=pp.ap[:, 2:4], func=AF.Sigmoid), reads=[pp], writes=[pp])
    P.op("dve", lambda e: e.tensor_scalar(out=pp.ap[:, 4:6], in0=pp.ap[:, 2:4], scalar1=-1.0, scalar2=1.0, op0=ALU.mult, op1=ALU.add), reads=[pp], writes=[pp])

    x_all = P.sbuf("xg", [128, KC, NT]); xg = P.subs(x_all, KC)
    pf_all = P.sbuf("pf", [128, NFM, NT]); pf = P.subs(pf_all, NFM)
    gt = {n: P.sbuf("g_" + n, [128, NT]) for n in ["b", "a", "cm", "gi", "fl"]}
    gt["M"] = gt["cm"]; gt["negM"] = gt["cm"]; gt["w"] = c.nrm_tmp[0]
    mv = P.sbuf("mvec", [128, 2 * NCH + 2])
    pk = P.sbuf("pk", [64, NT])
    qg_all = P.sbuf("qg", [128, 2, NT]); qg = P.subs(qg_all, 2)
    hA = [P.sbuf(f"hA{i}", [128, NT]) for i in range(2)]
    heA = [P.sbuf(f"heA{i}", [128, NT]) for i in range(2)]
    hqt = [P.sbuf(f"hqt{i}", [128, NT]) for i in range(2)]
    hkt = [P.sbuf(f"hkt{i}", [128, NT]) for i in range(2)]
    hkh = hA
    Cst = [P.sbuf(f"Cst{i}", [128, 256]) for i in range(2)]
    nbc = [P.sbuf(f"nbc{i}", [128, 128]) for i in range(2)]
    Sst = [P.sbuf(f"Sst{i}", [128, 128]) for i in range(2)]
    for t_ in Cst + nbc + Sst:
        P.op("pool", lambda e, t_=t_: e.memset(t_.ap, 0.0), writes=[t_])
    P.op("dve", lambda e: e.memset(mv.ap, 0.0), writes=[mv])
    tmA = P.sbuf("tmA", [64, 512])
    tmB = P.sbuf("tmB", [64, 320])
    tmC = P.sbuf("tmC", [64, 256])
    kw = P.sbuf("kw", [64, 256])
    dmt = P.sbuf("dmt", [64, 64]); smt = P.sbuf("smt", [64, 64])
    hsm = [P.sbuf(f"hsm{i}", [64, 64]) for i in range(2)]
    rec = P.sbuf("rec", [128, 64])
    hm_all = P.sbuf("hm", [128, 2, NT]); hm = P.subs(hm_all, 2)
    ho = [P.sbuf(f"ho{i}", [128, NT]) for i in range(2)]
    ssr = P.sbuf("ssr", [128, NT])
    tr1 = P.psum("tr1", [128, 512]); tr2 = P.psum("tr2", [128, 512]); tr3 = None
    nd = P.psum("nd", [128, 512]); cps = P.psum("cps", [128, 512]); nps = P.psum("nps", [128, 512])
    sTp = P.psum("sTp", [128, 512])
    sT = T(sTp.ap[0:64, 0:192], "sT")
    tr2a = tr2
    f32 = lambda t: t.ap

    for g in range(NG):
        t0 = g * NT
        for k in range(KC):
            P.dma("sp" if k % 2 == 0 else "act", xg[k].r32(), xT.ap[k * 128:(k + 1) * 128, t0:t0 + NT].bitcast(F32R), writes=[xg[k]])
        rstd = rms_stats(c, xg, float(D), EPS)
        rms_apply(c, xg, xg, g_mix, rstd)

        def evac(oc, acc):
            a_ = acc.ap[:, 0:NT]; o_ = pf[oc]
            if oc in (0, 1):
                P.op("act", lambda e: e.activation(out=o_.ap, in_=a_, func=AF.Copy, scale=1.0 / 16.0), reads=[acc], writes=[o_])
            elif oc in (2, 3, 14, 15, 16, 17):
                P.op("dve", lambda e: e.tensor_copy(out=o_.ap, in_=a_), reads=[acc], writes=[o_])
            elif oc in (4, 5, 10, 11):
                P.op("act", lambda e: e.activation(out=o_.ap, in_=a_, func=AF.Sigmoid), reads=[acc], writes=[o_])
            elif oc in (6, 7):
                P.op("act", lambda e: e.activation(out=o_.ap, in_=a_, func=AF.Tanh, scale=1.0 / 15.0, bias=pp.ap[:, oc - 6:oc - 5]),
                     reads=[acc, pp], writes=[o_])
            else:
                P.op("act", lambda e: e.activation(out=o_.ap, in_=a_, func=AF.Silu), reads=[acc], writes=[o_])
        gemm_fm(c, wfm, 0, KC, 0, NFM, xg, evac)

        ti_, tf_ = pf[6], pf[7]
        b, a, cm, M, negM, gi, fl, wv = (gt[n] for n in ["b", "a", "cm", "M", "negM", "gi", "fl", "w"])
        P.op("act", lambda e: e.activation(out=tf_.ap, in_=tf_.ap, func=AF.Sigmoid, scale=15.0), reads=[tf_], writes=[tf_])
        P.op("act", lambda e: e.activation(out=tf_.ap, in_=tf_.ap, func=AF.Ln), reads=[tf_], writes=[tf_])
        P.op("dve", lambda e: e.tensor_tensor_scan(out=b.ap, data0=rmask, data1=tf_.ap, initial=0.0, op0=ALU.mult, op1=ALU.add),
             reads=[scm, tf_], writes=[b])
        P.op("dve", lambda e: e.scalar_tensor_tensor(out=a.ap, in0=ti_.ap, scalar=15.0, in1=b.ap, op0=ALU.mult, op1=ALU.subtract),
             reads=[ti_, b], writes=[a])
        P.op("dve", lambda e: e.tensor_tensor_scan(out=cm.ap, data0=rneg, data1=a.ap, initial=-1e30, op0=ALU.add, op1=ALU.max),
             reads=[scm, a], writes=[cm])
        cm_end = cm.ap.rearrange("p (c l) -> p c l", l=L)[:, :, L - 1]
        b_end = b.ap.rearrange("p (c l) -> p c l", l=L)[:, :, L - 1]
        P.op("dve", lambda e: e.tensor_tensor_scan(out=mv.ap[:, NCH:2 * NCH], data0=cm_end, data1=b_end, initial=mv.ap[:, 2 * NCH:2 * NCH + 1],
                                                   op0=ALU.max, op1=ALU.add), reads=[cm, b, mv], writes=[mv])
        P.op("dve", lambda e: e.tensor_copy(out=mv.ap[:, 0:1], in_=mv.ap[:, 2 * NCH:2 * NCH + 1]), reads=[mv], writes=[mv])
        P.op("dve", lambda e: e.tensor_copy(out=mv.ap[:, 1:NCH], in_=mv.ap[:, NCH:2 * NCH - 1]), reads=[mv], writes=[mv])
        P.op("dve", lambda e: e.tensor_copy(out=mv.ap[:, 2 * NCH:2 * NCH + 1], in_=mv.ap[:, 2 * NCH - 1:2 * NCH]), reads=[mv], writes=[mv])
        mprev_bc = mv.ap[:, 0:NCH].unsqueeze(2).to_broadcast([128, NCH, L])
        v3 = lambda t: t.ap.rearrange("p (c l) -> p c l", l=L)
        P.op("dve", lambda e: e.tensor_tensor(out=v3(M), in0=v3(cm), in1=mprev_bc, op=ALU.max), reads=[cm, mv], writes=[M])
        P.op("dve", lambda e: e.tensor_single_scalar(out=negM.ap, in_=M.ap, scalar=-1.0, op=ALU.mult), reads=[M], writes=[negM])
        P.op("dve", lambda e: e.tensor_tensor(out=v3(gi), in0=v3(negM), in1=mprev_bc, op=ALU.add), reads=[negM, mv], writes=[gi])
        P.op("act", lambda e: e.activation(out=gi.ap, in_=gi.ap, func=AF.Exp), reads=[gi], writes=[gi])
        P.op("dve", lambda e: e.tensor_tensor(out=fl.ap, in0=negM.ap, in1=b.ap, op=ALU.subtract), reads=[negM, b], writes=[fl])
        P.op("act", lambda e: e.activation(out=fl.ap, in_=fl.ap, func=AF.Exp), reads=[fl], writes=[fl])
        negML_bc = v3(negM)[:, :, L - 1:L].to_broadcast([128, NCH, L])
        P.op("dve", lambda e: e.tensor_tensor(out=v3(wv), in0=v3(a), in1=negML_bc, op=ALU.add), reads=[a, negM], writes=[wv])
        P.op("act", lambda e: e.activation(out=wv.ap, in_=wv.ap, func=AF.Exp), reads=[wv], writes=[wv])
        P.op("dve", lambda e: e.tensor_copy(out=pk.ap[0:32, :], in_=a.ap[0:32, :]), reads=[a], writes=[pk])
        P.op("dve", lambda e: e.tensor_copy(out=pk.ap[32:64, :], in_=wv.ap[32:64, :]), reads=[wv, pk], writes=[pk])
        for dch in range(2):
            P.op("pool", lambda e, dch=dch: e.tensor_tensor(out=qg[dch].ap, in0=pf[dch].ap, in1=gi.ap, op=ALU.mult),
                 reads=[pf[dch], gi], writes=[qg[dch]])

        for hh in range(2):
            sg = pf[10 + hh]
            P.op("dve", lambda e, hh=hh, sg=sg: e.tensor_scalar(out=sg.ap, in0=sg.ap, scalar1=pp.ap[:, 4 + hh:5 + hh], scalar2=pp.ap[:, 2 + hh:3 + hh],
                                                         op0=ALU.mult, op1=ALU.add), reads=[sg, pp], writes=[sg])
            P.op("act", lambda e, hh=hh, sg=sg: e.activation(out=hA[hh].ap, in_=sg.ap, func=AF.Ln), reads=[sg], writes=[hA[hh]])
            P.op("dve", lambda e, hh=hh: e.tensor_tensor_scan(out=hA[hh].ap, data0=rmask, data1=hA[hh].ap, initial=0.0, op0=ALU.mult, op1=ALU.add),
                 reads=[scm, hA[hh]], writes=[hA[hh]])
            P.op("act", lambda e, hh=hh: e.activation(out=heA[hh].ap, in_=hA[hh].ap, func=AF.Exp), reads=[hA[hh]], writes=[heA[hh]])
            P.op("pool", lambda e, hh=hh: e.tensor_tensor(out=hqt[hh].ap, in0=pf[8 + hh].ap, in1=heA[hh].ap, op=ALU.mult),
                 reads=[pf[8 + hh], heA[hh]], writes=[hqt[hh]])
            P.op("dve", lambda e, hh=hh, sg=sg: e.tensor_scalar(out=sg.ap, in0=sg.ap, scalar1=-1.0, scalar2=1.0, op0=ALU.mult, op1=ALU.add),
                 reads=[sg], writes=[sg])
            P.op("act", lambda e, hh=hh: e.activation(out=hkt[hh].ap, in_=hA[hh].ap, func=AF.Exp, scale=-1.0), reads=[hA[hh]], writes=[hkt[hh]])
            P.op("pool", lambda e, hh=hh, sg=sg: e.tensor_tensor(out=hkt[hh].ap, in0=hkt[hh].ap, in1=sg.ap, op=ALU.mult),
                 reads=[hkt[hh], sg], writes=[hkt[hh]])
            eAl_bc = heA[hh].ap.rearrange("p (c l) -> p c l", l=L)[:, :, L - 1:L].to_broadcast([128, NCH, L])
            P.op("dve", lambda e, hh=hh: e.tensor_tensor(out=v3(hkh[hh]), in0=v3(hkt[hh]), in1=eAl_bc, op=ALU.mult),
                 reads=[hkt[hh], heA[hh]], writes=[hkh[hh]])

        for ch in range(NCH):
            tc = slice(ch * L, (ch + 1) * L)
            for j in range(2):
                P.op("pe", lambda e, j=j: e.transpose(tr1.ap[0:64, j * 128:(j + 1) * 128], pf[2 + j].ap[:, tc], ident.ap), reads=[pf[2 + j], ident], writes=[tr1])
            for j in range(2):
                P.op("pe", lambda e, j=j: e.transpose(tr1.ap[0:64, 256 + j * 128:256 + (j + 1) * 128], pf[14 + j].ap[:, tc], ident.ap), reads=[pf[14 + j], ident], writes=[tr1])
            P.op("act", lambda e: e.activation(out=tmA.ap, in_=tr1.ap[0:64, :], func=AF.Copy), reads=[tr1], writes=[tmA])
            P.op("pe", lambda e: e.transpose(tr2.ap[0:64, 0:64], pk.ap[:, tc], ident.ap[0:64, 0:64]), reads=[pk, ident], writes=[tr2a])
            for j in range(2):
                P.op("pe", lambda e, j=j: e.transpose(tr2.ap[0:64, 64 + j * 128:64 + (j + 1) * 128], pf[16 + j].ap[:, tc], ident.ap), reads=[pf[16 + j], ident], writes=[tr2a])
            P.op("dve", lambda e: e.tensor_copy(out=tmB.ap, in_=tr2.ap[0:64, 0:320]), reads=[tr2a], writes=[tmB])
            for j in range(2):
                P.op("pe", lambda e, j=j: e.transpose(sTp.ap[0:64, 192 + j * 128:192 + (j + 1) * 128], hkh[j].ap[:, tc], ident.ap), reads=[hkh[j], ident], writes=[sTp])
            P.op("act", lambda e: e.activation(out=tmC.ap, in_=sTp.ap[0:64, 192:448], func=AF.Copy), reads=[sTp], writes=[tmC])
            ktm = tmA.ap[:, 0:256]; vtm = tmA.ap[:, 256:512]
            a_col = tmB.ap[:, 0:1]; w_col = tmB.ap[:, 32:33]
            for dch in range(2):
                P.op("pe", lambda e, dch=dch: e.matmul(sT.ap[:, 0:64], lhsT=pf[2 + dch].ap[:, tc], rhs=pf[dch].ap[:, tc], start=(dch == 0), stop=(dch == 1)),
                     reads=[pf[2 + dch], pf[dch]], writes=[sTp])
            for hh in range(2):
                P.op("pe", lambda e, hh=hh: e.matmul(sT.ap[:, 64 + hh * 64:128 + hh * 64], lhsT=hkt[hh].ap[:, tc], rhs=hqt[hh].ap[:, tc], start=True, stop=True),
                     reads=[hkt[hh], hqt[hh]], writes=[sTp])
            P.op("dve", lambda e: e.scalar_tensor_tensor(out=dmt.ap, in0=negM.ap[0:64, tc], scalar=a_col, in1=maskneg, op0=ALU.add, op1=ALU.add),
                 reads=[negM, tmB, msk], writes=[dmt])
            P.op("act", lambda e: e.activation(out=dmt.ap, in_=dmt.ap, func=AF.Exp), reads=[dmt], writes=[dmt])
            P.op("dve", lambda e: e.tensor_tensor(out=smt.ap, in0=sT.ap[:, 0:64], in1=dmt.ap, op=ALU.mult), reads=[sTp, dmt], writes=[smt])
            for hh in range(2):
                P.op("dve", lambda e, hh=hh: e.tensor_tensor(out=hsm[hh].ap, in0=sT.ap[:, 64 + hh * 64:128 + hh * 64], in1=mask01, op=ALU.mult),
                     reads=[sTp, msk], writes=[hsm[hh]])
            for vch in range(2):
                o_ = nd.ap[:, vch * 64:(vch + 1) * 64]
                P.op("pe", lambda e, vch=vch, o_=o_: e.matmul(o_, lhsT=vtm[:, vch * 128:(vch + 1) * 128], rhs=smt.ap, start=True, stop=False),
                     reads=[tmA, smt], writes=[nd])
                for dch in range(2):
                    P.op("pe", lambda e, vch=vch, dch=dch, o_=o_: e.matmul(o_, lhsT=Cst[dch].ap[:, vch * 128:(vch + 1) * 128], rhs=qg[dch].ap[:, tc],
                                                                          start=False, stop=(dch == 1)), reads=[Cst[dch], qg[dch]], writes=[nd])
            o_ = nd.ap[:, 128:192]
            P.op("pe", lambda e: e.matmul(o_, lhsT=c.ones.ap[0:64, :], rhs=smt.ap, start=True, stop=False), reads=[c.ones, smt], writes=[nd])
            for dch in range(2):
                P.op("pe", lambda e, dch=dch: e.matmul(o_, lhsT=nbc[dch].ap, rhs=qg[dch].ap[:, tc], start=False, stop=(dch == 1)),
                     reads=[nbc[dch], qg[dch]], writes=[nd])
            for hh in range(2):
                oo = nd.ap[:, 192 + hh * 64:256 + hh * 64]
                P.op("pe", lambda e, hh=hh, oo=oo: e.matmul(oo, lhsT=tmB.ap[:, 64 + hh * 128:192 + hh * 128], rhs=hsm[hh].ap, start=True, stop=False),
                     reads=[tmB, hsm[hh]], writes=[nd])
                P.op("pe", lambda e, hh=hh, oo=oo: e.matmul(oo, lhsT=Sst[hh].ap, rhs=hqt[hh].ap[:, tc], start=False, stop=True),
                     reads=[Sst[hh], hqt[hh]], writes=[nd])
            P.op("act", lambda e: e.activation(out=rec.ap, in_=nd.ap[:, 128:192], func=AF.Abs), reads=[nd], writes=[rec])
            P.op("dve", lambda e: e.tensor_tensor(out=rec.ap, in0=rec.ap, in1=fl.ap[:, tc], op=ALU.max), reads=[rec, fl], writes=[rec])
            P.op("dve", lambda e: e.reciprocal(out=rec.ap, in_=rec.ap), reads=[rec], writes=[rec])
            for vch in range(2):
                P.op("dve", lambda e, vch=vch: e.tensor_tensor(out=hm[vch].ap[:, tc], in0=nd.ap[:, vch * 64:(vch + 1) * 64], in1=rec.ap, op=ALU.mult),
                     reads=[nd, rec], writes=[hm[vch]])
            for hh in range(2):
                P.op("act", lambda e, hh=hh: e.activation(out=ho[hh].ap[:, tc], in_=nd.ap[:, 192 + hh * 64:256 + hh * 64], func=AF.Copy),
                     reads=[nd], writes=[ho[hh]])
            P.op("dve", lambda e: e.tensor_scalar(out=kw.ap, in0=ktm, scalar1=w_col, scalar2=None, op0=ALU.mult), reads=[tmA, tmB], writes=[kw])
            for dch in range(2):
                P.op("pe", lambda e, dch=dch: e.matmul(cps.ap[:, dch * 256:(dch + 1) * 256], lhsT=kw.ap[:, dch * 128:(dch + 1) * 128], rhs=vtm, start=True, stop=True),
                     reads=[kw, tmA], writes=[cps])
            for dch in range(2):
                P.op("pe", lambda e, dch=dch: e.matmul(nps.ap[:, dch * 128:(dch + 1) * 128], lhsT=kw.ap[:, dch * 128:(dch + 1) * 128], rhs=c.ones.ap[0:64, :], start=True, stop=True),
                     reads=[kw, c.ones], writes=[nps])
            for hh in range(2):
                P.op("pe", lambda e, hh=hh: e.matmul(nps.ap[:, 256 + hh * 128:384 + hh * 128], lhsT=tmC.ap[:, hh * 128:(hh + 1) * 128],
                                                     rhs=tmB.ap[:, 64 + hh * 128:192 + hh * 128], start=True, stop=True), reads=[tmC, tmB], writes=[nps])
            dec = gi.ap[:, ch * L + L - 1:ch * L + L]
            for dch in range(2):
                P.op("dve", lambda e, dch=dch: e.scalar_tensor_tensor(out=Cst[dch].ap, in0=Cst[dch].ap, scalar=dec, in1=cps.ap[:, dch * 256:(dch + 1) * 256],
                                                                      op0=ALU.mult, op1=ALU.add), reads=[Cst[dch], gi, cps], writes=[Cst[dch]])
                P.op("dve", lambda e, dch=dch: e.scalar_tensor_tensor(out=nbc[dch].ap, in0=nbc[dch].ap, scalar=dec, in1=nps.ap[:, dch * 128:(dch + 1) * 128],
                                                                      op0=ALU.mult, op1=ALU.add), reads=[nbc[dch], gi, nps], writes=[nbc[dch]])
            for hh in range(2):
                eal = heA[hh].ap[:, ch * L + L - 1:ch * L + L]
                P.op("dve", lambda e, hh=hh, eal=eal: e.scalar_tensor_tensor(out=Sst[hh].ap, in0=Sst[hh].ap, scalar=eal, in1=nps.ap[:, 256 + hh * 128:384 + hh * 128],
                                                                            op0=ALU.mult, op1=ALU.add), reads=[Sst[hh], heA[hh], nps], writes=[Sst[hh]])

        outs = [(hm[0], pf[4]), (hm[1], pf[5]), (ho[0], pf[12]), (ho[1], pf[13])]
        groups = [[0, 1], [2], [3]]
        for gi_, idxs in enumerate(groups):
            for ii, ix in enumerate(idxs):
                sq = c.sq[c.sqi % 2]; c.sqi += 1
                src = outs[ix][0]
                P.op("act", lambda e, sq=sq, src=src: e.activation(out=sq.r32(), in_=src.ap, func=AF.Square), reads=[src], writes=[sq])
                P.op("pe", lambda e, sq=sq, ii=ii, idxs=idxs: e.matmul(c.accs[1].ap[:, 0:NT], lhsT=c.ones.r32(), rhs=sq.r32(), start=(ii == 0), stop=(ii == len(idxs) - 1)),
                     reads=[c.ones, sq], writes=[c.accs[1]])
            P.op("dve", lambda e: e.tensor_copy(out=ssr.ap, in_=c.accs[1].ap[:, 0:NT]), reads=[c.accs[1]], writes=[ssr])
            P.dma("sp", ssO.ap[gi_:gi_ + 1, t0:t0 + NT], ssr.ap[0:1, :], reads=[ssr], writes=[ssO], join=True)
        for ix, (src, gate) in enumerate(outs):
            P.op("pool", lambda e, ix=ix, src=src, gate=gate: e.tensor_tensor(out=gate.ap, in0=src.ap, in1=gate.ap, op=ALU.mult),
                 reads=[src, gate], writes=[gate])
            P.dma("sp", zT.ap[ix * 128:(ix + 1) * 128, t0:t0 + NT], gate.ap, reads=[gate], writes=[zT], join=True)
    P.finish([zT, ssO], "sp")
    P.close()
    return nc, P


IN_SIZES = (1024, 1024, 2048, 2048, 4, 4, 2048, 2048, 2048, 2048)


def fm(a):
    return np.ascontiguousarray(np.asarray(a).T)


def pp(g):
    g = np.asarray(g, dtype=np.float32).reshape(-1)
    return np.ascontiguousarray(g.reshape(-1, 128).T)


def mixA_consts(NT):
    s = np.arange(64)[:, None]; t = np.arange(64)[None, :]
    m01 = (s <= t).astype(np.float32)
    mneg = np.where(s <= t, 0.0, -1e30).astype(np.float32)
    masks = np.concatenate([m01, mneg, np.zeros((64, 64), np.float32)], axis=1)
    st = (np.arange(NT) % 64 == 0)
    rmask = np.where(st, 0.0, 1.0).astype(np.float32)
    rneg = np.where(st, -1e30, 0.0).astype(np.float32)
    scan = np.broadcast_to(np.concatenate([rmask, rneg])[None, :], (128, 2 * NT)).copy()
    return masks, scan


def mixA_core_inputs(inp, c):
    W = inp["ab_w_in"][0]
    offs = np.cumsum([0] + list(IN_SIZES))
    hm, half = c // 2, c % 2
    r = np.arange
    cols = np.concatenate([
        offs[0] + hm * 256 + r(256), offs[1] + hm * 256 + r(256), offs[3] + hm * 512 + half * 256 + r(256),
        np.full(128, offs[4] + hm), np.full(128, offs[5] + hm),
        offs[6] + 2 * c * 128 + r(256), offs[7] + 2 * c * 128 + r(256), offs[9] + 2 * c * 128 + r(256),
        offs[2] + hm * 512 + half * 256 + r(256), offs[8] + 2 * c * 128 + r(256)])
    wfm = np.ascontiguousarray(W[:, cols])
    par = np.zeros((128, 8), np.float32)
    par[:, 0] = inp["mlstm_b_i"][0, hm]
    par[:, 1] = inp["mlstm_b_f"][0, hm]
    lg = inp["hgrn_lb_logits"]
    for hh in range(2):
        par[:, 2 + hh] = lg[0, (2 * c + hh) * 128:(2 * c + hh + 1) * 128]
        par[:, 4 + hh] = lg[1, (2 * c + hh) * 128:(2 * c + hh + 1) * 128]
    return wfm, par


CN = 64
EXPM05 = 0.6065306597126334


def build_mixC(TT, NT=256, limit=None):
    nc = bass.Bass("TRN2", target_bir_lowering=False)
    nc.dge_precook = False
    P = Prog(nc)
    P.limit = limit
    NG = TT // NT
    NCH = NT // CN
    hT = P.dram("hT", [D, TT], F32, "ExternalInput")
    g_d = P.dram("g_mix", [128, KC], F32, "ExternalInput")
    mu_d = P.dram("mu", [128, 6 * KC], F32, "ExternalInput")
    wr_d = P.dram("wr", [D, 512], F32, "ExternalInput")
    wk_d = P.dram("wk", [D, 512], F32, "ExternalInput")
    wv_d = P.dram("wv", [D, 512], F32, "ExternalInput")
    w1_d = P.dram("w1", [D, 128], F32, "ExternalInput")
    a1_d = P.dram("a1", [D, 128], F32, "ExternalInput")
    g1_d = P.dram("g1", [D, 480], F32, "ExternalInput")
    w2_d = P.dram("w2", [128, 512], F32, "ExternalInput")
    a2_d = P.dram("a2", [128, 512], F32, "ExternalInput")
    g2_d = P.dram("g2", [480, 512], F32, "ExternalInput")
    par_d = P.dram("par", [128, 28], F32, "ExternalInput")
    msk_d = P.dram("masks", [64, 192], F32, "ExternalInput")
    scm_d = P.dram("scanmask", [128, NT], F32, "ExternalInput")
    ident_d = P.dram("ident", [128, 128], F32, "ExternalInput")
    bones_d = P.dram("bones", [128, 128], F32, "ExternalInput")
    zT = P.dram("zT", [512, TT], F32, "ExternalOutput")

    def ld(name, src, shape, dtype=F32, ap=None):
        t = P.sbuf(name, shape, dtype)
        P.dma("sp", t.ap, ap if ap is not None else src.ap, writes=[t])
        return t
    g_mix = ld("g_mix_sb", g_d, [128, KC])
    mu = ld("mu_sb", mu_d, [128, 6 * KC])
    par = ld("par_sb", par_d, [128, 28])
    msk = ld("msk_sb", msk_d, [64, 192])
    scm = ld("scm_sb", scm_d, [128, NT])
    ident = ld("ident_sb", ident_d, [128, 128])
    bones = ld("bones_sb", bones_d, [128, 128])
    w2 = ld("w2_sb", w2_d, [128, 512], F32R, w2_d.ap.bitcast(F32R))
    a2 = ld("a2_sb", a2_d, [128, 512], F32R, a2_d.ap.bitcast(F32R))
    g2 = ld("g2_sb", g2_d, [120, 4, 512], F32R, g2_d.ap.bitcast(F32R).rearrange("(k p) c -> p k c", p=120))
    omu = P.sbuf("omu", [128, 6 * KC])
    P.op("dve", lambda e: e.tensor_scalar(out=omu.ap, in0=mu.ap, scalar1=-1.0, scalar2=1.0, op0=ALU.mult, op1=ALU.add), reads=[mu], writes=[omu])
    m_le = msk.ap[:, 0:64]; m_lt = msk.ap[:, 64:128]; m_gt = msk.ap[:, 128:192]
    PW0, PA0, PKK, PKA, PRK, PLW, PLB = ((lambda i: (lambda pr: par.ap[:, 4 * i + pr:4 * i + pr + 1]))(i) for i in range(7))

    ones = P.sbuf("ones", [128, 128]); ones_f = P.sbuf("ones_f", [128, 128])
    P.op("dve", lambda e: e.memset(ones_f.ap, 1.0), writes=[ones_f])
    P.op("dve", lambda e: e.tensor_copy(out=ones.r32(), in_=ones_f.ap), reads=[ones_f], writes=[ones])

    pb = [P.psum(f"pb{i}", [128, 512]) for i in range(8)]
    pbi = [0]

    def nb():
        b = pb[pbi[0] % 8]; pbi[0] += 1
        return b

    hx_all = P.sbuf("hx", [128, KC, NT + 1]); hx = P.subs(hx_all, KC)
    P.op("pool", lambda e: e.memset(hx_all.ap[:, :, 0:1], 0.0), writes=hx)
    xs = [P.sbuf(f"xs{i}", [128, NT]) for i in range(3)]
    xt = [P.sbuf(f"xt{i}", [128, NT]) for i in range(3)]
    KB = 4
    wk_ = [P.sbuf(f"wkb{i}", [128, KB, 512], F32R) for i in range(2)]
    sq = [P.sbuf(f"sq{i}", [128, NT]) for i in range(2)]
    rstd = P.sbuf("rstd", [128, NT])
    lora = [P.sbuf(f"lora{i}", [128, NT]) for i in range(2)]
    lg = P.sbuf("lorag", [120, 4, NT])
    fmn = ["r", "k", "v", "a", "g", "kk", "bon", "lw", "P", "y"]
    fmt = {n: P.sbuf("t_" + n, [128, 4, NT]) for n in fmn}
    t_r, t_k, t_v, t_a, t_g, t_kk, t_bon, t_lw, t_P, t_y = (fmt[n] for n in fmn)
    tmp4 = P.sbuf("tmp4", [128, 4, NT])
    tmB = P.sbuf("tmB", [64, 512]); tmK = P.sbuf("tmK", [64, 512]); tmA = P.sbuf("tmA", [64, 512]); tmV = P.sbuf("tmV", [64, 512])
    Nn = [P.sbuf(f"Nn{i}", [64, 512]) for i in range(2)]
    Nt = [P.sbuf(f"Nt{i}", [64, 512]) for i in range(2)]
    Tt = P.sbuf("Tt", [64, 512]); RbT = P.sbuf("RbT", [64, 512]); RkT = P.sbuf("RkT", [64, 512]); Aak = P.sbuf("Aak", [64, 512])
    Wt = P.sbuf("Wt", [128, 4, 64]); Gt = P.sbuf("Gt", [64, 512]); Usb = P.sbuf("Usb", [64, 512])
    Hst = P.sbuf("Hst", [128, 4, 64]); Htmp = P.sbuf("Htmp", [128, 4, 64])
    P.op("pool", lambda e: e.memset(Hst.ap, 0.0), writes=[Hst])
    identI = P.sbuf("identI", [64, 8, 64])
    for h_ in range(8):
        P.op("pool", lambda e, h_=h_: e.tensor_copy(out=identI.ap[:, h_, :], in_=ident.ap[0:64, 0:64]), reads=[ident], writes=[identI])
    v8 = lambda t: t.ap.rearrange("p (h c) -> p h c", c=64)
    bc8 = lambda m: m.unsqueeze(1).to_broadcast([64, 8, 64])

    projs = [(0, wr_d, 512), (1, w1_d, 128), (2, wk_d, 512), (3, wv_d, 512), (4, a1_d, 128), (5, g1_d, 480)]
    xsi = [0]
    wki = [0]

    for g in range(NG):
        t0 = g * NT
        if g > 0:
            P.op("pool", lambda e: e.tensor_copy(out=hx_all.ap[:, :, 0:1], in_=hx_all.ap[:, :, NT:NT + 1]), reads=hx, writes=hx)
        for k in range(KC):
            P.dma("sp" if k % 2 == 0 else "act", hx[k].ap[:, 1:NT + 1], hT.ap[k * 128:(k + 1) * 128, t0:t0 + NT], writes=[hx[k]])
        ssps = nb()
        for k in range(KC):
            s_ = sq[k % 2]
            P.op("act", lambda e, k=k, s_=s_: e.activation(out=s_.r32(), in_=hx[k].ap[:, 1:NT + 1], func=AF.Square), reads=[hx[k]], writes=[s_])
            P.op("pe", lambda e, k=k, s_=s_: e.matmul(ssps.ap[:, 0:NT], lhsT=ones.r32(), rhs=s_.r32(), start=(k == 0), stop=(k == KC - 1)),
                 reads=[ones, s_], writes=[ssps])
        P.op("dve", lambda e: e.tensor_scalar(out=rstd.ap, in0=ssps.ap[:, 0:NT], scalar1=1.0 / D, scalar2=EPS, op0=ALU.mult, op1=ALU.add), reads=[ssps], writes=[rstd])
        P.op("act", lambda e: e.activation(out=rstd.ap, in_=rstd.ap, func=AF.Sqrt), reads=[rstd], writes=[rstd])
        P.op("dve", lambda e: e.reciprocal(out=rstd.ap, in_=rstd.ap), reads=[rstd], writes=[rstd])
        for k in range(KC):
            P.op("dve", lambda e, k=k: e.scalar_tensor_tensor(out=hx[k].ap[:, 1:NT + 1], in0=hx[k].ap[:, 1:NT + 1], scalar=g_mix.ap[:, k:k + 1], in1=rstd.ap,
                                                              op0=ALU.mult, op1=ALU.mult), reads=[hx[k], g_mix, rstd], writes=[hx[k]])
        for (j, Wd, ncols) in projs:
            ocw = 120 if ncols == 480 else 128
            noc = ncols // ocw
            accs = [nb() for _ in range(noc)]
            for k in range(KC):
                if k % KB == 0:
                    wt = wk_[wki[0] % 2]; wki[0] += 1
                    P.dma("sp", wt.ap[:, :, 0:ncols], Wd.ap[k * 128:(k + KB) * 128, :].bitcast(F32R).rearrange("(k p) c -> p k c", p=128), writes=[wt])
                x_ = xs[xsi[0] % 3]; xt_ = xt[xsi[0] % 3]; xsi[0] += 1
                mcol = j * KC + k
                P.op("act", lambda e, k=k, xt_=xt_, mcol=mcol: e.activation(out=xt_.ap, in_=hx[k].ap[:, 0:NT], func=AF.Copy, scale=mu.ap[:, mcol:mcol + 1]),
                     reads=[hx[k], mu], writes=[xt_])
                P.op("dve", lambda e, k=k, x_=x_, xt_=xt_, mcol=mcol: e.scalar_tensor_tensor(out=x_.r32(), in0=hx[k].ap[:, 1:NT + 1], scalar=omu.ap[:, mcol:mcol + 1],
                                                                                            in1=xt_.ap, op0=ALU.mult, op1=ALU.add),
                     reads=[hx[k], omu, xt_], writes=[x_])
                for oc in range(noc):
                    P.op("pe", lambda e, oc=oc, k=k, wt=wt, x_=x_: e.matmul(accs[oc].ap[0:ocw, 0:NT], lhsT=wt.ap[:, k % KB, oc * ocw:(oc + 1) * ocw], rhs=x_.r32(),
                                                                          start=(k == 0), stop=(k == KC - 1)), reads=[wt, x_], writes=[accs[oc]])
            if j in (0, 2, 3):
                dst = {0: t_r, 2: t_k, 3: t_v}[j]
                for oc in range(4):
                    P.op("act" if oc % 2 == 0 else "dve",
                         (lambda e, oc=oc, dst=dst: e.activation(out=dst.ap[:, oc, :], in_=accs[oc].ap[:, 0:NT], func=AF.Copy)) if oc % 2 == 0 else
                         (lambda e, oc=oc, dst=dst: e.tensor_copy(out=dst.ap[:, oc, :], in_=accs[oc].ap[:, 0:NT])),
                         reads=[accs[oc]], writes=[dst])
            elif j == 1:
                P.op("act", lambda e: e.activation(out=lora[0].r32(), in_=accs[0].ap[:, 0:NT], func=AF.Tanh), reads=[accs[0]], writes=[lora[0]])
            elif j == 4:
                P.op("act", lambda e: e.activation(out=lora[1].r32(), in_=accs[0].ap[:, 0:NT], func=AF.Copy), reads=[accs[0]], writes=[lora[1]])
            else:
                for oc in range(4):
                    P.op("act", lambda e, oc=oc: e.activation(out=lg.r32()[:, oc, :], in_=accs[oc].ap[0:120, 0:NT], func=AF.Sigmoid), reads=[accs[oc]], writes=[lg])
        for oc in range(4):
            acc = nb()
            P.op("pe", lambda e, oc=oc, acc=acc: e.matmul(acc.ap[:, 0:NT], lhsT=w2.ap[:, oc * 128:(oc + 1) * 128], rhs=lora[0].r32(), start=True, stop=True),
                 reads=[w2, lora[0]], writes=[acc])
            P.op("act", lambda e, oc=oc, acc=acc: e.activation(out=t_lw.ap[:, oc, :], in_=acc.ap[:, 0:NT], func=AF.Sigmoid, bias=PW0(oc)), reads=[acc, par], writes=[t_lw])
            acc = nb()
            P.op("pe", lambda e, oc=oc, acc=acc: e.matmul(acc.ap[:, 0:NT], lhsT=a2.ap[:, oc * 128:(oc + 1) * 128], rhs=lora[1].r32(), start=True, stop=True),
                 reads=[a2, lora[1]], writes=[acc])
            P.op("act", lambda e, oc=oc, acc=acc: e.activation(out=t_a.ap[:, oc, :], in_=acc.ap[:, 0:NT], func=AF.Sigmoid, bias=PA0(oc)), reads=[acc, par], writes=[t_a])
            acc = nb()
            for kk_ in range(4):
                P.op("pe", lambda e, oc=oc, acc=acc, kk_=kk_: e.matmul(acc.ap[:, 0:NT], lhsT=g2.ap[:, kk_, oc * 128:(oc + 1) * 128], rhs=lg.r32()[:, kk_, :],
                                                                      start=(kk_ == 0), stop=(kk_ == 3)), reads=[g2, lg], writes=[acc])
            P.op("dve", lambda e, oc=oc, acc=acc: e.tensor_copy(out=t_g.ap[:, oc, :], in_=acc.ap[:, 0:NT]), reads=[acc], writes=[t_g])
        for pr in range(4):
            P.op("dve", lambda e, pr=pr: e.tensor_scalar(out=t_kk.ap[:, pr, :], in0=t_k.ap[:, pr, :], scalar1=PKK(pr), scalar2=None, op0=ALU.mult),
                 reads=[t_k, par], writes=[t_kk])
            P.op("pool", lambda e, pr=pr: e.tensor_tensor(out=tmp4.ap[:, pr, :], in0=t_kk.ap[:, pr, :], in1=t_kk.ap[:, pr, :], op=ALU.mult), reads=[t_kk], writes=[tmp4])
            acc = nb()
            P.op("pe", lambda e, pr=pr, acc=acc: e.matmul(acc.ap[:, 0:NT], lhsT=bones.ap, rhs=tmp4.ap[:, pr, :], start=True, stop=True), reads=[bones, tmp4], writes=[acc])
            P.op("act", lambda e, pr=pr, acc=acc: e.activation(out=tmp4.ap[:, pr, :], in_=acc.ap[:, 0:NT], func=AF.Sqrt), reads=[acc], writes=[tmp4])
            P.op("dve", lambda e, pr=pr: e.tensor_scalar(out=tmp4.ap[:, pr, :], in0=tmp4.ap[:, pr, :], scalar1=1e-12, scalar2=None, op0=ALU.max), reads=[tmp4], writes=[tmp4])
            P.op("dve", lambda e, pr=pr: e.reciprocal(out=tmp4.ap[:, pr, :], in_=tmp4.ap[:, pr, :]), reads=[tmp4], writes=[tmp4])
            P.op("pool", lambda e, pr=pr: e.tensor_tensor(out=t_kk.ap[:, pr, :], in0=t_kk.ap[:, pr, :], in1=tmp4.ap[:, pr, :], op=ALU.mult), reads=[t_kk, tmp4], writes=[t_kk])
            P.op("dve", lambda e, pr=pr: e.tensor_scalar(out=tmp4.ap[:, pr, :], in0=t_a.ap[:, pr, :], scalar1=-1.0, scalar2=PKA(pr), op0=ALU.add, op1=ALU.mult),
                 reads=[t_a, par, tmp4], writes=[tmp4])
            P.op("dve", lambda e, pr=pr: e.scalar_tensor_tensor(out=t_k.ap[:, pr, :], in0=tmp4.ap[:, pr, :], scalar=1.0, in1=t_k.ap[:, pr, :], op0=ALU.add, op1=ALU.mult),
                 reads=[tmp4, t_k], writes=[t_k])
            P.op("dve", lambda e, pr=pr: e.scalar_tensor_tensor(out=tmp4.ap[:, pr, :], in0=t_r.ap[:, pr, :], scalar=PRK(pr), in1=t_k.ap[:, pr, :], op0=ALU.mult, op1=ALU.mult),
                 reads=[t_r, par, t_k, tmp4], writes=[tmp4])
            acc = nb()
            P.op("pe", lambda e, pr=pr, acc=acc: e.matmul(acc.ap[:, 0:NT], lhsT=bones.ap, rhs=tmp4.ap[:, pr, :], start=True, stop=True), reads=[bones, tmp4], writes=[acc])
            P.op("dve", lambda e, pr=pr, acc=acc: e.tensor_tensor(out=t_bon.ap[:, pr, :], in0=acc.ap[:, 0:NT], in1=t_v.ap[:, pr, :], op=ALU.mult), reads=[acc, t_v], writes=[t_bon])
        fl = lambda t: t.ap.rearrange("p a n -> p (a n)")
        P.op("dve", lambda e: e.tensor_single_scalar(out=fl(t_lw), in_=fl(t_lw), scalar=-EXPM05, op=ALU.mult), reads=[t_lw], writes=[t_lw])
        for pr in range(4):
            P.op("dve", lambda e, pr=pr: e.tensor_tensor_scan(out=t_P.ap[:, pr, :], data0=scm.ap, data1=t_lw.ap[:, pr, :], initial=0.0, op0=ALU.mult, op1=ALU.add),
                 reads=[scm, t_lw], writes=[t_P])
        P.op("dve", lambda e: e.tensor_tensor(out=fl(t_lw), in0=fl(t_P), in1=fl(t_lw), op=ALU.subtract), reads=[t_P, t_lw], writes=[t_lw])
        P.op("act", lambda e: e.activation(out=fl(t_lw), in_=fl(t_lw), func=AF.Exp), reads=[t_lw], writes=[t_lw])
        P.op("dve", lambda e: e.scalar_tensor_tensor(out=fl(t_lw), in0=fl(t_lw), scalar=-1.0, in1=fl(t_kk), op0=ALU.mult, op1=ALU.mult), reads=[t_lw, t_kk], writes=[t_lw])
        P.op("pool", lambda e: e.tensor_tensor(out=fl(t_a), in0=fl(t_a), in1=fl(t_kk), op=ALU.mult), reads=[t_a, t_kk], writes=[t_a])
        P.op("act", lambda e: e.activation(out=fl(t_kk), in_=fl(t_P), func=AF.Exp, scale=-1.0), reads=[t_P, t_a, t_lw], writes=[t_kk])
        P.op("pool", lambda e: e.tensor_tensor(out=fl(t_a), in0=fl(t_a), in1=fl(t_kk), op=ALU.mult), reads=[t_a, t_kk], writes=[t_a])
        P.op("dve", lambda e: e.tensor_tensor(out=fl(t_k), in0=fl(t_k), in1=fl(t_kk), op=ALU.mult), reads=[t_k, t_kk], writes=[t_k])
        P.op("act", lambda e: e.activation(out=fl(t_P), in_=fl(t_P), func=AF.Exp), reads=[t_P, t_kk], writes=[t_P])
        P.op("pool", lambda e: e.tensor_tensor(out=fl(t_r), in0=fl(t_r), in1=fl(t_P), op=ALU.mult), reads=[t_r, t_P, t_bon], writes=[t_r])
        abar, bbar, kbar, rbar, eP = t_lw, t_a, t_k, t_r, t_P

        for ch in range(NCH):
            tc = slice(ch * CN, (ch + 1) * CN)
            hs = lambda t, h_: t.ap[(h_ % 2) * 64:(h_ % 2) * 64 + 64, h_ // 2, tc]
            for (src, dst) in ((bbar, tmB), (kbar, tmK), (abar, tmA), (t_v, tmV)):
                bk = nb()
                for pr in range(4):
                    P.op("pe", lambda e, pr=pr, bk=bk, src=src: e.transpose(bk.ap[0:64, pr * 128:(pr + 1) * 128], src.ap[:, pr, tc], ident.ap), reads=[src, ident], writes=[bk])
                P.op("act" if dst in (tmB, tmA) else "dve",
                     (lambda e, bk=bk, dst=dst: e.activation(out=dst.ap, in_=bk.ap[0:64, :], func=AF.Copy)) if dst in (tmB, tmA) else
                     (lambda e, bk=bk, dst=dst: e.tensor_copy(out=dst.ap, in_=bk.ap[0:64, :])), reads=[bk], writes=[dst])
            specs = [(bbar, abar, Nt[0], m_lt), (abar, bbar, Nn[0], m_gt), (abar, kbar, Aak, m_gt), (bbar, rbar, RbT, m_le), (kbar, rbar, RkT, m_le)]
            for si, (la, ra, dst, mk) in enumerate(specs):
                bke = [nb(), nb()]
                for h_ in range(8):
                    bk = bke[h_ % 2]
                    P.op("pe", lambda e, h_=h_, bk=bk, la=la, ra=ra: e.matmul(bk.ap[0:64, (h_ // 2) * 64:(h_ // 2 + 1) * 64], lhsT=hs(la, h_), rhs=hs(ra, h_), start=True, stop=True),
                         reads=[la, ra], writes=[bk])
                for e_ in range(2):
                    bk = bke[e_]
                    dv = dst.ap.rearrange("p (a e c) -> p a e c", e=2, c=64)[:, :, e_, :]
                    P.op("dve", lambda e, bk=bk, dv=dv, mk=mk: e.tensor_tensor(out=dv, in0=bk.ap[0:64, 0:256].rearrange("p (a c) -> p a c", c=64),
                                                                               in1=mk.unsqueeze(1).to_broadcast([64, 4, 64]), op=ALU.mult),
                         reads=[bk, msk], writes=[dst])
            P.op("pool", lambda e: e.tensor_tensor(out=v8(Tt), in0=v8(Nt[0]), in1=identI.ap, op=ALU.add), reads=[Nt[0], identI], writes=[Tt])
            cur = 0
            for st in range(5):
                nxt = 1 - cur
                bN = nb()
                for h_ in range(8):
                    hc = slice(h_ * 64, (h_ + 1) * 64)
                    P.op("pe", lambda e, hc=hc, bN=bN, cur=cur: e.matmul(bN.ap[0:64, hc], lhsT=Nt[cur].ap[:, hc], rhs=Nn[cur].ap[:, hc], start=True, stop=True),
                         reads=[Nt[cur], Nn[cur]], writes=[bN])
                P.op("act", lambda e, bN=bN, nxt=nxt: e.activation(out=Nn[nxt].ap, in_=bN.ap[0:64, :], func=AF.Copy), reads=[bN], writes=[Nn[nxt]])
                if st < 4:
                    bNt = nb()
                    for h_ in range(8):
                        hc = slice(h_ * 64, (h_ + 1) * 64)
                        P.op("pe", lambda e, hc=hc, bNt=bNt, cur=cur: e.matmul(bNt.ap[0:64, hc], lhsT=Nn[cur].ap[:, hc], rhs=Nt[cur].ap[:, hc], start=True, stop=True),
                             reads=[Nt[cur], Nn[cur]], writes=[bNt])
                    P.op("dve", lambda e, bNt=bNt, nxt=nxt: e.tensor_copy(out=Nt[nxt].ap, in_=bNt.ap[0:64, :]), reads=[bNt], writes=[Nt[nxt]])
                bT = nb()
                for h_ in range(8):
                    hc = slice(h_ * 64, (h_ + 1) * 64)
                    P.op("pe", lambda e, hc=hc, bT=bT, nxt=nxt: e.matmul(bT.ap[0:64, hc], lhsT=Nn[nxt].ap[:, hc], rhs=Tt.ap[:, hc], start=True, stop=True),
                         reads=[Nn[nxt], Tt], writes=[bT])
                P.op("dve", lambda e, bT=bT: e.tensor_tensor(out=Tt.ap, in0=bT.ap[0:64, :], in1=Tt.ap, op=ALU.add), reads=[bT, Tt], writes=[Tt])
                cur = nxt
            bW = nb()
            for h_ in range(8):
                e_, pr = h_ % 2, h_ // 2
                P.op("pe", lambda e, h_=h_, e_=e_, pr=pr, bW=bW: e.matmul(bW.ap[e_ * 64:e_ * 64 + 64, pr * 64:(pr + 1) * 64], lhsT=tmA.ap[:, h_ * 64:(h_ + 1) * 64],
                                                                          rhs=Tt.ap[:, h_ * 64:(h_ + 1) * 64], start=True, stop=True), reads=[tmA, Tt], writes=[bW])
            P.op("act", lambda e, bW=bW: e.activation(out=Wt.ap, in_=bW.ap[:, 0:256].rearrange("p (a c) -> p a c", c=64), func=AF.Copy), reads=[bW], writes=[Wt])
            bG = nb()
            for h_ in range(8):
                hc = slice(h_ * 64, (h_ + 1) * 64)
                P.op("pe", lambda e, hc=hc, bG=bG: e.matmul(bG.ap[0:64, hc], lhsT=Aak.ap[:, hc], rhs=Tt.ap[:, hc], start=True, stop=True), reads=[Aak, Tt], writes=[bG])
            P.op("dve", lambda e, bG=bG: e.tensor_copy(out=Gt.ap, in_=bG.ap[0:64, :]), reads=[bG], writes=[Gt])
            bU0 = nb()
            for h_ in range(8):
                hc = slice(h_ * 64, (h_ + 1) * 64)
                P.op("pe", lambda e, hc=hc, bU0=bU0: e.matmul(bU0.ap[0:64, hc], lhsT=Gt.ap[:, hc], rhs=tmV.ap[:, hc], start=True, stop=True), reads=[Gt, tmV], writes=[bU0])
            P.op("act", lambda e, bU0=bU0: e.activation(out=Usb.ap, in_=bU0.ap[0:64, :], func=AF.Copy), reads=[bU0], writes=[Usb])
            bUe = [nb(), nb()]
            for h_ in range(8):
                e_, pr = h_ % 2, h_ // 2
                P.op("pe", lambda e, e_=e_, pr=pr: e.matmul(bUe[e_].ap[0:64, pr * 64:(pr + 1) * 64], lhsT=Wt.ap[e_ * 64:e_ * 64 + 64, pr, :], rhs=Hst.ap[e_ * 64:e_ * 64 + 64, pr, :],
                                                           start=True, stop=True), reads=[Wt, Hst], writes=[bUe[e_]])
            for e_ in range(2):
                uv = Usb.ap.rearrange("p (a e c) -> p a e c", e=2, c=64)[:, :, e_, :]
                P.op("dve", lambda e, e_=e_, uv=uv: e.tensor_tensor(out=uv, in0=bUe[e_].ap[0:64, 0:256].rearrange("p (a c) -> p a c", c=64), in1=uv, op=ALU.add),
                     reads=[bUe[e_], Usb], writes=[Usb])
            bY = nb()
            for h_ in range(8):
                e_, pr = h_ % 2, h_ // 2
                hc = slice(h_ * 64, (h_ + 1) * 64)
                o_ = bY.ap[e_ * 64:e_ * 64 + 64, pr * 64:(pr + 1) * 64]
                P.op("pe", lambda e, hc=hc, o_=o_: e.matmul(o_, lhsT=Usb.ap[:, hc], rhs=RbT.ap[:, hc], start=True, stop=False), reads=[Usb, RbT], writes=[bY])
                P.op("pe", lambda e, hc=hc, o_=o_: e.matmul(o_, lhsT=tmV.ap[:, hc], rhs=RkT.ap[:, hc], start=False, stop=True), reads=[tmV, RkT], writes=[bY])
            P.op("act", lambda e, bY=bY: e.activation(out=t_y.ap[:, :, tc], in_=bY.ap[:, 0:256].rearrange("p (a c) -> p a c", c=64), func=AF.Copy), reads=[bY], writes=[t_y])
            bYe = [nb(), nb()]
            for h_ in range(8):
                e_, pr = h_ % 2, h_ // 2
                P.op("pe", lambda e, h_=h_, e_=e_, pr=pr: e.matmul(bYe[e_].ap[e_ * 64:e_ * 64 + 64, pr * 64:(pr + 1) * 64], lhsT=Hst.ap[e_ * 64:e_ * 64 + 64, pr, :], rhs=hs(rbar, h_),
                                                                  start=True, stop=True), reads=[Hst, rbar], writes=[bYe[e_]])
            for e_ in range(2):
                yv = t_y.ap[e_ * 64:e_ * 64 + 64, :, tc]
                P.op("dve", lambda e, e_=e_, yv=yv: e.tensor_tensor(out=yv, in0=bYe[e_].ap[e_ * 64:e_ * 64 + 64, 0:256].rearrange("p (a c) -> p a c", c=64), in1=yv, op=ALU.add),
                     reads=[bYe[e_], t_y], writes=[t_y])
            bH = nb()
            for h_ in range(8):
                e_, pr = h_ % 2, h_ // 2
                hc = slice(h_ * 64, (h_ + 1) * 64)
                o_ = bH.ap[e_ * 64:e_ * 64 + 64, pr * 64:(pr + 1) * 64]
                P.op("pe", lambda e, hc=hc, o_=o_: e.matmul(o_, lhsT=tmB.ap[:, hc], rhs=Usb.ap[:, hc], start=True, stop=False), reads=[tmB, Usb], writes=[bH])
                P.op("pe", lambda e, hc=hc, o_=o_: e.matmul(o_, lhsT=tmK.ap[:, hc], rhs=tmV.ap[:, hc], start=False, stop=True), reads=[tmK, tmV], writes=[bH])
            P.op("dve", lambda e, bH=bH: e.tensor_tensor(out=Htmp.ap, in0=bH.ap[:, 0:256].rearrange("p (a c) -> p a c", c=64), in1=Hst.ap, op=ALU.add), reads=[bH, Hst], writes=[Htmp])
            pC = eP.ap[:, :, ch * CN + CN - 1:ch * CN + CN].to_broadcast([128, 4, 64])
            P.op("dve", lambda e, pC=pC: e.tensor_tensor(out=Hst.ap, in0=Htmp.ap, in1=pC, op=ALU.mult), reads=[Htmp, eP], writes=[Hst])

        for pr in range(4):
            b1 = nb()
            P.op("pe", lambda e, pr=pr, b1=b1: e.matmul(b1.ap[:, 0:NT], lhsT=bones.ap, rhs=t_y.ap[:, pr, :], start=True, stop=True), reads=[bones, t_y], writes=[b1])
            P.op("pool", lambda e, pr=pr: e.tensor_tensor(out=tmp4.ap[:, pr, :], in0=t_y.ap[:, pr, :], in1=t_y.ap[:, pr, :], op=ALU.mult), reads=[t_y, tmp4], writes=[tmp4])
            b2 = nb()
            P.op("pe", lambda e, pr=pr, b2=b2: e.matmul(b2.ap[:, 0:NT], lhsT=bones.ap, rhs=tmp4.ap[:, pr, :], start=True, stop=True), reads=[bones, tmp4], writes=[b2])
            mean = xt[0]; var = xt[1]
            P.op("act", lambda e, b1=b1: e.activation(out=mean.ap, in_=b1.ap[:, 0:NT], func=AF.Copy, scale=1.0 / 64.0), reads=[b1], writes=[mean])
            P.op("pool", lambda e: e.tensor_tensor(out=var.ap, in0=mean.ap, in1=mean.ap, op=ALU.mult), reads=[mean], writes=[var])
            P.op("dve", lambda e, b2=b2: e.scalar_tensor_tensor(out=var.ap, in0=b2.ap[:, 0:NT], scalar=1.0 / 64.0, in1=var.ap, op0=ALU.mult, op1=ALU.subtract), reads=[b2, var], writes=[var])
            P.op("dve", lambda e: e.tensor_scalar(out=var.ap, in0=var.ap, scalar1=64e-5, scalar2=None, op0=ALU.add), reads=[var], writes=[var])
            P.op("act", lambda e: e.activation(out=var.ap, in_=var.ap, func=AF.Sqrt), reads=[var], writes=[var])
            P.op("dve", lambda e: e.reciprocal(out=var.ap, in_=var.ap), reads=[var], writes=[var])
            yv = t_y.ap[:, pr, :]
            P.op("dve", lambda e, yv=yv: e.tensor_tensor(out=yv, in0=yv, in1=mean.ap, op=ALU.subtract), reads=[t_y, mean], writes=[t_y])
            P.op("pool", lambda e, yv=yv: e.tensor_tensor(out=yv, in0=yv, in1=var.ap, op=ALU.mult), reads=[t_y, var], writes=[t_y])
            P.op("dve", lambda e, yv=yv, pr=pr: e.tensor_scalar(out=yv, in0=yv, scalar1=PLW(pr), scalar2=PLB(pr), op0=ALU.mult, op1=ALU.add), reads=[t_y, par], writes=[t_y])
            P.op("pool", lambda e, yv=yv, pr=pr: e.tensor_tensor(out=yv, in0=yv, in1=t_bon.ap[:, pr, :], op=ALU.add), reads=[t_y, t_bon], writes=[t_y])
            P.op("dve", lambda e, yv=yv, pr=pr: e.tensor_tensor(out=tmp4.ap[:, pr, :], in0=yv, in1=t_g.ap[:, pr, :], op=ALU.mult), reads=[t_y, t_g, tmp4], writes=[tmp4])
            P.dma("sp", zT.ap[pr * 128:(pr + 1) * 128, t0:t0 + NT], tmp4.ap[:, pr, :], reads=[tmp4], writes=[zT], join=True)
    P.finish([zT], "sp")
    P.close()
    return nc, P


def mixC_consts(NT):
    s = np.arange(64)[:, None]; t = np.arange(64)[None, :]
    masks = np.concatenate([(s <= t), (s < t), (s > t)], axis=1).astype(np.float32)
    rmask = np.where(np.arange(NT) % 64 == 0, 0.0, 1.0).astype(np.float32)
    scan = np.broadcast_to(rmask[None, :], (128, NT)).copy()
    bones = np.zeros((128, 128), np.float32); bones[:64, :64] = 1.0; bones[64:, 64:] = 1.0
    return masks, scan, bones


def mixC_core_inputs(inp, c):
    cs = slice(c * 512, (c + 1) * 512)
    d = dict(wr=np.ascontiguousarray(inp["rwkv_w_r"][0][:, cs]), wk=np.ascontiguousarray(inp["rwkv_w_k"][0][:, cs]),
             wv=np.ascontiguousarray(inp["rwkv_w_v"][0][:, cs]), w1=inp["rwkv_w1"][0], a1=inp["rwkv_a1"][0], g1=inp["rwkv_g1"][0],
             w2=np.ascontiguousarray(inp["rwkv_w2"][0][:, cs]), a2=np.ascontiguousarray(inp["rwkv_a2"][0][:, cs]),
             g2=np.ascontiguousarray(inp["rwkv_g2"][0][:, cs]))
    names = ["rwkv_w0", "rwkv_a0", "rwkv_k_k", "rwkv_k_a", "rwkv_r_k", "rwkv_lnx_w", "rwkv_lnx_b"]
    d["par"] = np.concatenate([pp(np.asarray(inp[n][0]).reshape(-1)[cs]) for n in names], axis=1)
    return d


def build_memkv():
    nc = bass.Bass("TRN2", target_bir_lowering=False)
    nc.dge_precook = False
    P = Prog(nc)
    NT = NMEM
    memT = P.dram("memT", [D, NMEM], F32, "ExternalInput")
    wkv = P.dram("wkv", [D, 1024], F32, "ExternalInput")
    g_d = P.dram("g_mem", [128, KC], F32, "ExternalInput")
    ident_d = P.dram("ident", [128, 128], F32, "ExternalInput")
    kvT = P.dram("kvT", [1024, NMEM], F32, "ExternalOutput")
    kvM = P.dram("kvM", [NMEM, 1024], F32, "ExternalOutput")
    c = make_common(P, NT)
    g = P.sbuf("g_sb", [128, KC]); P.dma("sp", g.ap, g_d.ap, writes=[g])
    ident = P.sbuf("ident_sb", [128, 128]); P.dma("sp", ident.ap, ident_d.ap, writes=[ident])
    m_all = P.sbuf("m", [128, KC, NT]); m = P.subs(m_all, KC)
    n_all = P.sbuf("mn", [128, KC, NT]); mn = P.subs(n_all, KC)
    kv_all = P.sbuf("kv", [128, 8, NT]); kv = P.subs(kv_all, 8)
    kvm = P.sbuf("kvm", [128, 2, 1024])
    trp = [P.psum(f"trp{i}", [128, 512]) for i in range(2)]
    for k in range(KC):
        P.dma("sp" if k % 2 == 0 else "act", m[k].ap, memT.ap[k * 128:(k + 1) * 128, :], writes=[m[k]])
    rstd = rms_stats(c, m, float(D), EPS)
    rms_apply(c, m, mn, g, rstd)

    def evac(oc, acc):
        P.op("act", lambda e: e.activation(out=kv[oc].ap, in_=acc.ap[:, 0:NT], func=AF.Copy), reads=[acc], writes=[kv[oc]])
        P.dma("sp", kvT.ap[oc * 128:(oc + 1) * 128, :], kv[oc].ap, reads=[kv[oc]], writes=[kvT], join=True)
        for mt in range(2):
            tp = trp[mt]
            P.op("pe", lambda e, mt=mt, tp=tp: e.transpose(tp.ap[:, 0:128], kv[oc].ap[:, mt * 128:(mt + 1) * 128], ident.ap), reads=[kv[oc], ident], writes=[tp])
            P.op("dve", lambda e, mt=mt, tp=tp: e.tensor_copy(out=kvm.ap[:, mt, oc * 128:(oc + 1) * 128], in_=tp.ap[:, 0:128]), reads=[tp], writes=[kvm])
    gemm_fm(c, wkv, 0, KC, 0, 8, mn, evac)
    P.dma("sp", kvM.ap.rearrange("(t p) d -> p t d", p=128), kvm.ap, reads=[kvm], writes=[kvM])
    P.finish([kvT, kvM], "sp")
    P.close()
    return nc, P


NCORE = 8
_CACHE = {}


def _prog(name, fn):
    if name not in _CACHE:
        _CACHE[name] = fn()[0]
    return _CACHE[name]


def _run(nc, in_maps):
    res = run_bass_kernel_spmd(nc, in_maps, core_ids=list(range(len(in_maps))))
    return res.results


def kernel(**inp):
    inp = {k: np.asarray(v) for k, v in inp.items()}
    Tn = inp["x"].shape[1]
    TOK = Tn // NCORE
    ident = np.eye(128, dtype=np.float32)
    xT = fm(inp["x"][0])
    memT = fm(inp["mem"][0])
    g_mem = pp(inp["mem_norm_g"])
    res = _run(_prog("memkv", build_memkv),
               [dict(memT=memT, wkv=np.ascontiguousarray(inp["mem_w_kv"][:, c * 1024:(c + 1) * 1024]), g_mem=g_mem, ident=ident) for c in range(NCORE)])
    kT = np.concatenate([res[c]["kvT"] for c in range(4)], axis=0)
    vM = np.concatenate([res[c]["kvM"] for c in range(4, 8)], axis=1)
    masks, scan = mixA_consts(512)
    g_mix0 = pp(inp["norm_mix_g"][0])
    in_maps = []
    for c in range(NCORE):
        wfm, par = mixA_core_inputs(inp, c)
        in_maps.append(dict(xT=xT, wfm=wfm, g_mix=g_mix0, par=par, masks=masks, scanmask=scan, ident=ident))
    res = _run(_prog(("mixA", Tn), lambda: build_mixA(Tn)), in_maps)
    z0T = np.empty((D, Tn), np.float32)
    ss = np.empty((24, Tn), np.float32)
    for c in range(NCORE):
        hm, half = c // 2, c % 2
        z0T[hm * 512 + half * 256:hm * 512 + half * 256 + 256] = res[c]["zT"][0:256]
        z0T[2048 + 2 * c * 128:2048 + (2 * c + 2) * 128] = res[c]["zT"][256:512]
        ss[c] = res[c]["ssO"][0]
        ss[8 + 2 * c] = res[c]["ssO"][1]
        ss[8 + 2 * c + 1] = res[c]["ssO"][2]
    del res, in_maps
    g_head = pp(np.concatenate([inp["mlstm_norm_g"][0], inp["hgrn_norm_g"][0]]))

    def tok_layer(layer, hT_full, zT_full, w_mo, layer0, final, ss_full=None):
        gains = np.concatenate([pp(inp["norm_xattn_g"][layer]), pp(inp["norm_mlp_g"][layer]), pp(inp["final_norm_g"]),
                                g_head if layer0 else pp(np.ones(D, np.float32))], axis=1)
        common = dict(w_mo=w_mo, w_q=inp["xattn_w_q"][layer], w_o=inp["xattn_w_o"][layer], w_up=inp["mlp_w_up"][layer],
                      w_dn=inp["mlp_w_down"][layer], kT=kT, vM=vM, gains=gains, ident=ident)
        maps = []
        for c in range(NCORE):
            ts = slice(c * TOK, (c + 1) * TOK)
            d = dict(common, hT=np.ascontiguousarray(hT_full[:, ts]), zT=np.ascontiguousarray(zT_full[:, ts]))
            if layer0:
                d["ss"] = np.ascontiguousarray(ss_full[:, ts])
            maps.append(d)
        r = _run(_prog(("tok", TOK, layer0, final), lambda: build_tok(TOK, 256, layer0, final)), maps)
        return np.concatenate([r[c]["oT"] for c in range(NCORE)], axis=1)

    h1T = tok_layer(0, xT, z0T, inp["ab_w_out"][0], True, False, ss)
    del z0T
    masksC, scanC, bones = mixC_consts(256)
    mu = np.concatenate([pp(inp["rwkv_mu"][0, j]) for j in range(6)], axis=1)
    g_mix1 = pp(inp["norm_mix_g"][1])
    in_maps = []
    for c in range(NCORE):
        d = mixC_core_inputs(inp, c)
        d.update(hT=h1T, g_mix=g_mix1, mu=mu, masks=masksC, scanmask=scanC, ident=ident, bones=bones)
        in_maps.append(d)
    res = _run(_prog(("mixC", Tn), lambda: build_mixC(Tn, 256)), in_maps)
    z1T = np.concatenate([res[c]["zT"] for c in range(NCORE)], axis=0)
    del res, in_maps
    outT = tok_layer(1, h1T, z1T, inp["rwkv_w_o"][0], False, True)
    return np.ascontiguousarray(outT.T)[None].astype(np.float32)
```

```python
import numpy as np
import concourse.bass as bass
import concourse.mybir as mybir
from concourse.bass_utils import run_bass_kernel_spmd

F32 = mybir.dt.float32
F32R = mybir.dt.float32r
ALU = mybir.AluOpType
AF = mybir.ActivationFunctionType
AX = mybir.AxisListType

D = 4096
KC = D // 128
SEQ = 8192
NMEM = 256
EPS = 1e-6
SEM_CHUNK = 32000


class T:
    __slots__ = ("ap", "name", "w", "r")

    def __init__(self, ap, name):
        self.ap = ap
        self.name = name
        self.w = []
        self.r = {}

    def __getitem__(self, idx):
        return self.ap[idx]

    def r32(self):
        return self.ap.bitcast(F32R)


NPOOL = 20


class Prog:
    def __init__(self, nc, same_engine_sync=True):
        self.nc = nc
        self.eng = {"pe": nc.tensor, "act": nc.scalar, "dve": nc.vector,
                    "pool": nc.gpsimd, "sp": nc.sync}
        self.seq = {k: 0 for k in self.eng}
        self.sems = {k: [] for k in self.eng}
        self.waited = {}
        self.same_engine_sync = same_engine_sync
        self.stack = []
        self.nsem = 0
        self.n_wait = 0
        self.n_ins = 0
        self.dpool = {}
        self.dpi = {}
        self.last = {}
        self.dma_toks = {}

    def _enter(self, cm):
        obj = cm.__enter__()
        self.stack.append(cm)
        return obj

    def close(self):
        while self.stack:
            self.stack.pop().__exit__(None, None, None)

    def sem(self, name):
        self.nsem += 1
        return self._enter(self.nc.semaphore(name))

    def sbuf(self, name, shape, dtype=F32):
        h = self._enter(self.nc.sbuf_tensor(name, list(shape), dtype))
        return T(h[tuple(slice(None) for _ in shape)], name)

    def psum(self, name, shape, dtype=F32):
        h = self._enter(self.nc.psum_tensor(name, list(shape), dtype))
        return T(h[tuple(slice(None) for _ in shape)], name)

    def dram(self, name, shape, dtype=F32, kind="Internal"):
        return T(self.nc.dram_tensor(name, list(shape), dtype, kind=kind).ap(), name)

    def subs(self, t, n, axis=1):
        out = []
        for i in range(n):
            idx = [slice(None)] * len(t.ap.shape)
            idx[axis] = i
            out.append(T(t.ap[tuple(idx)], f"{t.name}_{i}"))
        return out

    def _csem(self, e, k):
        ci = (k - 1) // SEM_CHUNK
        while len(self.sems[e]) <= ci:
            self.sems[e].append(self.sem(f"s_{e}_{len(self.sems[e])}"))
        return self.sems[e][ci], (k - 1) % SEM_CHUNK + 1

    def _wait(self, e, tok):
        sem, val, src = tok
        if src == e and (e == "pe" or not self.same_engine_sync):
            return
        key = (e, id(sem))
        if self.waited.get(key, 0) >= val:
            return
        self.waited[key] = val
        self.eng[e].wait_ge(sem, val)
        self.n_wait += 1

    def _deps(self, e, reads, writes, join=False):
        toks = []
        for t in reads:
            toks.extend(t.w)
        for t in writes:
            if not join:
                toks.extend(t.w)
            else:
                toks.extend(tk for tk in t.w if tk[2] != "dma")
            toks.extend(t.r.values())
        for tok in toks:
            self._wait(e, tok)

    def _commit(self, tok, reads, writes, join=False):
        key = id(tok[0])
        for t in reads:
            old = t.r.get(key)
            if old is None or old[1] < tok[1]:
                t.r[key] = tok
        for t in writes:
            if join:
                t.w = [tk for tk in t.w if tk[2] == "dma"] + [tok]
            else:
                t.w = [tok]
                t.r = {}

    limit = None

    def op(self, e, fn, reads=(), writes=()):
        if self.limit is not None and self.n_ins >= self.limit:
            return None
        self._deps(e, reads, writes)
        ins = fn(self.eng[e])
        self.seq[e] += 1
        sem, val = self._csem(e, self.seq[e])
        ins.then_inc(sem, 1)
        self._commit((sem, val, e), reads, writes)
        self.last[e] = (sem, val, e)
        self.n_ins += 1
        return ins

    def barrier(self):
        toks = list(self.last.values()) + list(self.dma_toks.values())
        for e in self.eng:
            for tok in toks:
                self._wait(e, tok)

    def dma(self, q, out_ap, in_ap, reads=(), writes=(), join=False, **kw):
        if self.limit is not None and self.n_ins >= self.limit:
            return None
        if q not in self.dpool:
            self.dpool[q] = [[self.sem(f"dq_{q}_{i}"), 0] for i in range(NPOOL)]
            self.dpi[q] = 0
        slot = self.dpool[q][self.dpi[q] % NPOOL]
        self.dpi[q] += 1
        if slot[1] > 0:
            self._wait(q, (slot[0], slot[1], "dma"))
        self._deps(q, reads, writes, join=join)
        ins = self.eng[q].dma_start(out=out_ap, in_=in_ap, **kw)
        slot[1] += 16
        ins.then_inc(slot[0], 16)
        self._commit((slot[0], slot[1], "dma"), reads, writes, join=join)
        self.dma_toks[id(slot[0])] = (slot[0], slot[1], "dma")
        self.n_ins += 1
        return ins

    def finish(self, tiles, e="sp"):
        for t in tiles:
            for tok in t.w:
                self._wait(e, tok)
            for tok in t.r.values():
                self._wait(e, tok)


class Ctx:
    pass


def make_common(P, NT, n_wbuf=3, wcols=128, n_acc=3, own_ssps=True):
    c = Ctx()
    c.P = P
    c.NT = NT
    c.wcols = wcols
    c.wbufs = [P.sbuf(f"wbuf{i}", [128, KC, wcols], F32R) for i in range(n_wbuf)]
    c.accs = [P.psum(f"acc{i}", [128, 512], F32) for i in range(n_acc)]
    c.wi = 0
    c.ai = 0
    c.ones = P.sbuf("ones", [128, 128], F32)
    c.ones_f = P.sbuf("ones_f", [128, 128], F32)
    P.op("dve", lambda e: e.memset(c.ones_f.ap, 1.0), writes=[c.ones_f])
    P.op("dve", lambda e: e.tensor_copy(out=c.ones.r32(), in_=c.ones_f.ap), reads=[c.ones_f], writes=[c.ones])
    c.sq = [P.sbuf(f"sq{i}", [128, NT], F32) for i in range(2)]
    c.sqi = 0
    c.ssps = P.psum("ssps", [128, 512], F32) if own_ssps else c.accs[0]
    c.rstd = P.sbuf("rstd", [128, NT], F32)
    c.nrm_tmp = [P.sbuf(f"nrm_tmp{i}", [128, NT], F32) for i in range(2)]
    c.nti = 0
    return c


def next_w(c):
    w = c.wbufs[c.wi % len(c.wbufs)]
    c.wi += 1
    return w


def next_acc(c):
    a = c.accs[c.ai % len(c.accs)]
    c.ai += 1
    return a


def gemm_fm(c, W, row0, kc, col0, n_oc, act_chunks, evac, NT=None, q="sp"):
    P = c.P
    NT = NT or c.NT
    G = 4 if (kc * 4 <= KC and n_oc % 4 == 0) else 1
    for og in range(n_oc // G):
        wb = next_w(c)
        c0 = col0 + og * G * 128
        wv = wb.ap.rearrange("p k c -> p (k c)")[:, 0:kc * G * 128].rearrange("p (k c) -> p k c", c=G * 128)
        src = W.ap[row0:row0 + kc * 128, c0:c0 + G * 128].bitcast(F32R).rearrange("(k p) c -> p k c", p=128)
        half = (kc + 1) // 2
        P.dma(q, wv[:, 0:half, :], src[:, 0:half, :], writes=[wb])
        if kc > half:
            P.dma(q, wv[:, half:kc, :], src[:, half:kc, :], writes=[wb], join=True)
        for g_ in range(G):
            oc = og * G + g_
            acc = next_acc(c)
            for k in range(kc):
                a = act_chunks[k]
                P.op("pe", lambda e, k=k, a=a, g_=g_, acc=acc: e.matmul(acc.ap[:, 0:NT], lhsT=wv[:, k, g_ * 128:(g_ + 1) * 128], rhs=a.r32(),
                                                                        start=(k == 0), stop=(k == kc - 1)),
                     reads=[wb, a], writes=[acc])
            evac(oc, acc)


def rms_stats(c, chunks, dim, eps, NT=None):
    P = c.P
    NT = NT or c.NT
    n = len(chunks)
    for i, x in enumerate(chunks):
        sq = c.sq[c.sqi % 2]
        c.sqi += 1
        P.op("act", lambda e, x=x, sq=sq: e.activation(out=sq.r32(), in_=x.ap, func=AF.Square),
             reads=[x], writes=[sq])
        P.op("pe", lambda e, sq=sq, i=i: e.matmul(c.ssps.ap[:, 0:NT], lhsT=c.ones.r32(), rhs=sq.r32(),
                                                  start=(i == 0), stop=(i == n - 1)),
             reads=[c.ones, sq], writes=[c.ssps])
    P.op("dve", lambda e: e.tensor_scalar(out=c.rstd.ap, in0=c.ssps.ap[:, 0:NT], scalar1=1.0 / dim, scalar2=eps,
                                          op0=ALU.mult, op1=ALU.add), reads=[c.ssps], writes=[c.rstd])
    P.op("act", lambda e: e.activation(out=c.rstd.ap, in_=c.rstd.ap, func=AF.Sqrt), reads=[c.rstd], writes=[c.rstd])
    P.op("dve", lambda e: e.reciprocal(out=c.rstd.ap, in_=c.rstd.ap), reads=[c.rstd], writes=[c.rstd])
    return c.rstd


def rms_apply(c, src_chunks, dst_chunks, gain, rstd):
    P = c.P
    for k, (s, d) in enumerate(zip(src_chunks, dst_chunks)):
        if k % 2 == 0:
            P.op("dve", lambda e, s=s, d=d, k=k: e.scalar_tensor_tensor(out=d.r32(), in0=s.ap, scalar=gain.ap[:, k:k + 1],
                                                                        in1=rstd.ap, op0=ALU.mult, op1=ALU.mult),
                 reads=[s, gain, rstd], writes=[d])
        else:
            tmp = c.nrm_tmp[c.nti % 2]; c.nti += 1
            P.op("act", lambda e, s=s, k=k, tmp=tmp: e.activation(out=tmp.ap, in_=s.ap, func=AF.Copy, scale=gain.ap[:, k:k + 1]),
                 reads=[s, gain], writes=[tmp])
            P.op("pool", lambda e, d=d, tmp=tmp: e.tensor_tensor(out=d.r32(), in0=tmp.ap, in1=rstd.ap, op=ALU.mult),
                 reads=[tmp, rstd], writes=[d])


def build_tok(NTOK, NT, layer0, final, HB=8):
    nc = bass.Bass("TRN2", target_bir_lowering=False)
    nc.dge_precook = False
    P = Prog(nc)
    NTILE = NTOK // NT
    hT = P.dram("hT", [D, NTOK], F32, "ExternalInput")
    zT = P.dram("zT", [D, NTOK], F32, "ExternalInput")
    w_mo = P.dram("w_mo", [D, D], F32, "ExternalInput")
    w_q = P.dram("w_q", [D, D], F32, "ExternalInput")
    w_o = P.dram("w_o", [D, D], F32, "ExternalInput")
    w_up = P.dram("w_up", [D, 4 * D], F32, "ExternalInput")
    w_dn = P.dram("w_dn", [4 * D, D], F32, "ExternalInput")
    kT = P.dram("kT", [D, NMEM], F32, "ExternalInput")
    vM = P.dram("vM", [NMEM, D], F32, "ExternalInput")
    gains_d = P.dram("gains", [128, 4 * KC], F32, "ExternalInput")
    ident_d = P.dram("ident", [128, 128], F32, "ExternalInput")
    if layer0:
        ss_d = P.dram("ss", [24, NTOK], F32, "ExternalInput")
    oT = P.dram("oT", [D, NTOK], F32, "ExternalOutput")

    MT = 2 * NT
    c = make_common(P, NT, n_wbuf=2)
    c5 = Ctx()
    c5.__dict__.update(c.__dict__)
    c5.NT = MT
    c5.sq = [P.sbuf(f"sq5_{i}", [128, MT], F32) for i in range(2)]
    c5.rstd = P.sbuf("rstd5", [128, MT], F32)
    c5.nrm_tmp = [P.sbuf(f"nrm5_{i}", [128, MT], F32) for i in range(2)]
    gains = P.sbuf("gains_sb", [128, 4 * KC], F32)
    P.dma("sp", gains.ap, gains_d.ap, writes=[gains])
    g_x = T(gains.ap[:, 0:KC], "g_x"); g_m = T(gains.ap[:, KC:2 * KC], "g_m")
    g_f = T(gains.ap[:, 2 * KC:3 * KC], "g_f"); g_h = T(gains.ap[:, 3 * KC:4 * KC], "g_h")
    for g in (g_x, g_m, g_f, g_h):
        g.w = gains.w
    ident = P.sbuf("ident_sb", [128, 128], F32)
    P.dma("sp", ident.ap, ident_d.ap, writes=[ident])

    H = P.sbuf("h", [128, KC, MT], F32)
    XO = P.sbuf("xo", [128, KC, MT], F32)
    hs = [[T(H.ap[:, k, sb * NT:(sb + 1) * NT], f"h{sb}_{k}") for k in range(KC)] for sb in range(2)]
    hm = [T(H.ap[:, k, :], f"hm_{k}") for k in range(KC)]
    xin = [T(XO.ap[:, k, 0:NT], f"xin_{k}") for k in range(KC)]
    ot = [T(XO.ap[:, k, NT:MT], f"ot_{k}") for k in range(KC)]
    xm = [T(XO.ap[:, k, :], f"xm_{k}") for k in range(KC)]
    assert HB * MT == 8 * NT + 8 * NMEM
    arena = P.sbuf("arena", [128, HB * MT], F32)
    qh = [T(arena.ap[:, j * NT:(j + 1) * NT], f"qh_{j}") for j in range(8)]
    kth = T(arena.ap[:, 8 * NT:8 * NT + 8 * NMEM].bitcast(F32R).rearrange("p (k m) -> p k m", m=NMEM), "kth")
    hid = [T(arena.ap[:, j * MT:(j + 1) * MT], f"hid_{j}") for j in range(HB)]
    relu_t = c5.nrm_tmp
    vh = P.sbuf("vh", [128, 2, 1024], F32R)
    pexp = [P.sbuf(f"pexp{i}", [128, NMEM], F32) for i in range(2)]
    pT = P.sbuf("pT", [128, 2, NT], F32)
    stat = [P.sbuf(f"stat{i}", [128, 4], F32) for i in range(2)]
    sps = [P.psum(f"sps{i}", [128, 512], F32) for i in range(2)]
    trps = [P.psum(f"trps{i}", [128, 512], F32) for i in range(2)]
    ssb = c.nrm_tmp if layer0 else None
    NSUB = NT // 128

    hcur = [None]

    def resid_evac(oc, acc):
        h = hcur[0]
        n_ = h[oc].ap.shape[-1]
        P.op("dve", lambda e: e.tensor_tensor(out=h[oc].ap, in0=acc.ap[:, 0:n_], in1=h[oc].ap, op=ALU.add),
             reads=[acc, h[oc]], writes=[h[oc]])

    for ti in range(NTILE):
        t0 = ti * NT
        sb = ti % 2
        h = hs[sb]
        hcur[0] = h
        for k in range(KC):
            P.dma("sp" if k % 2 == 0 else "act", h[k].ap, hT.ap[k * 128:(k + 1) * 128, t0:t0 + NT], writes=[h[k]])
        for k in range(KC):
            P.dma("act" if k % 2 == 0 else "sp", xin[k].r32(), zT.ap[k * 128:(k + 1) * 128, t0:t0 + NT].bitcast(F32R),
                  writes=[xin[k]])
        if layer0:
            for hd in range(20):
                sa = ssb[0]
                if hd < 4:
                    P.dma("sp", sa.ap, ss_d.ap[2 * hd:2 * hd + 1, t0:t0 + NT].partition_broadcast(128), writes=[sa])
                    sb_ = ssb[1]
                    P.dma("sp", sb_.ap, ss_d.ap[2 * hd + 1:2 * hd + 2, t0:t0 + NT].partition_broadcast(128), writes=[sb_])
                    P.op("dve", lambda e: e.tensor_tensor(out=sa.ap, in0=sa.ap, in1=sb_.ap, op=ALU.add), reads=[sa, sb_], writes=[sa])
                    dv = 512.0
                    chunks = list(range(hd * 4, hd * 4 + 4))
                else:
                    P.dma("sp", sa.ap, ss_d.ap[8 + hd - 4:8 + hd - 3, t0:t0 + NT].partition_broadcast(128), writes=[sa])
                    dv = 128.0
                    chunks = [16 + hd - 4]
                P.op("dve", lambda e: e.tensor_scalar(out=sa.ap, in0=sa.ap, scalar1=1.0 / dv, scalar2=EPS, op0=ALU.mult, op1=ALU.add),
                     reads=[sa], writes=[sa])
                P.op("act", lambda e: e.activation(out=sa.ap, in_=sa.ap, func=AF.Sqrt), reads=[sa], writes=[sa])
                P.op("dve", lambda e: e.reciprocal(out=sa.ap, in_=sa.ap), reads=[sa], writes=[sa])
                for k in chunks:
                    P.op("dve", lambda e, k=k: e.scalar_tensor_tensor(out=xin[k].r32(), in0=xin[k].ap, scalar=g_h.ap[:, k:k + 1],
                                                                      in1=sa.ap, op0=ALU.mult, op1=ALU.mult),
                         reads=[xin[k], g_h, sa], writes=[xin[k]])
        gemm_fm(c, w_mo, 0, KC, 0, KC, xin, resid_evac)
        rstd = rms_stats(c, h, float(D), EPS)
        rms_apply(c, h, xin, g_x, rstd)
        for hd in range(4):
            P.dma("act", kth.ap, kT.ap[hd * 1024:(hd + 1) * 1024, :].bitcast(F32R).rearrange("(k p) m -> p k m", p=128),
                  writes=[kth])
            P.dma("act", vh.ap, vM.ap[:, hd * 1024:(hd + 1) * 1024].bitcast(F32R).rearrange("(t p) d -> p t d", p=128),
                  writes=[vh])

            def q_evac(oc, acc):
                P.op("act", lambda e: e.activation(out=qh[oc].r32(), in_=acc.ap[:, 0:NT], func=AF.Copy, scale=1.0 / 32.0),
                     reads=[acc], writes=[qh[oc]])
            gemm_fm(c, w_q, 0, KC, hd * 1024, 8, xin, q_evac)
            for ts in range(NSUB):
                sp_ = sps[ts % 2]; pe_ = pexp[ts % 2]; st = stat[ts % 2]
                for j in range(8):
                    P.op("pe", lambda e, j=j: e.matmul(sp_.ap[:, 0:NMEM], lhsT=qh[j].r32()[:, ts * 128:(ts + 1) * 128],
                                                       rhs=kth.ap[:, j, :], start=(j == 0), stop=(j == 7)),
                         reads=[qh[j], kth], writes=[sp_])
                P.op("dve", lambda e: e.reduce_max(out=st.ap[:, 0:1], in_=sp_.ap[:, 0:NMEM], axis=AX.X), reads=[sp_], writes=[st])
                P.op("dve", lambda e: e.tensor_single_scalar(out=st.ap[:, 1:2], in_=st.ap[:, 0:1], scalar=-1.0, op=ALU.mult),
                     reads=[st], writes=[st])
                P.op("act", lambda e: e.activation(out=pe_.ap, in_=sp_.ap[:, 0:NMEM], func=AF.Exp, bias=st.ap[:, 1:2], scale=1.0,
                                                   accum_out=st.ap[:, 2:3]), reads=[sp_, st], writes=[pe_, st])
                P.op("dve", lambda e: e.reciprocal(out=st.ap[:, 3:4], in_=st.ap[:, 2:3]), reads=[st], writes=[st])
                P.op("dve", lambda e: e.tensor_scalar(out=pe_.ap, in0=pe_.ap, scalar1=st.ap[:, 3:4], scalar2=None, op0=ALU.mult),
                     reads=[pe_, st], writes=[pe_])
                for mt in range(2):
                    tp = trps[mt]
                    P.op("pe", lambda e, mt=mt, tp=tp: e.transpose(tp.ap[:, 0:128], pe_.ap[:, mt * 128:(mt + 1) * 128], ident.ap),
                         reads=[pe_, ident], writes=[tp])
                    P.op("act" if mt == 0 else "dve",
                         (lambda e, mt=mt, tp=tp: e.activation(out=pT.r32()[:, mt, ts * 128:(ts + 1) * 128], in_=tp.ap[:, 0:128], func=AF.Copy))
                         if mt == 0 else
                         (lambda e, mt=mt, tp=tp: e.tensor_copy(out=pT.r32()[:, mt, ts * 128:(ts + 1) * 128], in_=tp.ap[:, 0:128])),
                         reads=[tp], writes=[pT])
            for j in range(8):
                acc = next_acc(c)
                for mt in range(2):
                    P.op("pe", lambda e, j=j, mt=mt: e.matmul(acc.ap[:, 0:NT], lhsT=vh.ap[:, mt, j * 128:(j + 1) * 128],
                                                              rhs=pT.r32()[:, mt, :], start=(mt == 0), stop=(mt == 1)),
                         reads=[vh, pT], writes=[acc])
                oc = hd * 8 + j
                P.op("act", lambda e, oc=oc, acc=acc: e.activation(out=ot[oc].r32(), in_=acc.ap[:, 0:NT], func=AF.Copy),
                     reads=[acc], writes=[ot[oc]])
        gemm_fm(c, w_o, 0, KC, 0, KC, ot, resid_evac)
        if sb == 0:
            continue
        P.barrier()
        tm = (ti - 1) * NT
        hcur[0] = hm
        rstd = rms_stats(c5, hm, float(D), EPS, NT=MT)
        rms_apply(c5, hm, xm, g_m, rstd)
        for hb in range(4 * KC // HB):
            def up_evac(oc, acc):
                rt = relu_t[oc % 2]
                P.op("act", lambda e: e.activation(out=rt.ap, in_=acc.ap[:, 0:MT], func=AF.Relu), reads=[acc], writes=[rt])
                P.op("pool", lambda e: e.tensor_tensor(out=hid[oc].r32(), in0=rt.ap, in1=rt.ap, op=ALU.mult), reads=[rt], writes=[hid[oc]])
            gemm_fm(c5, w_up, 0, KC, hb * HB * 128, HB, xm, up_evac, NT=MT)
            gemm_fm(c5, w_dn, hb * HB * 128, HB, 0, KC, hid, resid_evac, NT=MT)
        if final:
            rstd = rms_stats(c5, hm, float(D), EPS, NT=MT)
            rms_apply(c5, hm, xm, g_f, rstd)
            src = xm
        else:
            src = hm
        for k in range(KC):
            P.dma("sp" if k % 2 == 0 else "act", oT.ap[k * 128:(k + 1) * 128, tm:tm + MT], src[k].ap, reads=[src[k]], writes=[oT], join=True)
        P.barrier()
    P.finish([oT], "sp")
    P.finish([oT], "act")
    P.close()
    return nc, P


NFM = 18
L = 64


def build_mixA(TT, NT=512, limit=None):
    nc = bass.Bass("TRN2", target_bir_lowering=False)
    nc.dge_precook = False
    P = Prog(nc)
    P.limit = limit
    NG = TT // NT
    NCH = NT // L
    xT = P.dram("xT", [D, TT], F32, "ExternalInput")
    wfm = P.dram("wfm", [D, NFM * 128], F32, "ExternalInput")
    g_d = P.dram("g_mix", [128, KC], F32, "ExternalInput")
    par_d = P.dram("par", [128, 8], F32, "ExternalInput")
    msk_d = P.dram("masks", [64, 3 * 64], F32, "ExternalInput")
    scm_d = P.dram("scanmask", [128, 2 * NT], F32, "ExternalInput")
    ident_d = P.dram("ident", [128, 128], F32, "ExternalInput")
    zT = P.dram("zT", [512, TT], F32, "ExternalOutput")
    ssO = P.dram("ssO", [3, TT], F32, "ExternalOutput")

    c = make_common(P, NT, n_wbuf=2, n_acc=2, own_ssps=False)
    g_mix = P.sbuf("g_mix_sb", [128, KC]); P.dma("sp", g_mix.ap, g_d.ap, writes=[g_mix])
    par = P.sbuf("par_sb", [128, 8]); P.dma("sp", par.ap, par_d.ap, writes=[par])
    msk = P.sbuf("msk_sb", [64, 192]); P.dma("sp", msk.ap, msk_d.ap, writes=[msk])
    scm = P.sbuf("scm_sb", [128, 2 * NT]); P.dma("sp", scm.ap, scm_d.ap, writes=[scm])
    ident = P.sbuf("ident_sb", [128, 128]); P.dma("sp", ident.ap, ident_d.ap, writes=[ident])
    mask01 = msk.ap[:, 0:64]; maskneg = msk.ap[:, 64:128]
    rmask = scm.ap[:, 0:NT]; rneg = scm.ap[:, NT:2 * NT]
    pp = P.sbuf("pp", [128, 8])
    P.op("dve", lambda e: e.tensor_single_scalar(out=pp.ap[:, 0:2], in_=par.ap[:, 0:2], scalar=1.0 / 15.0, op=ALU.mult), reads=[par], writes=[pp])
    P.op("dve", lambda e: e.tensor_tensor(out=pp.ap[:, 2:4], in0=par.ap[:, 2:4], in1=par.ap[:, 4:6], op=ALU.subtract), reads=[par, pp], writes=[pp])
    P.op("act", lambda e: e.activation(out=pp.ap[:, 2:4], in_=pp.ap[:, 2:4], func=AF.Sigmoid), reads=[pp], writes=[pp])
    P.op("dve", lambda e: e.tensor_scalar(out=pp.ap[:, 4:6], in0=pp.ap[:, 2:4], scalar1=-1.0, scalar2=1.0, op0=ALU.mult, op1=ALU.add), reads=[pp], writes=[pp])

    x_all = P.sbuf("xg", [128, KC, NT]); xg = P.subs(x_all, KC)
    pf_all = P.sbuf("pf", [128, NFM, NT]); pf = P.subs(pf_all, NFM)
    gt = {n: P.sbuf("g_" + n, [128, NT]) for n in ["b", "a", "cm", "gi", "fl"]}
    gt["M"] = gt["cm"]; gt["negM"] = gt["cm"]; gt["w"] = c.nrm_tmp[0]
    mv = P.sbuf("mvec", [128, 2 * NCH + 2])
    pk = P.sbuf("pk", [64, NT])
    qg_all = P.sbuf("qg", [128, 2, NT]); qg = P.subs(qg_all, 2)
    hA = [P.sbuf(f"hA{i}", [128, NT]) for i in range(2)]
    heA = [P.sbuf(f"heA{i}", [128, NT]) for i in range(2)]
    hqt = [P.sbuf(f"hqt{i}", [128, NT]) for i in range(2)]
    hkt = [P.sbuf(f"hkt{i}", [128, NT]) for i in range(2)]
    hkh = hA
    Cst = [P.sbuf(f"Cst{i}", [128, 256]) for i in range(2)]
    nbc = [P.sbuf(f"nbc{i}", [128, 128]) for i in range(2)]
    Sst = [P.sbuf(f"Sst{i}", [128, 128]) for i in range(2)]
    for t_ in Cst + nbc + Sst:
        P.op("pool", lambda e, t_=t_: e.memset(t_.ap, 0.0), writes=[t_])
    P.op("dve", lambda e: e.memset(mv.ap, 0.0), writes=[mv])
    tmA = P.sbuf("tmA", [64, 512])
    tmB = P.sbuf("tmB", [64, 320])
    tmC = P.sbuf("tmC", [64, 256])
    kw = P.sbuf("kw", [64, 256])
    dmt = P.sbuf("dmt", [64, 64]); smt = P.sbuf("smt", [64, 64])
    hsm = [P.sbuf(f"hsm{i}", [64, 64]) for i in range(2)]
    rec = P.sbuf("rec", [128, 64])
    hm_all = P.sbuf("hm", [128, 2, NT]); hm = P.subs(hm_all, 2)
    ho = [P.sbuf(f"ho{i}", [128, NT]) for i in range(2)]
    ssr = P.sbuf("ssr", [128, NT])
    tr1 = P.psum("tr1", [128, 512]); tr2 = P.psum("tr2", [128, 512]); tr3 = None
    nd = P.psum("nd", [128, 512]); cps = P.psum("cps", [128, 512]); nps = P.psum("nps", [128, 512])
    sTp = P.psum("sTp", [128, 512])
    sT = T(sTp.ap[0:64, 0:192], "sT")
    tr2a = tr2
    f32 = lambda t: t.ap

    for g in range(NG):
        t0 = g * NT
        for k in range(KC):
            P.dma("sp" if k % 2 == 0 else "act", xg[k].r32(), xT.ap[k * 128:(k + 1) * 128, t0:t0 + NT].bitcast(F32R), writes=[xg[k]])
        rstd = rms_stats(c, xg, float(D), EPS)
        rms_apply(c, xg, xg, g_mix, rstd)

        def evac(oc, acc):
            a_ = acc.ap[:, 0:NT]; o_ = pf[oc]
            if oc in (0, 1):
                P.op("act", lambda e: e.activation(out=o_.ap, in_=a_, func=AF.Copy, scale=1.0 / 16.0), reads=[acc], writes=[o_])
            elif oc in (2, 3, 14, 15, 16, 17):
                P.op("dve", lambda e: e.tensor_copy(out=o_.ap, in_=a_), reads=[acc], writes=[o_])
            elif oc in (4, 5, 10, 11):
                P.op("act", lambda e: e.activation(out=o_.ap, in_=a_, func=AF.Sigmoid), reads=[acc], writes=[o_])
            elif oc in (6, 7):
                P.op("act", lambda e: e.activation(out=o_.ap, in_=a_, func=AF.Tanh, scale=1.0 / 15.0, bias=pp.ap[:, oc - 6:oc - 5]),
                     reads=[acc, pp], writes=[o_])
            else:
                P.op("act", lambda e: e.activation(out=o_.ap, in_=a_, func=AF.Silu), reads=[acc], writes=[o_])
        gemm_fm(c, wfm, 0, KC, 0, NFM, xg, evac)

        ti_, tf_ = pf[6], pf[7]
        b, a, cm, M, negM, gi, fl, wv = (gt[n] for n in ["b", "a", "cm", "M", "negM", "gi", "fl", "w"])
        P.op("act", lambda e: e.activation(out=tf_.ap, in_=tf_.ap, func=AF.Sigmoid, scale=15.0), reads=[tf_], writes=[tf_])
        P.op("act", lambda e: e.activation(out=tf_.ap, in_=tf_.ap, func=AF.Ln), reads=[tf_], writes=[tf_])
        P.op("dve", lambda e: e.tensor_tensor_scan(out=b.ap, data0=rmask, data1=tf_.ap, initial=0.0, op0=ALU.mult, op1=ALU.add),
             reads=[scm, tf_], writes=[b])
        P.op("dve", lambda e: e.scalar_tensor_tensor(out=a.ap, in0=ti_.ap, scalar=15.0, in1=b.ap, op0=ALU.mult, op1=ALU.subtract),
             reads=[ti_, b], writes=[a])
        P.op("dve", lambda e: e.tensor_tensor_scan(out=cm.ap, data0=rneg, data1=a.ap, initial=-1e30, op0=ALU.add, op1=ALU.max),
             reads=[scm, a], writes=[cm])
        cm_end = cm.ap.rearrange("p (c l) -> p c l", l=L)[:, :, L - 1]
        b_end = b.ap.rearrange("p (c l) -> p c l", l=L)[:, :, L - 1]
        P.op("dve", lambda e: e.tensor_tensor_scan(out=mv.ap[:, NCH:2 * NCH], data0=cm_end, data1=b_end, initial=mv.ap[:, 2 * NCH:2 * NCH + 1],
                                                   op0=ALU.max, op1=ALU.add), reads=[cm, b, mv], writes=[mv])
        P.op("dve", lambda e: e.tensor_copy(out=mv.ap[:, 0:1], in_=mv.ap[:, 2 * NCH:2 * NCH + 1]), reads=[mv], writes=[mv])
        P.op("dve", lambda e: e.tensor_copy(out=mv.ap[:, 1:NCH], in_=mv.ap[:, NCH:2 * NCH - 1]), reads=[mv], writes=[mv])
        P.op("dve", lambda e: e.tensor_copy(out=mv.ap[:, 2 * NCH:2 * NCH + 1], in_=mv.ap[:, 2 * NCH - 1:2 * NCH]), reads=[mv], writes=[mv])
        mprev_bc = mv.ap[:, 0:NCH].unsqueeze(2).to_broadcast([128, NCH, L])
        v3 = lambda t: t.ap.rearrange("p (c l) -> p c l", l=L)
        P.op("dve", lambda e: e.tensor_tensor(out=v3(M), in0=v3(cm), in1=mprev_bc, op=ALU.max), reads=[cm, mv], writes=[M])
        P.op("dve", lambda e: e.tensor_single_scalar(out=negM.ap, in_=M.ap, scalar=-1.0, op=ALU.mult), reads=[M], writes=[negM])
        P.op("dve", lambda e: e.tensor_tensor(out=v3(gi), in0=v3(negM), in1=mprev_bc, op=ALU.add), reads=[negM, mv], writes=[gi])
        P.op("act", lambda e: e.activation(out=gi.ap, in_=gi.ap, func=AF.Exp), reads=[gi], writes=[gi])
        P.op("dve", lambda e: e.tensor_tensor(out=fl.ap, in0=negM.ap, in1=b.ap, op=ALU.subtract), reads=[negM, b], writes=[fl])
        P.op("act", lambda e: e.activation(out=fl.ap, in_=fl.ap, func=AF.Exp), reads=[fl], writes=[fl])
        negML_bc = v3(negM)[:, :, L - 1:L].to_broadcast([128, NCH, L])
        P.op("dve", lambda e: e.tensor_tensor(out=v3(wv), in0=v3(a), in1=negML_bc, op=ALU.add), reads=[a, negM], writes=[wv])
        P.op("act", lambda e: e.activation(out=wv.ap, in_=wv.ap, func=AF.Exp), reads=[wv], writes=[wv])
        P.op("dve", lambda e: e.tensor_copy(out=pk.ap[0:32, :], in_=a.ap[0:32, :]), reads=[a], writes=[pk])
        P.op("dve", lambda e: e.tensor_copy(out=pk.ap[32:64, :], in_=wv.ap[32:64, :]), reads=[wv, pk], writes=[pk])
        for dch in range(2):
            P.op("pool", lambda e, dch=dch: e.tensor_tensor(out=qg[dch].ap, in0=pf[dch].ap, in1=gi.ap, op=ALU.mult),
                 reads=[pf[dch], gi], writes=[qg[dch]])

        for hh in range(2):
            sg = pf[10 + hh]
            P.op("dve", lambda e, hh=hh, sg=sg: e.tensor_scalar(out=sg.ap, in0=sg.ap, scalar1=pp.ap[:, 4 + hh:5 + hh], scalar2=pp.ap[:, 2 + hh:3 + hh],
                                                         op0=ALU.mult, op1=ALU.add), reads=[sg, pp], writes=[sg])
            P.op("act", lambda e, hh=hh, sg=sg: e.activation(out=hA[hh].ap, in_=sg.ap, func=AF.Ln), reads=[sg], writes=[hA[hh]])
            P.op("dve", lambda e, hh=hh: e.tensor_tensor_scan(out=hA[hh].ap, data0=rmask, data1=hA[hh].ap, initial=0.0, op0=ALU.mult, op1=ALU.add),
                 reads=[scm, hA[hh]], writes=[hA[hh]])
            P.op("act", lambda e, hh=hh: e.activation(out=heA[hh].ap, in_=hA[hh].ap, func=AF.Exp), reads=[hA[hh]], writes=[heA[hh]])
            P.op("pool", lambda e, hh=hh: e.tensor_tensor(out=hqt[hh].ap, in0=pf[8 + hh].ap, in1=heA[hh].ap, op=ALU.mult),
                 reads=[pf[8 + hh], heA[hh]], writes=[hqt[hh]])
            P.op("dve", lambda e, hh=hh, sg=sg: e.tensor_scalar(out=sg.ap, in0=sg.ap, scalar1=-1.0, scalar2=1.0, op0=ALU.mult, op1=ALU.add),
                 reads=[sg], writes=[sg])
            P.op("act", lambda e, hh=hh: e.activation(out=hkt[hh].ap, in_=hA[hh].ap, func=AF.Exp, scale=-1.0), reads=[hA[hh]], writes=[hkt[hh]])
            P.op("pool", lambda e, hh=hh, sg=sg: e.tensor_tensor(out=hkt[hh].ap, in0=hkt[hh].ap, in1=sg.ap, op=ALU.mult),
                 reads=[hkt[hh], sg], writes=[hkt[hh]])
            eAl_bc = heA[hh].ap.rearrange("p (c l) -> p c l", l=L)[:, :, L - 1:L].to_broadcast([128, NCH, L])
            P.op("dve", lambda e, hh=hh: e.tensor_tensor(out=v3(hkh[hh]), in0=v3(hkt[hh]), in1=eAl_bc, op=ALU.mult),
                 reads=[hkt[hh], heA[hh]], writes=[hkh[hh]])

        for ch in range(NCH):
            tc = slice(ch * L, (ch + 1) * L)
            for j in range(2):
                P.op("pe", lambda e, j=j: e.transpose(tr1.ap[0:64, j * 128:(j + 1) * 128], pf[2 + j].ap[:, tc], ident.ap), reads=[pf[2 + j], ident], writes=[tr1])
            for j in range(2):
                P.op("pe", lambda e, j=j: e.transpose(tr1.ap[0:64, 256 + j * 128:256 + (j + 1) * 128], pf[14 + j].ap[:, tc], ident.ap), reads=[pf[14 + j], ident], writes=[tr1])
            P.op("act", lambda e: e.activation(out=tmA.ap, in_=tr1.ap[0:64, :], func=AF.Copy), reads=[tr1], writes=[tmA])
            P.op("pe", lambda e: e.transpose(tr2.ap[0:64, 0:64], pk.ap[:, tc], ident.ap[0:64, 0:64]), reads=[pk, ident], writes=[tr2a])
            for j in range(2):
                P.op("pe", lambda e, j=j: e.transpose(tr2.ap[0:64, 64 + j * 128:64 + (j + 1) * 128], pf[16 + j].ap[:, tc], ident.ap), reads=[pf[16 + j], ident], writes=[tr2a])
            P.op("dve", lambda e: e.tensor_copy(out=tmB.ap, in_=tr2.ap[0:64, 0:320]), reads=[tr2a], writes=[tmB])
            for j in range(2):
                P.op("pe", lambda e, j=j: e.transpose(sTp.ap[0:64, 192 + j * 128:192 + (j + 1) * 128], hkh[j].ap[:, tc], ident.ap), reads=[hkh[j], ident], writes=[sTp])
            P.op("act", lambda e: e.activation(out=tmC.ap, in_=sTp.ap[0:64, 192:448], func=AF.Copy), reads=[sTp], writes=[tmC])
            ktm = tmA.ap[:, 0:256]; vtm = tmA.ap[:, 256:512]
            a_col = tmB.ap[:, 0:1]; w_col = tmB.ap[:, 32:33]
            for dch in range(2):
                P.op("pe", lambda e, dch=dch: e.matmul(sT.ap[:, 0:64], lhsT=pf[2 + dch].ap[:, tc], rhs=pf[dch].ap[:, tc], start=(dch == 0), stop=(dch == 1)),
                     reads=[pf[2 + dch], pf[dch]], writes=[sTp])
            for hh in range(2):
                P.op("pe", lambda e, hh=hh: e.matmul(sT.ap[:, 64 + hh * 64:128 + hh * 64], lhsT=hkt[hh].ap[:, tc], rhs=hqt[hh].ap[:, tc], start=True, stop=True),
                     reads=[hkt[hh], hqt[hh]], writes=[sTp])
            P.op("dve", lambda e: e.scalar_tensor_tensor(out=dmt.ap, in0=negM.ap[0:64, tc], scalar=a_col, in1=maskneg, op0=ALU.add, op1=ALU.add),
                 reads=[negM, tmB, msk], writes=[dmt])
            P.op("act", lambda e: e.activation(out=dmt.ap, in_=dmt.ap, func=AF.Exp), reads=[dmt], writes=[dmt])
            P.op("dve", lambda e: e.tensor_tensor(out=smt.ap, in0=sT.ap[:, 0:64], in1=dmt.ap, op=ALU.mult), reads=[sTp, dmt], writes=[smt])
            for hh in range(2):
                P.op("dve", lambda e, hh=hh: e.tensor_tensor(out=hsm[hh].ap, in0=sT.ap[:, 64 + hh * 64:128 + hh * 64], in1=mask01, op=ALU.mult),
                     reads=[sTp, msk], writes=[hsm[hh]])
            for vch in range(2):
                o_ = nd.ap[:, vch * 64:(vch + 1) * 64]
                P.op("pe", lambda e, vch=vch, o_=o_: e.matmul(o_, lhsT=vtm[:, vch * 128:(vch + 1) * 128], rhs=smt.ap, start=True, stop=False),
                     reads=[tmA, smt], writes=[nd])
                for dch in range(2):
                    P.op("pe", lambda e, vch=vch, dch=dch, o_=o_: e.matmul(o_, lhsT=Cst[dch].ap[:, vch * 128:(vch + 1) * 128], rhs=qg[dch].ap[:, tc],
                                                                          start=False, stop=(dch == 1)), reads=[Cst[dch], qg[dch]], writes=[nd])
            o_ = nd.ap[:, 128:192]
            P.op("pe", lambda e: e.matmul(o_, lhsT=c.ones.ap[0:64, :], rhs=smt.ap, start=True, stop=False), reads=[c.ones, smt], writes=[nd])
            for dch in range(2):
                P.op("pe", lambda e, dch=dch: e.matmul(o_, lhsT=nbc[dch].ap, rhs=qg[dch].ap[:, tc], start=False, stop=(dch == 1)),
                     reads=[nbc[dch], qg[dch]], writes=[nd])
            for hh in range(2):
                oo = nd.ap[:, 192 + hh * 64:256 + hh * 64]
                P.op("pe", lambda e, hh=hh, oo=oo: e.matmul(oo, lhsT=tmB.ap[:, 64 + hh * 128:192 + hh * 128], rhs=hsm[hh].ap, start=True, stop=False),
                     reads=[tmB, hsm[hh]], writes=[nd])
                P.op("pe", lambda e, hh=hh, oo=oo: e.matmul(oo, lhsT=Sst[hh].ap, rhs=hqt[hh].ap[:, tc], start=False, stop=True),
                     reads=[Sst[hh], hqt[hh]], writes=[nd])
            P.op("act", lambda e: e.activation(out=rec.ap, in_=nd.ap[:, 128:192], func=AF.Abs), reads=[nd], writes=[rec])
            P.op("dve", lambda e: e.tensor_tensor(out=rec.ap, in0=rec.ap, in1=fl.ap[:, tc], op=ALU.max), reads=[rec, fl], writes=[rec])
            P.op("dve", lambda e: e.reciprocal(out=rec.ap, in_=rec.ap), reads=[rec], writes=[rec])
            for vch in range(2):
                P.op("dve", lambda e, vch=vch: e.tensor_tensor(out=hm[vch].ap[:, tc], in0=nd.ap[:, vch * 64:(vch + 1) * 64], in1=rec.ap, op=ALU.mult),
                     reads=[nd, rec], writes=[hm[vch]])
            for hh in range(2):
                P.op("act", lambda e, hh=hh: e.activation(out=ho[hh].ap[:, tc], in_=nd.ap[:, 192 + hh * 64:256 + hh * 64], func=AF.Copy),
                     reads=[nd], writes=[ho[hh]])
            P.op("dve", lambda e: e.tensor_scalar(out=kw.ap, in0=ktm, scalar1=w_col, scalar2=None, op0=ALU.mult), reads=[tmA, tmB], writes=[kw])
            for dch in range(2):
                P.op("pe", lambda e, dch=dch: e.matmul(cps.ap[:, dch * 256:(dch + 1) * 256], lhsT=kw.ap[:, dch * 128:(dch + 1) * 128], rhs=vtm, start=True, stop=True),
                     reads=[kw, tmA], writes=[cps])
            for dch in range(2):
                P.op("pe", lambda e, dch=dch: e.matmul(nps.ap[:, dch * 128:(dch + 1) * 128], lhsT=kw.ap[:, dch * 128:(dch + 1) * 128], rhs=c.ones.ap[0:64, :], start=True, stop=True),
                     reads=[kw, c.ones], writes=[nps])
            for hh in range(2):
                P.op("pe", lambda e, hh=hh: e.matmul(nps.ap[:, 256 + hh * 128:384 + hh * 128], lhsT=tmC.ap[:, hh * 128:(hh + 1) * 128],
                                                     rhs=tmB.ap[:, 64 + hh * 128:192 + hh * 128], start=True, stop=True), reads=[tmC, tmB], writes=[nps])
            dec = gi.ap[:, ch * L + L - 1:ch * L + L]
            for dch in range(2):
                P.op("dve", lambda e, dch=dch: e.scalar_tensor_tensor(out=Cst[dch].ap, in0=Cst[dch].ap, scalar=dec, in1=cps.ap[:, dch * 256:(dch + 1) * 256],
                                                                      op0=ALU.mult, op1=ALU.add), reads=[Cst[dch], gi, cps], writes=[Cst[dch]])
                P.op("dve", lambda e, dch=dch: e.scalar_tensor_tensor(out=nbc[dch].ap, in0=nbc[dch].ap, scalar=dec, in1=nps.ap[:, dch * 128:(dch + 1) * 128],
                                                                      op0=ALU.mult, op1=ALU.add), reads=[nbc[dch], gi, nps], writes=[nbc[dch]])
            for hh in range(2):
                eal = heA[hh].ap[:, ch * L + L - 1:ch * L + L]
                P.op("dve", lambda e, hh=hh, eal=eal: e.scalar_tensor_tensor(out=Sst[hh].ap, in0=Sst[hh].ap, scalar=eal, in1=nps.ap[:, 256 + hh * 128:384 + hh * 128],
                                                                            op0=ALU.mult, op1=ALU.add), reads=[Sst[hh], heA[hh], nps], writes=[Sst[hh]])

        outs = [(hm[0], pf[4]), (hm[1], pf[5]), (ho[0], pf[12]), (ho[1], pf[13])]
        groups = [[0, 1], [2], [3]]
        for gi_, idxs in enumerate(groups):
            for ii, ix in enumerate(idxs):
                sq = c.sq[c.sqi % 2]; c.sqi += 1
                src = outs[ix][0]
                P.op("act", lambda e, sq=sq, src=src: e.activation(out=sq.r32(), in_=src.ap, func=AF.Square), reads=[src], writes=[sq])
                P.op("pe", lambda e, sq=sq, ii=ii, idxs=idxs: e.matmul(c.accs[1].ap[:, 0:NT], lhsT=c.ones.r32(), rhs=sq.r32(), start=(ii == 0), stop=(ii == len(idxs) - 1)),
                     reads=[c.ones, sq], writes=[c.accs[1]])
            P.op("dve", lambda e: e.tensor_copy(out=ssr.ap, in_=c.accs[1].ap[:, 0:NT]), reads=[c.accs[1]], writes=[ssr])
            P.dma("sp", ssO.ap[gi_:gi_ + 1, t0:t0 + NT], ssr.ap[0:1, :], reads=[ssr], writes=[ssO], join=True)
        for ix, (src, gate) in enumerate(outs):
            P.op("pool", lambda e, ix=ix, src=src, gate=gate: e.tensor_tensor(out=gate.ap, in0=src.ap, in1=gate.ap, op=ALU.mult),
                 reads=[src, gate], writes=[gate])
            P.dma("sp", zT.ap[ix * 128:(ix + 1) * 128, t0:t0 + NT], gate.ap, reads=[gate], writes=[zT], join=True)
    P.finish([zT, ssO], "sp")
    P.close()
    return nc, P


IN_SIZES = (1024, 1024, 2048, 2048, 4, 4, 2048, 2048, 2048, 2048)


def fm(a):
    return np.ascontiguousarray(np.asarray(a).T)


def pp(g):
    g = np.asarray(g, dtype=np.float32).reshape(-1)
    return np.ascontiguousarray(g.reshape(-1, 128).T)


def mixA_consts(NT):
    s = np.arange(64)[:, None]; t = np.arange(64)[None, :]
    m01 = (s <= t).astype(np.float32)
    mneg = np.where(s <= t, 0.0, -1e30).astype(np.float32)
    masks = np.concatenate([m01, mneg, np.zeros((64, 64), np.float32)], axis=1)
    st = (np.arange(NT) % 64 == 0)
    rmask = np.where(st, 0.0, 1.0).astype(np.float32)
    rneg = np.where(st, -1e30, 0.0).astype(np.float32)
    scan = np.broadcast_to(np.concatenate([rmask, rneg])[None, :], (128, 2 * NT)).copy()
    return masks, scan


def mixA_core_inputs(inp, c):
    W = inp["ab_w_in"][0]
    offs = np.cumsum([0] + list(IN_SIZES))
    hm, half = c // 2, c % 2
    r = np.arange
    cols = np.concatenate([
        offs[0] + hm * 256 + r(256), offs[1] + hm * 256 + r(256), offs[3] + hm * 512 + half * 256 + r(256),
        np.full(128, offs[4] + hm), np.full(128, offs[5] + hm),
        offs[6] + 2 * c * 128 + r(256), offs[7] + 2 * c * 128 + r(256), offs[9] + 2 * c * 128 + r(256),
        offs[2] + hm * 512 + half * 256 + r(256), offs[8] + 2 * c * 128 + r(256)])
    wfm = np.ascontiguousarray(W[:, cols])
    par = np.zeros((128, 8), np.float32)
    par[:, 0] = inp["mlstm_b_i"][0, hm]
    par[:, 1] = inp["mlstm_b_f"][0, hm]
    lg = inp["hgrn_lb_logits"]
    for hh in range(2):
        par[:, 2 + hh] = lg[0, (2 * c + hh) * 128:(2 * c + hh + 1) * 128]
        par[:, 4 + hh] = lg[1, (2 * c + hh) * 128:(2 * c + hh + 1) * 128]
    return wfm, par


CN = 64
EXPM05 = 0.6065306597126334


def build_mixC(TT, NT=256, limit=None):
    nc = bass.Bass("TRN2", target_bir_lowering=False)
    nc.dge_precook = False
    P = Prog(nc)
    P.limit = limit
    NG = TT // NT
    NCH = NT // CN
    hT = P.dram("hT", [D, TT], F32, "ExternalInput")
    g_d = P.dram("g_mix", [128, KC], F32, "ExternalInput")
    mu_d = P.dram("mu", [128, 6 * KC], F32, "ExternalInput")
    wr_d = P.dram("wr", [D, 512], F32, "ExternalInput")
    wk_d = P.dram("wk", [D, 512], F32, "ExternalInput")
    wv_d = P.dram("wv", [D, 512], F32, "ExternalInput")
    w1_d = P.dram("w1", [D, 128], F32, "ExternalInput")
    a1_d = P.dram("a1", [D, 128], F32, "ExternalInput")
    g1_d = P.dram("g1", [D, 480], F32, "ExternalInput")
    w2_d = P.dram("w2", [128, 512], F32, "ExternalInput")
    a2_d = P.dram("a2", [128, 512], F32, "ExternalInput")
    g2_d = P.dram("g2", [480, 512], F32, "ExternalInput")
    par_d = P.dram("par", [128, 28], F32, "ExternalInput")
    msk_d = P.dram("masks", [64, 192], F32, "ExternalInput")
    scm_d = P.dram("scanmask", [128, NT], F32, "ExternalInput")
    ident_d = P.dram("ident", [128, 128], F32, "ExternalInput")
    bones_d = P.dram("bones", [128, 128], F32, "ExternalInput")
    zT = P.dram("zT", [512, TT], F32, "ExternalOutput")

    def ld(name, src, shape, dtype=F32, ap=None):
        t = P.sbuf(name, shape, dtype)
        P.dma("sp", t.ap, ap if ap is not None else src.ap, writes=[t])
        return t
    g_mix = ld("g_mix_sb", g_d, [128, KC])
    mu = ld("mu_sb", mu_d, [128, 6 * KC])
    par = ld("par_sb", par_d, [128, 28])
    msk = ld("msk_sb", msk_d, [64, 192])
    scm = ld("scm_sb", scm_d, [128, NT])
    ident = ld("ident_sb", ident_d, [128, 128])
    bones = ld("bones_sb", bones_d, [128, 128])
    w2 = ld("w2_sb", w2_d, [128, 512], F32R, w2_d.ap.bitcast(F32R))
    a2 = ld("a2_sb", a2_d, [128, 512], F32R, a2_d.ap.bitcast(F32R))
    g2 = ld("g2_sb", g2_d, [120, 4, 512], F32R, g2_d.ap.bitcast(F32R).rearrange("(k p) c -> p k c", p=120))
    omu = P.sbuf("omu", [128, 6 * KC])
    P.op("dve", lambda e: e.tensor_scalar(out=omu.ap, in0=mu.ap, scalar1=-1.0, scalar2=1.0, op0=ALU.mult, op1=ALU.add), reads=[mu], writes=[omu])
    m_le = msk.ap[:, 0:64]; m_lt = msk.ap[:, 64:128]; m_gt = msk.ap[:, 128:192]
    PW0, PA0, PKK, PKA, PRK, PLW, PLB = ((lambda i: (lambda pr: par.ap[:, 4 * i + pr:4 * i + pr + 1]))(i) for i in range(7))

    ones = P.sbuf("ones", [128, 128]); ones_f = P.sbuf("ones_f", [128, 128])
    P.op("dve", lambda e: e.memset(ones_f.ap, 1.0), writes=[ones_f])
    P.op("dve", lambda e: e.tensor_copy(out=ones.r32(), in_=ones_f.ap), reads=[ones_f], writes=[ones])

    pb = [P.psum(f"pb{i}", [128, 512]) for i in range(8)]
    pbi = [0]

    def nb():
        b = pb[pbi[0] % 8]; pbi[0] += 1
        return b

    hx_all = P.sbuf("hx", [128, KC, NT + 1]); hx = P.subs(hx_all, KC)
    P.op("pool", lambda e: e.memset(hx_all.ap[:, :, 0:1], 0.0), writes=hx)
    xs = [P.sbuf(f"xs{i}", [128, NT]) for i in range(3)]
    xt = [P.sbuf(f"xt{i}", [128, NT]) for i in range(3)]
    KB = 4
    wk_ = [P.sbuf(f"wkb{i}", [128, KB, 512], F32R) for i in range(2)]
    sq = [P.sbuf(f"sq{i}", [128, NT]) for i in range(2)]
    rstd = P.sbuf("rstd", [128, NT])
    lora = [P.sbuf(f"lora{i}", [128, NT]) for i in range(2)]
    lg = P.sbuf("lorag", [120, 4, NT])
    fmn = ["r", "k", "v", "a", "g", "kk", "bon", "lw", "P", "y"]
    fmt = {n: P.sbuf("t_" + n, [128, 4, NT]) for n in fmn}
    t_r, t_k, t_v, t_a, t_g, t_kk, t_bon, t_lw, t_P, t_y = (fmt[n] for n in fmn)
    tmp4 = P.sbuf("tmp4", [128, 4, NT])
    tmB = P.sbuf("tmB", [64, 512]); tmK = P.sbuf("tmK", [64, 512]); tmA = P.sbuf("tmA", [64, 512]); tmV = P.sbuf("tmV", [64, 512])
    Nn = [P.sbuf(f"Nn{i}", [64, 512]) for i in range(2)]
    Nt = [P.sbuf(f"Nt{i}", [64, 512]) for i in range(2)]
    Tt = P.sbuf("Tt", [64, 512]); RbT = P.sbuf("RbT", [64, 512]); RkT = P.sbuf("RkT", [64, 512]); Aak = P.sbuf("Aak", [64, 512])
    Wt = P.sbuf("Wt", [128, 4, 64]); Gt = P.sbuf("Gt", [64, 512]); Usb = P.sbuf("Usb", [64, 512])
    Hst = P.sbuf("Hst", [128, 4, 64]); Htmp = P.sbuf("Htmp", [128, 4, 64])
    P.op("pool", lambda e: e.memset(Hst.ap, 0.0), writes=[Hst])
    identI = P.sbuf("identI", [64, 8, 64])
    for h_ in range(8):
        P.op("pool", lambda e, h_=h_: e.tensor_copy(out=identI.ap[:, h_, :], in_=ident.ap[0:64, 0:64]), reads=[ident], writes=[identI])
    v8 = lambda t: t.ap.rearrange("p (h c) -> p h c", c=64)
    bc8 = lambda m: m.unsqueeze(1).to_broadcast([64, 8, 64])

    projs = [(0, wr_d, 512), (1, w1_d, 128), (2, wk_d, 512), (3, wv_d, 512), (4, a1_d, 128), (5, g1_d, 480)]
    xsi = [0]
    wki = [0]

    for g in range(NG):
        t0 = g * NT
        if g > 0:
            P.op("pool", lambda e: e.tensor_copy(out=hx_all.ap[:, :, 0:1], in_=hx_all.ap[:, :, NT:NT + 1]), reads=hx, writes=hx)
        for k in range(KC):
            P.dma("sp" if k % 2 == 0 else "act", hx[k].ap[:, 1:NT + 1], hT.ap[k * 128:(k + 1) * 128, t0:t0 + NT], writes=[hx[k]])
        ssps = nb()
        for k in range(KC):
            s_ = sq[k % 2]
            P.op("act", lambda e, k=k, s_=s_: e.activation(out=s_.r32(), in_=hx[k].ap[:, 1:NT + 1], func=AF.Square), reads=[hx[k]], writes=[s_])
            P.op("pe", lambda e, k=k, s_=s_: e.matmul(ssps.ap[:, 0:NT], lhsT=ones.r32(), rhs=s_.r32(), start=(k == 0), stop=(k == KC - 1)),
                 reads=[ones, s_], writes=[ssps])
        P.op("dve", lambda e: e.tensor_scalar(out=rstd.ap, in0=ssps.ap[:, 0:NT], scalar1=1.0 / D, scalar2=EPS, op0=ALU.mult, op1=ALU.add), reads=[ssps], writes=[rstd])
        P.op("act", lambda e: e.activation(out=rstd.ap, in_=rstd.ap, func=AF.Sqrt), reads=[rstd], writes=[rstd])
        P.op("dve", lambda e: e.reciprocal(out=rstd.ap, in_=rstd.ap), reads=[rstd], writes=[rstd])
        for k in range(KC):
            P.op("dve", lambda e, k=k: e.scalar_tensor_tensor(out=hx[k].ap[:, 1:NT + 1], in0=hx[k].ap[:, 1:NT + 1], scalar=g_mix.ap[:, k:k + 1], in1=rstd.ap,
                                                              op0=ALU.mult, op1=ALU.mult), reads=[hx[k], g_mix, rstd], writes=[hx[k]])
        for (j, Wd, ncols) in projs:
            ocw = 120 if ncols == 480 else 128
            noc = ncols // ocw
            accs = [nb() for _ in range(noc)]
            for k in range(KC):
                if k % KB == 0:
                    wt = wk_[wki[0] % 2]; wki[0] += 1
                    P.dma("sp", wt.ap[:, :, 0:ncols], Wd.ap[k * 128:(k + KB) * 128, :].bitcast(F32R).rearrange("(k p) c -> p k c", p=128), writes=[wt])
                x_ = xs[xsi[0] % 3]; xt_ = xt[xsi[0] % 3]; xsi[0] += 1
                mcol = j * KC + k
                P.op("act", lambda e, k=k, xt_=xt_, mcol=mcol: e.activation(out=xt_.ap, in_=hx[k].ap[:, 0:NT], func=AF.Copy, scale=mu.ap[:, mcol:mcol + 1]),
                     reads=[hx[k], mu], writes=[xt_])
                P.op("dve", lambda e, k=k, x_=x_, xt_=xt_, mcol=mcol: e.scalar_tensor_tensor(out=x_.r32(), in0=hx[k].ap[:, 1:NT + 1], scalar=omu.ap[:, mcol:mcol + 1],
                                                                                            in1=xt_.ap, op0=ALU.mult, op1=ALU.add),
                     reads=[hx[k], omu, xt_], writes=[x_])
                for oc in range(noc):
                    P.op("pe", lambda e, oc=oc, k=k, wt=wt, x_=x_: e.matmul(accs[oc].ap[0:ocw, 0:NT], lhsT=wt.ap[:, k % KB, oc * ocw:(oc + 1) * ocw], rhs=x_.r32(),
                                                                          start=(k == 0), stop=(k == KC - 1)), reads=[wt, x_], writes=[accs[oc]])
            if j in (0, 2, 3):
                dst = {0: t_r, 2: t_k, 3: t_v}[j]
                for oc in range(4):
                    P.op("act" if oc % 2 == 0 else "dve",
                         (lambda e, oc=oc, dst=dst: e.activation(out=dst.ap[:, oc, :], in_=accs[oc].ap[:, 0:NT], func=AF.Copy)) if oc % 2 == 0 else
                         (lambda e, oc=oc, dst=dst: e.tensor_copy(out=dst.ap[:, oc, :], in_=accs[oc].ap[:, 0:NT])),
                         reads=[accs[oc]], writes=[dst])
            elif j == 1:
                P.op("act", lambda e: e.activation(out=lora[0].r32(), in_=accs[0].ap[:, 0:NT], func=AF.Tanh), reads=[accs[0]], writes=[lora[0]])
            elif j == 4:
                P.op("act", lambda e: e.activation(out=lora[1].r32(), in_=accs[0].ap[:, 0:NT], func=AF.Copy), reads=[accs[0]], writes=[lora[1]])
            else:
                for oc in range(4):
                    P.op("act", lambda e, oc=oc: e.activation(out=lg.r32()[:, oc, :], in_=accs[oc].ap[0:120, 0:NT], func=AF.Sigmoid), reads=[accs[oc]], writes=[lg])
        for oc in range(4):
            acc = nb()
            P.op("pe", lambda e, oc=oc, acc=acc: e.matmul(acc.ap[:, 0:NT], lhsT=w2.ap[:, oc * 128:(oc + 1) * 128], rhs=lora[0].r32(), start=True, stop=True),
                 reads=[w2, lora[0]], writes=[acc])
            P.op("act", lambda e, oc=oc, acc=acc: e.activation(out=t_lw.ap[:, oc, :], in_=acc.ap[:, 0:NT], func=AF.Sigmoid, bias=PW0(oc)), reads=[acc, par], writes=[t_lw])
            acc = nb()
            P.op("pe", lambda e, oc=oc, acc=acc: e.matmul(acc.ap[:, 0:NT], lhsT=a2.ap[:, oc * 128:(oc + 1) * 128], rhs=lora[1].r32(), start=True, stop=True),
                 reads=[a2, lora[1]], writes=[acc])
            P.op("act", lambda e, oc=oc, acc=acc: e.activation(out=t_a.ap[:, oc, :], in_=acc.ap[:, 0:NT], func=AF.Sigmoid, bias=PA0(oc)), reads=[acc, par], writes=[t_a])
            acc = nb()
            for kk_ in range(4):
                P.op("pe", lambda e, oc=oc, acc=acc, kk_=kk_: e.matmul(acc.ap[:, 0:NT], lhsT=g2.ap[:, kk_, oc * 128:(oc + 1) * 128], rhs=lg.r32()[:, kk_, :],
                                                                      start=(kk_ == 0), stop=(kk_ == 3)), reads=[g2, lg], writes=[acc])
            P.op("dve", lambda e, oc=oc, acc=acc: e.tensor_copy(out=t_g.ap[:, oc, :], in_=acc.ap[:, 0:NT]), reads=[acc], writes=[t_g])
        for pr in range(4):
            P.op("dve", lambda e, pr=pr: e.tensor_scalar(out=t_kk.ap[:, pr, :], in0=t_k.ap[:, pr, :], scalar1=PKK(pr), scalar2=None, op0=ALU.mult),
                 reads=[t_k, par], writes=[t_kk])
            P.op("pool", lambda e, pr=pr: e.tensor_tensor(out=tmp4.ap[:, pr, :], in0=t_kk.ap[:, pr, :], in1=t_kk.ap[:, pr, :], op=ALU.mult), reads=[t_kk], writes=[tmp4])
            acc = nb()
            P.op("pe", lambda e, pr=pr, acc=acc: e.matmul(acc.ap[:, 0:NT], lhsT=bones.ap, rhs=tmp4.ap[:, pr, :], start=True, stop=True), reads=[bones, tmp4], writes=[acc])
            P.op("act", lambda e, pr=pr, acc=acc: e.activation(out=tmp4.ap[:, pr, :], in_=acc.ap[:, 0:NT], func=AF.Sqrt), reads=[acc], writes=[tmp4])
            P.op("dve", lambda e, pr=pr: e.tensor_scalar(out=tmp4.ap[:, pr, :], in0=tmp4.ap[:, pr, :], scalar1=1e-12, scalar2=None, op0=ALU.max), reads=[tmp4], writes=[tmp4])
            P.op("dve", lambda e, pr=pr: e.reciprocal(out=tmp4.ap[:, pr, :], in_=tmp4.ap[:, pr, :]), reads=[tmp4], writes=[tmp4])
            P.op("pool", lambda e, pr=pr: e.tensor_tensor(out=t_kk.ap[:, pr, :], in0=t_kk.ap[:, pr, :], in1=tmp4.ap[:, pr, :], op=ALU.mult), reads=[t_kk, tmp4], writes=[t_kk])
            P.op("dve", lambda e, pr=pr: e.tensor_scalar(out=tmp4.ap[:, pr, :], in0=t_a.ap[:, pr, :], scalar1=-1.0, scalar2=PKA(pr), op0=ALU.add, op1=ALU.mult),
                 reads=[t_a, par, tmp4], writes=[tmp4])
            P.op("dve", lambda e, pr=pr: e.scalar_tensor_tensor(out=t_k.ap[:, pr, :], in0=tmp4.ap[:, pr, :], scalar=1.0, in1=t_k.ap[:, pr, :], op0=ALU.add, op1=ALU.mult),
                 reads=[tmp4, t_k], writes=[t_k])
            P.op("dve", lambda e, pr=pr: e.scalar_tensor_tensor(out=tmp4.ap[:, pr, :], in0=t_r.ap[:, pr, :], scalar=PRK(pr), in1=t_k.ap[:, pr, :], op0=ALU.mult, op1=ALU.mult),
                 reads=[t_r, par, t_k, tmp4], writes=[tmp4])
            acc = nb()
            P.op("pe", lambda e, pr=pr, acc=acc: e.matmul(acc.ap[:, 0:NT], lhsT=bones.ap, rhs=tmp4.ap[:, pr, :], start=True, stop=True), reads=[bones, tmp4], writes=[acc])
            P.op("dve", lambda e, pr=pr, acc=acc: e.tensor_tensor(out=t_bon.ap[:, pr, :], in0=acc.ap[:, 0:NT], in1=t_v.ap[:, pr, :], op=ALU.mult), reads=[acc, t_v], writes=[t_bon])
        fl = lambda t: t.ap.rearrange("p a n -> p (a n)")
        P.op("dve", lambda e: e.tensor_single_scalar(out=fl(t_lw), in_=fl(t_lw), scalar=-EXPM05, op=ALU.mult), reads=[t_lw], writes=[t_lw])
        for pr in range(4):
            P.op("dve", lambda e, pr=pr: e.tensor_tensor_scan(out=t_P.ap[:, pr, :], data0=scm.ap, data1=t_lw.ap[:, pr, :], initial=0.0, op0=ALU.mult, op1=ALU.add),
                 reads=[scm, t_lw], writes=[t_P])
        P.op("dve", lambda e: e.tensor_tensor(out=fl(t_lw), in0=fl(t_P), in1=fl(t_lw), op=ALU.subtract), reads=[t_P, t_lw], writes=[t_lw])
        P.op("act", lambda e: e.activation(out=fl(t_lw), in_=fl(t_lw), func=AF.Exp), reads=[t_lw], writes=[t_lw])
        P.op("dve", lambda e: e.scalar_tensor_tensor(out=fl(t_lw), in0=fl(t_lw), scalar=-1.0, in1=fl(t_kk), op0=ALU.mult, op1=ALU.mult), reads=[t_lw, t_kk], writes=[t_lw])
        P.op("pool", lambda e: e.tensor_tensor(out=fl(t_a), in0=fl(t_a), in1=fl(t_kk), op=ALU.mult), reads=[t_a, t_kk], writes=[t_a])
        P.op("act", lambda e: e.activation(out=fl(t_kk), in_=fl(t_P), func=AF.Exp, scale=-1.0), reads=[t_P, t_a, t_lw], writes=[t_kk])
        P.op("pool", lambda e: e.tensor_tensor(out=fl(t_a), in0=fl(t_a), in1=fl(t_kk), op=ALU.mult), reads=[t_a, t_kk], writes=[t_a])
        P.op("dve", lambda e: e.tensor_tensor(out=fl(t_k), in0=fl(t_k), in1=fl(t_kk), op=ALU.mult), reads=[t_k, t_kk], writes=[t_k])
        P.op("act", lambda e: e.activation(out=fl(t_P), in_=fl(t_P), func=AF.Exp), reads=[t_P, t_kk], writes=[t_P])
        P.op("pool", lambda e: e.tensor_tensor(out=fl(t_r), in0=fl(t_r), in1=fl(t_P), op=ALU.mult), reads=[t_r, t_P, t_bon], writes=[t_r])
        abar, bbar, kbar, rbar, eP = t_lw, t_a, t_k, t_r, t_P

        for ch in range(NCH):
            tc = slice(ch * CN, (ch + 1) * CN)
            hs = lambda t, h_: t.ap[(h_ % 2) * 64:(h_ % 2) * 64 + 64, h_ // 2, tc]
            for (src, dst) in ((bbar, tmB), (kbar, tmK), (abar, tmA), (t_v, tmV)):
                bk = nb()
                for pr in range(4):
                    P.op("pe", lambda e, pr=pr, bk=bk, src=src: e.transpose(bk.ap[0:64, pr * 128:(pr + 1) * 128], src.ap[:, pr, tc], ident.ap), reads=[src, ident], writes=[bk])
                P.op("act" if dst in (tmB, tmA) else "dve",
                     (lambda e, bk=bk, dst=dst: e.activation(out=dst.ap, in_=bk.ap[0:64, :], func=AF.Copy)) if dst in (tmB, tmA) else
                     (lambda e, bk=bk, dst=dst: e.tensor_copy(out=dst.ap, in_=bk.ap[0:64, :])), reads=[bk], writes=[dst])
            specs = [(bbar, abar, Nt[0], m_lt), (abar, bbar, Nn[0], m_gt), (abar, kbar, Aak, m_gt), (bbar, rbar, RbT, m_le), (kbar, rbar, RkT, m_le)]
            for si, (la, ra, dst, mk) in enumerate(specs):
                bke = [nb(), nb()]
                for h_ in range(8):
                    bk = bke[h_ % 2]
                    P.op("pe", lambda e, h_=h_, bk=bk, la=la, ra=ra: e.matmul(bk.ap[0:64, (h_ // 2) * 64:(h_ // 2 + 1) * 64], lhsT=hs(la, h_), rhs=hs(ra, h_), start=True, stop=True),
                         reads=[la, ra], writes=[bk])
                for e_ in range(2):
                    bk = bke[e_]
                    dv = dst.ap.rearrange("p (a e c) -> p a e c", e=2, c=64)[:, :, e_, :]
                    P.op("dve", lambda e, bk=bk, dv=dv, mk=mk: e.tensor_tensor(out=dv, in0=bk.ap[0:64, 0:256].rearrange("p (a c) -> p a c", c=64),
                                                                               in1=mk.unsqueeze(1).to_broadcast([64, 4, 64]), op=ALU.mult),
                         reads=[bk, msk], writes=[dst])
            P.op("pool", lambda e: e.tensor_tensor(out=v8(Tt), in0=v8(Nt[0]), in1=identI.ap, op=ALU.add), reads=[Nt[0], identI], writes=[Tt])
            cur = 0
            for st in range(5):
                nxt = 1 - cur
                bN = nb()
                for h_ in range(8):
                    hc = slice(h_ * 64, (h_ + 1) * 64)
                    P.op("pe", lambda e, hc=hc, bN=bN, cur=cur: e.matmul(bN.ap[0:64, hc], lhsT=Nt[cur].ap[:, hc], rhs=Nn[cur].ap[:, hc], start=True, stop=True),
                         reads=[Nt[cur], Nn[cur]], writes=[bN])
                P.op("act", lambda e, bN=bN, nxt=nxt: e.activation(out=Nn[nxt].ap, in_=bN.ap[0:64, :], func=AF.Copy), reads=[bN], writes=[Nn[nxt]])
                if st < 4:
                    bNt = nb()
                    for h_ in range(8):
                        hc = slice(h_ * 64, (h_ + 1) * 64)
                        P.op("pe", lambda e, hc=hc, bNt=bNt, cur=cur: e.matmul(bNt.ap[0:64, hc], lhsT=Nn[cur].ap[:, hc], rhs=Nt[cur].ap[:, hc], start=True, stop=True),
                             reads=[Nt[cur], Nn[cur]], writes=[bNt])
                    P.op("dve", lambda e, bNt=bNt, nxt=nxt: e.tensor_copy(out=Nt[nxt].ap, in_=bNt.ap[0:64, :]), reads=[bNt], writes=[Nt[nxt]])
                bT = nb()
                for h_ in range(8):
                    hc = slice(h_ * 64, (h_ + 1) * 64)
                    P.op("pe", lambda e, hc=hc, bT=bT, nxt=nxt: e.matmul(bT.ap[0:64, hc], lhsT=Nn[nxt].ap[:, hc], rhs=Tt.ap[:, hc], start=True, stop=True),
                         reads=[Nn[nxt], Tt], writes=[bT])
                P.op("dve", lambda e, bT=bT: e.tensor_tensor(out=Tt.ap, in0=bT.ap[0:64, :], in1=Tt.ap, op=ALU.add), reads=[bT, Tt], writes=[Tt])
                cur = nxt
            bW = nb()
            for h_ in range(8):
                e_, pr = h_ % 2, h_ // 2
                P.op("pe", lambda e, h_=h_, e_=e_, pr=pr, bW=bW: e.matmul(bW.ap[e_ * 64:e_ * 64 + 64, pr * 64:(pr + 1) * 64], lhsT=tmA.ap[:, h_ * 64:(h_ + 1) * 64],
                                                                          rhs=Tt.ap[:, h_ * 64:(h_ + 1) * 64], start=True, stop=True), reads=[tmA, Tt], writes=[bW])
            P.op("act", lambda e, bW=bW: e.activation(out=Wt.ap, in_=bW.ap[:, 0:256].rearrange("p (a c) -> p a c", c=64), func=AF.Copy), reads=[bW], writes=[Wt])
            bG = nb()
            for h_ in range(8):
                hc = slice(h_ * 64, (h_ + 1) * 64)
                P.op("pe", lambda e, hc=hc, bG=bG: e.matmul(bG.ap[0:64, hc], lhsT=Aak.ap[:, hc], rhs=Tt.ap[:, hc], start=True, stop=True), reads=[Aak, Tt], writes=[bG])
            P.op("dve", lambda e, bG=bG: e.tensor_copy(out=Gt.ap, in_=bG.ap[0:64, :]), reads=[bG], writes=[Gt])
            bU0 = nb()
            for h_ in range(8):
                hc = slice(h_ * 64, (h_ + 1) * 64)
                P.op("pe", lambda e, hc=hc, bU0=bU0: e.matmul(bU0.ap[0:64, hc], lhsT=Gt.ap[:, hc], rhs=tmV.ap[:, hc], start=True, stop=True), reads=[Gt, tmV], writes=[bU0])
            P.op("act", lambda e, bU0=bU0: e.activation(out=Usb.ap, in_=bU0.ap[0:64, :], func=AF.Copy), reads=[bU0], writes=[Usb])
            bUe = [nb(), nb()]
            for h_ in range(8):
                e_, pr = h_ % 2, h_ // 2
                P.op("pe", lambda e, e_=e_, pr=pr: e.matmul(bUe[e_].ap[0:64, pr * 64:(pr + 1) * 64], lhsT=Wt.ap[e_ * 64:e_ * 64 + 64, pr, :], rhs=Hst.ap[e_ * 64:e_ * 64 + 64, pr, :],
                                                           start=True, stop=True), reads=[Wt, Hst], writes=[bUe[e_]])
            for e_ in range(2):
                uv = Usb.ap.rearrange("p (a e c) -> p a e c", e=2, c=64)[:, :, e_, :]
                P.op("dve", lambda e, e_=e_, uv=uv: e.tensor_tensor(out=uv, in0=bUe[e_].ap[0:64, 0:256].rearrange("p (a c) -> p a c", c=64), in1=uv, op=ALU.add),
                     reads=[bUe[e_], Usb], writes=[Usb])
            bY = nb()
            for h_ in range(8):
                e_, pr = h_ % 2, h_ // 2
                hc = slice(h_ * 64, (h_ + 1) * 64)
                o_ = bY.ap[e_ * 64:e_ * 64 + 64, pr * 64:(pr + 1) * 64]
                P.op("pe", lambda e, hc=hc, o_=o_: e.matmul(o_, lhsT=Usb.ap[:, hc], rhs=RbT.ap[:, hc], start=True, stop=False), reads=[Usb, RbT], writes=[bY])
                P.op("pe", lambda e, hc=hc, o_=o_: e.matmul(o_, lhsT=tmV.ap[:, hc], rhs=RkT.ap[:, hc], start=False, stop=True), reads=[tmV, RkT], writes=[bY])
            P.op("act", lambda e, bY=bY: e.activation(out=t_y.ap[:, :, tc], in_=bY.ap[:, 0:256].rearrange("p (a c) -> p a c", c=64), func=AF.Copy), reads=[bY], writes=[t_y])
            bYe = [nb(), nb()]
            for h_ in range(8):
                e_, pr = h_ % 2, h_ // 2
                P.op("pe", lambda e, h_=h_, e_=e_, pr=pr: e.matmul(bYe[e_].ap[e_ * 64:e_ * 64 + 64, pr * 64:(pr + 1) * 64], lhsT=Hst.ap[e_ * 64:e_ * 64 + 64, pr, :], rhs=hs(rbar, h_),
                                                                  start=True, stop=True), reads=[Hst, rbar], writes=[bYe[e_]])
            for e_ in range(2):
                yv = t_y.ap[e_ * 64:e_ * 64 + 64, :, tc]
                P.op("dve", lambda e, e_=e_, yv=yv: e.tensor_tensor(out=yv, in0=bYe[e_].ap[e_ * 64:e_ * 64 + 64, 0:256].rearrange("p (a c) -> p a c", c=64), in1=yv, op=ALU.add),
                     reads=[bYe[e_], t_y], writes=[t_y])
            bH = nb()
            for h_ in range(8):
                e_, pr = h_ % 2, h_ // 2
                hc = slice(h_ * 64, (h_ + 1) * 64)
                o_ = bH.ap[e_ * 64:e_ * 64 + 64, pr * 64:(pr + 1) * 64]
                P.op("pe", lambda e, hc=hc, o_=o_: e.matmul(o_, lhsT=tmB.ap[:, hc], rhs=Usb.ap[:, hc], start=True, stop=False), reads=[tmB, Usb], writes=[bH])
                P.op("pe", lambda e, hc=hc, o_=o_: e.matmul(o_, lhsT=tmK.ap[:, hc], rhs=tmV.ap[:, hc], start=False, stop=True), reads=[tmK, tmV], writes=[bH])
            P.op("dve", lambda e, bH=bH: e.tensor_tensor(out=Htmp.ap, in0=bH.ap[:, 0:256].rearrange("p (a c) -> p a c", c=64), in1=Hst.ap, op=ALU.add), reads=[bH, Hst], writes=[Htmp])
            pC = eP.ap[:, :, ch * CN + CN - 1:ch * CN + CN].to_broadcast([128, 4, 64])
            P.op("dve", lambda e, pC=pC: e.tensor_tensor(out=Hst.ap, in0=Htmp.ap, in1=pC, op=ALU.mult), reads=[Htmp, eP], writes=[Hst])

        for pr in range(4):
            b1 = nb()
            P.op("pe", lambda e, pr=pr, b1=b1: e.matmul(b1.ap[:, 0:NT], lhsT=bones.ap, rhs=t_y.ap[:, pr, :], start=True, stop=True), reads=[bones, t_y], writes=[b1])
            P.op("pool", lambda e, pr=pr: e.tensor_tensor(out=tmp4.ap[:, pr, :], in0=t_y.ap[:, pr, :], in1=t_y.ap[:, pr, :], op=ALU.mult), reads=[t_y, tmp4], writes=[tmp4])
            b2 = nb()
            P.op("pe", lambda e, pr=pr, b2=b2: e.matmul(b2.ap[:, 0:NT], lhsT=bones.ap, rhs=tmp4.ap[:, pr, :], start=True, stop=True), reads=[bones, tmp4], writes=[b2])
            mean = xt[0]; var = xt[1]
            P.op("act", lambda e, b1=b1: e.activation(out=mean.ap, in_=b1.ap[:, 0:NT], func=AF.Copy, scale=1.0 / 64.0), reads=[b1], writes=[mean])
            P.op("pool", lambda e: e.tensor_tensor(out=var.ap, in0=mean.ap, in1=mean.ap, op=ALU.mult), reads=[mean], writes=[var])
            P.op("dve", lambda e, b2=b2: e.scalar_tensor_tensor(out=var.ap, in0=b2.ap[:, 0:NT], scalar=1.0 / 64.0, in1=var.ap, op0=ALU.mult, op1=ALU.subtract), reads=[b2, var], writes=[var])
            P.op("dve", lambda e: e.tensor_scalar(out=var.ap, in0=var.ap, scalar1=64e-5, scalar2=None, op0=ALU.add), reads=[var], writes=[var])
            P.op("act", lambda e: e.activation(out=var.ap, in_=var.ap, func=AF.Sqrt), reads=[var], writes=[var])
            P.op("dve", lambda e: e.reciprocal(out=var.ap, in_=var.ap), reads=[var], writes=[var])
            yv = t_y.ap[:, pr, :]
            P.op("dve", lambda e, yv=yv: e.tensor_tensor(out=yv, in0=yv, in1=mean.ap, op=ALU.subtract), reads=[t_y, mean], writes=[t_y])
            P.op("pool", lambda e, yv=yv: e.tensor_tensor(out=yv, in0=yv, in1=var.ap, op=ALU.mult), reads=[t_y, var], writes=[t_y])
            P.op("dve", lambda e, yv=yv, pr=pr: e.tensor_scalar(out=yv, in0=yv, scalar1=PLW(pr), scalar2=PLB(pr), op0=ALU.mult, op1=ALU.add), reads=[t_y, par], writes=[t_y])
            P.op("pool", lambda e, yv=yv, pr=pr: e.tensor_tensor(out=yv, in0=yv, in1=t_bon.ap[:, pr, :], op=ALU.add), reads=[t_y, t_bon], writes=[t_y])
            P.op("dve", lambda e, yv=yv, pr=pr: e.tensor_tensor(out=tmp4.ap[:, pr, :], in0=yv, in1=t_g.ap[:, pr, :], op=ALU.mult), reads=[t_y, t_g, tmp4], writes=[tmp4])
            P.dma("sp", zT.ap[pr * 128:(pr + 1) * 128, t0:t0 + NT], tmp4.ap[:, pr, :], reads=[tmp4], writes=[zT], join=True)
    P.finish([zT], "sp")
    P.close()
    return nc, P


def mixC_consts(NT):
    s = np.arange(64)[:, None]; t = np.arange(64)[None, :]
    masks = np.concatenate([(s <= t), (s < t), (s > t)], axis=1).astype(np.float32)
    rmask = np.where(np.arange(NT) % 64 == 0, 0.0, 1.0).astype(np.float32)
    scan = np.broadcast_to(rmask[None, :], (128, NT)).copy()
    bones = np.zeros((128, 128), np.float32); bones[:64, :64] = 1.0; bones[64:, 64:] = 1.0
    return masks, scan, bones


def mixC_core_inputs(inp, c):
    cs = slice(c * 512, (c + 1) * 512)
    d = dict(wr=np.ascontiguousarray(inp["rwkv_w_r"][0][:, cs]), wk=np.ascontiguousarray(inp["rwkv_w_k"][0][:, cs]),
             wv=np.ascontiguousarray(inp["rwkv_w_v"][0][:, cs]), w1=inp["rwkv_w1"][0], a1=inp["rwkv_a1"][0], g1=inp["rwkv_g1"][0],
             w2=np.ascontiguousarray(inp["rwkv_w2"][0][:, cs]), a2=np.ascontiguousarray(inp["rwkv_a2"][0][:, cs]),
             g2=np.ascontiguousarray(inp["rwkv_g2"][0][:, cs]))
    names = ["rwkv_w0", "rwkv_a0", "rwkv_k_k", "rwkv_k_a", "rwkv_r_k", "rwkv_lnx_w", "rwkv_lnx_b"]
    d["par"] = np.concatenate([pp(np.asarray(inp[n][0]).reshape(-1)[cs]) for n in names], axis=1)
    return d


def build_memkv():
    nc = bass.Bass("TRN2", target_bir_lowering=False)
    nc.dge_precook = False
    P = Prog(nc)
    NT = NMEM
    memT = P.dram("memT", [D, NMEM], F32, "ExternalInput")
    wkv = P.dram("wkv", [D, 1024], F32, "ExternalInput")
    g_d = P.dram("g_mem", [128, KC], F32, "ExternalInput")
    ident_d = P.dram("ident", [128, 128], F32, "ExternalInput")
    kvT = P.dram("kvT", [1024, NMEM], F32, "ExternalOutput")
    kvM = P.dram("kvM", [NMEM, 1024], F32, "ExternalOutput")
    c = make_common(P, NT)
    g = P.sbuf("g_sb", [128, KC]); P.dma("sp", g.ap, g_d.ap, writes=[g])
    ident = P.sbuf("ident_sb", [128, 128]); P.dma("sp", ident.ap, ident_d.ap, writes=[ident])
    m_all = P.sbuf("m", [128, KC, NT]); m = P.subs(m_all, KC)
    n_all = P.sbuf("mn", [128, KC, NT]); mn = P.subs(n_all, KC)
    kv_all = P.sbuf("kv", [128, 8, NT]); kv = P.subs(kv_all, 8)
    kvm = P.sbuf("kvm", [128, 2, 1024])
    trp = [P.psum(f"trp{i}", [128, 512]) for i in range(2)]
    for k in range(KC):
        P.dma("sp" if k % 2 == 0 else "act", m[k].ap, memT.ap[k * 128:(k + 1) * 128, :], writes=[m[k]])
    rstd = rms_stats(c, m, float(D), EPS)
    rms_apply(c, m, mn, g, rstd)

    def evac(oc, acc):
        P.op("act", lambda e: e.activation(out=kv[oc].ap, in_=acc.ap[:, 0:NT], func=AF.Copy), reads=[acc], writes=[kv[oc]])
        P.dma("sp", kvT.ap[oc * 128:(oc + 1) * 128, :], kv[oc].ap, reads=[kv[oc]], writes=[kvT], join=True)
        for mt in range(2):
            tp = trp[mt]
            P.op("pe", lambda e, mt=mt, tp=tp: e.transpose(tp.ap[:, 0:128], kv[oc].ap[:, mt * 128:(mt + 1) * 128], ident.ap), reads=[kv[oc], ident], writes=[tp])
            P.op("dve", lambda e, mt=mt, tp=tp: e.tensor_copy(out=kvm.ap[:, mt, oc * 128:(oc + 1) * 128], in_=tp.ap[:, 0:128]), reads=[tp], writes=[kvm])
    gemm_fm(c, wkv, 0, KC, 0, 8, mn, evac)
    P.dma("sp", kvM.ap.rearrange("(t p) d -> p t d", p=128), kvm.ap, reads=[kvm], writes=[kvM])
    P.finish([kvT, kvM], "sp")
    P.close()
    return nc, P


NCORE = 8
_CACHE = {}


def _prog(name, fn):
    if name not in _CACHE:
        _CACHE[name] = fn()[0]
    return _CACHE[name]


def _run(nc, in_maps):
    res = run_bass_kernel_spmd(nc, in_maps, core_ids=list(range(len(in_maps))))
    return res.results


def kernel(**inp):
    inp = {k: np.asarray(v) for k, v in inp.items()}
    Tn = inp["x"].shape[1]
    TOK = Tn // NCORE
    ident = np.eye(128, dtype=np.float32)
    xT = fm(inp["x"][0])
    memT = fm(inp["mem"][0])
    g_mem = pp(inp["mem_norm_g"])
    res = _run(_prog("memkv", build_memkv),
               [dict(memT=memT, wkv=np.ascontiguousarray(inp["mem_w_kv"][:, c * 1024:(c + 1) * 1024]), g_mem=g_mem, ident=ident) for c in range(NCORE)])
    kT = np.concatenate([res[c]["kvT"] for c in range(4)], axis=0)
    vM = np.concatenate([res[c]["kvM"] for c in range(4, 8)], axis=1)
    masks, scan = mixA_consts(512)
    g_mix0 = pp(inp["norm_mix_g"][0])
    in_maps = []
    for c in range(NCORE):
        wfm, par = mixA_core_inputs(inp, c)
        in_maps.append(dict(xT=xT, wfm=wfm, g_mix=g_mix0, par=par, masks=masks, scanmask=scan, ident=ident))
    res = _run(_prog(("mixA", Tn), lambda: build_mixA(Tn)), in_maps)
    z0T = np.empty((D, Tn), np.float32)
    ss = np.empty((24, Tn), np.float32)
    for c in range(NCORE):
        hm, half = c // 2, c % 2
        z0T[hm * 512 + half * 256:hm * 512 + half * 256 + 256] = res[c]["zT"][0:256]
        z0T[2048 + 2 * c * 128:2048 + (2 * c + 2) * 128] = res[c]["zT"][256:512]
        ss[c] = res[c]["ssO"][0]
        ss[8 + 2 * c] = res[c]["ssO"][1]
        ss[8 + 2 * c + 1] = res[c]["ssO"][2]
    del res, in_maps
    g_head = pp(np.concatenate([inp["mlstm_norm_g"][0], inp["hgrn_norm_g"][0]]))

    def tok_layer(layer, hT_full, zT_full, w_mo, layer0, final, ss_full=None):
        gains = np.concatenate([pp(inp["norm_xattn_g"][layer]), pp(inp["norm_mlp_g"][layer]), pp(inp["final_norm_g"]),
                                g_head if layer0 else pp(np.ones(D, np.float32))], axis=1)
        common = dict(w_mo=w_mo, w_q=inp["xattn_w_q"][layer], w_o=inp["xattn_w_o"][layer], w_up=inp["mlp_w_up"][layer],
                      w_dn=inp["mlp_w_down"][layer], kT=kT, vM=vM, gains=gains, ident=ident)
        maps = []
        for c in range(NCORE):
            ts = slice(c * TOK, (c + 1) * TOK)
            d = dict(common, hT=np.ascontiguousarray(hT_full[:, ts]), zT=np.ascontiguousarray(zT_full[:, ts]))
            if layer0:
                d["ss"] = np.ascontiguousarray(ss_full[:, ts])
            maps.append(d)
        r = _run(_prog(("tok", TOK, layer0, final), lambda: build_tok(TOK, 256, layer0, final)), maps)
        return np.concatenate([r[c]["oT"] for c in range(NCORE)], axis=1)

    h1T = tok_layer(0, xT, z0T, inp["ab_w_out"][0], True, False, ss)
    del z0T
    masksC, scanC, bones = mixC_consts(256)
    mu = np.concatenate([pp(inp["rwkv_mu"][0, j]) for j in range(6)], axis=1)
    g_mix1 = pp(inp["norm_mix_g"][1])
    in_maps = []
    for c in range(NCORE):
        d = mixC_core_inputs(inp, c)
        d.update(hT=h1T, g_mix=g_mix1, mu=mu, masks=masksC, scanmask=scanC, ident=ident, bones=bones)
        in_maps.append(d)
    res = _run(_prog(("mixC", Tn), lambda: build_mixC(Tn, 256)), in_maps)
    z1T = np.concatenate([res[c]["zT"] for c in range(NCORE)], axis=0)
    del res, in_maps
    outT = tok_layer(1, h1T, z1T, inp["rwkv_w_o"][0], False, True)
    return np.ascontiguousarray(outT.T)[None].astype(np.float32)
```
